# Optimizing a Trainium2 kernel written in Bass

```python
import math
import jax, jax.numpy as jnp
from jax import lax
import numpy as np

D_MODEL = 1024
BATCH = 4
SEQ = 4096
DEPTH = 4

GROUP_WIDTH = 256
MIX_WIDTH = 4 * GROUP_WIDTH
Q_BLOCK = 128
EPS = 1e-6
D_FF = 2816

A_HEADS = 4
A_HEAD_DIM = 64
IDX_HEADS = 8
IDX_DIM = 32
TOPK_MAX = 256
REL_BUCKETS = 32
REL_MAX_DIST = 128

B_HEADS = 4
B_KEY_DIM = 32
B_VAL_DIM = 64
B_GATE_RANK = 16
B_GATE_TAU = 16.0
B_CHUNK = 64

C_CHANNELS = 256
C_KERNEL = 31

D_HEADS = 4
D_Q_RANK = 256
D_KV_RANK = 128
D_NOPE = 64
D_ROPE = 32
D_V = 64
D_QK = D_NOPE + D_ROPE
ROPE_THETA = 10000.0

IN_WIDTHS = (
    A_HEADS * A_HEAD_DIM, A_HEADS * A_HEAD_DIM, A_HEADS * A_HEAD_DIM,
    IDX_HEADS * IDX_DIM, IDX_DIM, IDX_HEADS,
    B_HEADS * B_KEY_DIM, B_HEADS * B_KEY_DIM, B_HEADS * B_VAL_DIM,
    B_GATE_RANK, B_HEADS * B_VAL_DIM,
    2 * C_CHANNELS,
    D_Q_RANK, D_KV_RANK, D_ROPE,
)
IN_WIDTH = sum(IN_WIDTHS)

kernel_name = "hybrid_parallel_heads_dsa_gla_conv_mla"


def rmsnorm(x, g):
    x32 = x.astype(jnp.float32)
    y = x32 * lax.rsqrt(jnp.mean(x32 * x32, axis=-1, keepdims=True) + EPS)
    return (y * g.astype(jnp.float32)).astype(x.dtype)


def swiglu(x, w_gate, w_up, w_down):
    return (jax.nn.silu(x @ w_gate) * (x @ w_up)) @ w_down


def t5_bucket(dist):
    max_exact = REL_BUCKETS // 2
    d = jnp.maximum(dist, 0)
    df = jnp.maximum(d, 1).astype(jnp.float32)
    large = max_exact + (jnp.log(df / max_exact) / math.log(REL_MAX_DIST / max_exact)
                         * (REL_BUCKETS - max_exact)).astype(jnp.int32)
    large = jnp.minimum(large, REL_BUCKETS - 1)
    return jnp.where(d < max_exact, d, large)


def rope(x, pos):
    half = x.shape[-1] // 2
    freqs = ROPE_THETA ** (-jnp.arange(half, dtype=jnp.float32) / half)
    ang = pos.astype(jnp.float32)[:, None] * freqs[None, :]
    cos = jnp.cos(ang)[:, None, :]
    sin = jnp.sin(ang)[:, None, :]
    x32 = x.astype(jnp.float32)
    x1, x2 = x32[..., :half], x32[..., half:]
    return jnp.concatenate([x1 * cos - x2 * sin, x2 * cos + x1 * sin], axis=-1).astype(x.dtype)


def dsa_mixer(q, k, v, iq, ik, iw, g_q, g_k, rel_bias):
    bsz, L = q.shape[0], q.shape[1]
    topk = min(TOPK_MAX, L // 4)
    q = rmsnorm(q, g_q)
    k = rmsnorm(k, g_k)
    scale = A_HEAD_DIM ** -0.5
    idx_scale = IDX_DIM ** -0.5
    w = iw.astype(jnp.float32) * (IDX_HEADS ** -0.5)
    key_pos = jnp.arange(L)

    def block(i):
        t0 = i * Q_BLOCK
        qb = lax.dynamic_slice_in_dim(q, t0, Q_BLOCK, axis=1)
        iqb = lax.dynamic_slice_in_dim(iq, t0, Q_BLOCK, axis=1)
        wb = lax.dynamic_slice_in_dim(w, t0, Q_BLOCK, axis=1)
        qpos = t0 + jnp.arange(Q_BLOCK)
        causal = key_pos[None, :] <= qpos[:, None]
        s = jax.nn.relu(jnp.einsum('bthd,bsd->bths', iqb, ik).astype(jnp.float32) * idx_scale)
        score = jnp.einsum('bths,bth->bts', s, wb)
        score = jnp.where(causal[None], score, -jnp.inf)
        _, sel = lax.top_k(score, topk)
        ks = jax.vmap(lambda kb, ib: kb[ib])(k, sel)
        vs = jax.vmap(lambda vb, ib: vb[ib])(v, sel)
        valid = sel <= qpos[None, :, None]
        bias = rel_bias[t5_bucket(qpos[None, :, None] - sel)]
        logits = (jnp.einsum('bthd,btkhd->bthk', qb, ks).astype(jnp.float32) * scale
                  + jnp.swapaxes(bias, 2, 3).astype(jnp.float32))
        logits = jnp.where(valid[:, :, None, :], logits, -jnp.inf)
        p = jax.nn.softmax(logits, axis=-1)
        return jnp.einsum('bthk,btkhd->bthd', p.astype(v.dtype), vs)

    out = lax.map(block, jnp.arange(L // Q_BLOCK))
    return jnp.moveaxis(out, 0, 1).reshape(bsz, L, A_HEADS * A_HEAD_DIM)


def gla_mixer(q, k, v, g_lat, r, w_gate_up, b_gate, g_out):
    bsz, L = q.shape[0], q.shape[1]
    f32 = jnp.float32
    log_a = jax.nn.log_sigmoid((g_lat @ w_gate_up + b_gate).astype(f32)) / B_GATE_TAU
    log_a = log_a.reshape(bsz, L, B_HEADS, B_KEY_DIM)
    qf = q.astype(f32) * (B_KEY_DIM ** -0.5)
    kf = k.astype(f32)
    vf = v.astype(f32)
    n = L // B_CHUNK

    def to_chunks(t):
        return jnp.moveaxis(t.reshape(bsz, n, B_CHUNK, t.shape[2], t.shape[3]), 1, 0)

    tri = jnp.tril(jnp.ones((B_CHUNK, B_CHUNK), dtype=bool))

    def step(S, inp):
        qc, kc, vc, gc = inp
        b = jnp.cumsum(gc, axis=1)
        o_inter = jnp.einsum('bchk,bhkv->bchv', qc * jnp.exp(b), S)
        diff = b[:, :, None] - b[:, None, :]
        decay = jnp.exp(jnp.where(tri[None, :, :, None, None], diff, -jnp.inf))
        A = jnp.einsum('bihk,bjhk,bijhk->bhij', qc, kc, decay)
        o_intra = jnp.einsum('bhij,bjhv->bihv', A, vc)
        b_last = b[:, -1]
        k_dec = kc * jnp.exp(b_last[:, None] - b)
        S = jnp.exp(b_last)[..., None] * S + jnp.einsum('bchk,bchv->bhkv', k_dec, vc)
        return S, o_inter + o_intra

    S0 = jnp.zeros((bsz, B_HEADS, B_KEY_DIM, B_VAL_DIM), f32)
    _, o = lax.scan(step, S0, (to_chunks(qf), to_chunks(kf), to_chunks(vf), to_chunks(log_a)))
    o = jnp.moveaxis(o, 0, 1).reshape(bsz, L, B_HEADS, B_VAL_DIM)
    o = rmsnorm(o, g_out).reshape(bsz, L, B_HEADS * B_VAL_DIM)
    return (o * jax.nn.silu(r.astype(f32))).astype(q.dtype)


def conv_mixer(u, w_dw, b_dw, g_norm):
    a, gate = jnp.split(u, 2, axis=-1)
    h = a * jax.nn.sigmoid(gate)
    h = lax.conv_general_dilated(h, w_dw.astype(h.dtype), window_strides=(1,),
                                 padding=[(C_KERNEL - 1, 0)],
                                 dimension_numbers=('NWC', 'WIO', 'NWC'),
                                 feature_group_count=C_CHANNELS) + b_dw
    return jax.nn.silu(rmsnorm(h, g_norm))


def mla_mixer(c_q, c_kv, k_pe, g_qa, w_uq, g_kva, w_ukv, g_q, g_k, pos):
    bsz, L = c_q.shape[0], c_q.shape[1]
    q = (rmsnorm(c_q, g_qa) @ w_uq).reshape(bsz, L, D_HEADS, D_QK)
    kv = (rmsnorm(c_kv, g_kva) @ w_ukv).reshape(bsz, L, D_HEADS, D_NOPE + D_V)
    k_nope, v = kv[..., :D_NOPE], kv[..., D_NOPE:]
    k = jnp.concatenate([k_nope, jnp.broadcast_to(k_pe[:, :, None, :], (bsz, L, D_HEADS, D_ROPE))], axis=-1)
    q = rmsnorm(q, g_q)
    k = rmsnorm(k, g_k)
    q = jnp.concatenate([q[..., :D_NOPE], rope(q[..., D_NOPE:], pos)], axis=-1)
    k = jnp.concatenate([k[..., :D_NOPE], rope(k[..., D_NOPE:], pos)], axis=-1)
    scale = D_QK ** -0.5
    key_pos = jnp.arange(L)

    def block(i):
        t0 = i * Q_BLOCK
        qb = lax.dynamic_slice_in_dim(q, t0, Q_BLOCK, axis=1)
        qpos = t0 + jnp.arange(Q_BLOCK)
        causal = key_pos[None, :] <= qpos[:, None]
        logits = jnp.einsum('bthd,bshd->bhts', qb, k).astype(jnp.float32) * scale
        logits = jnp.where(causal[None, None], logits, -jnp.inf)
        p = jax.nn.softmax(logits, axis=-1)
        return jnp.einsum('bhts,bshd->bthd', p.astype(v.dtype), v)

    out = lax.map(block, jnp.arange(L // Q_BLOCK))
    return jnp.moveaxis(out, 0, 1).reshape(bsz, L, D_HEADS * D_V)


def hybrid_layer(x, ffn1_norm, ffn1_gate, ffn1_up, ffn1_down, mix_norm, w_in,
                 a_q_norm, a_k_norm, rel_bias, b_gate_up, b_gate_bias, b_out_norm,
                 c_dw_w, c_dw_b, c_norm, d_qa_norm, d_uq, d_kva_norm, d_ukv, d_q_norm, d_k_norm,
                 w_out, ffn2_norm, ffn2_gate, ffn2_up, ffn2_down, pos):
    bsz, L = x.shape[0], x.shape[1]
    x = x + 0.5 * swiglu(rmsnorm(x, ffn1_norm), ffn1_gate, ffn1_up, ffn1_down)

    z = rmsnorm(x, mix_norm) @ w_in
    (aq, ak, av, iq, ik, iw, bq, bk, bv, bg, br, cu, dcq, dckv, dkpe) = jnp.split(
        z, np.cumsum(IN_WIDTHS)[:-1].tolist(), axis=-1)

    y_a = dsa_mixer(aq.reshape(bsz, L, A_HEADS, A_HEAD_DIM), ak.reshape(bsz, L, A_HEADS, A_HEAD_DIM),
                    av.reshape(bsz, L, A_HEADS, A_HEAD_DIM), iq.reshape(bsz, L, IDX_HEADS, IDX_DIM),
                    ik, iw, a_q_norm, a_k_norm, rel_bias)
    y_b = gla_mixer(bq.reshape(bsz, L, B_HEADS, B_KEY_DIM), bk.reshape(bsz, L, B_HEADS, B_KEY_DIM),
                    bv.reshape(bsz, L, B_HEADS, B_VAL_DIM), bg, br, b_gate_up, b_gate_bias, b_out_norm)
    y_c = conv_mixer(cu, c_dw_w, c_dw_b, c_norm)
    y_d = mla_mixer(dcq, dckv, dkpe, d_qa_norm, d_uq, d_kva_norm, d_ukv, d_q_norm, d_k_norm, pos)

    y = jnp.concatenate([y_a, y_b.astype(x.dtype), y_c.astype(x.dtype), y_d], axis=-1) @ w_out
    x = x + y

    x = x + 0.5 * swiglu(rmsnorm(x, ffn2_norm), ffn2_gate, ffn2_up, ffn2_down)
    return x


def setup_inputs(seed: int = 0) -> dict:
    key = jax.random.key(seed)
    ks = iter(jax.random.split(key, 40))
    f32 = jnp.float32

    def w(shape, fan_in):
        return jax.random.normal(next(ks), shape, f32) * (fan_in ** -0.5)

    def gain(shape):
        return 1.0 + 0.05 * jax.random.normal(next(ks), shape, f32)

    def small(shape, s):
        return s * jax.random.normal(next(ks), shape, f32)

    Ld = DEPTH
    return {
        "x": jax.random.normal(next(ks), (BATCH, SEQ, D_MODEL), f32),
        "ffn1_norm": gain((Ld, D_MODEL)),
        "ffn1_gate": w((Ld, D_MODEL, D_FF), D_MODEL),
        "ffn1_up": w((Ld, D_MODEL, D_FF), D_MODEL),
        "ffn1_down": w((Ld, D_FF, D_MODEL), D_FF),
        "mix_norm": gain((Ld, D_MODEL)),
        "w_in": w((Ld, D_MODEL, IN_WIDTH), D_MODEL),
        "a_q_norm": gain((Ld, A_HEAD_DIM)),
        "a_k_norm": gain((Ld, A_HEAD_DIM)),
        "rel_bias": small((REL_BUCKETS, A_HEADS), 0.2),
        "b_gate_up": w((Ld, B_GATE_RANK, B_HEADS * B_KEY_DIM), B_GATE_RANK),
        "b_gate_bias": small((Ld, B_HEADS * B_KEY_DIM), 0.1),
        "b_out_norm": gain((Ld, B_VAL_DIM)),
        "c_dw_w": w((Ld, C_KERNEL, 1, C_CHANNELS), C_KERNEL),
        "c_dw_b": small((Ld, C_CHANNELS), 0.02),
        "c_norm": gain((Ld, C_CHANNELS)),
        "d_qa_norm": gain((Ld, D_Q_RANK)),
        "d_uq": w((Ld, D_Q_RANK, D_HEADS * D_QK), D_Q_RANK),
        "d_kva_norm": gain((Ld, D_KV_RANK)),
        "d_ukv": w((Ld, D_KV_RANK, D_HEADS * (D_NOPE + D_V)), D_KV_RANK),
        "d_q_norm": gain((Ld, D_QK)),
        "d_k_norm": gain((Ld, D_QK)),
        "w_out": w((Ld, MIX_WIDTH, D_MODEL), MIX_WIDTH),
        "ffn2_norm": gain((Ld, D_MODEL)),
        "ffn2_gate": w((Ld, D_MODEL, D_FF), D_MODEL),
        "ffn2_up": w((Ld, D_MODEL, D_FF), D_MODEL),
        "ffn2_down": w((Ld, D_FF, D_MODEL), D_FF),
    }


def reference(x, ffn1_norm, ffn1_gate, ffn1_up, ffn1_down, mix_norm, w_in, a_q_norm, a_k_norm,
              rel_bias, b_gate_up, b_gate_bias, b_out_norm, c_dw_w, c_dw_b, c_norm,
              d_qa_norm, d_uq, d_kva_norm, d_ukv, d_q_norm, d_k_norm, w_out,
              ffn2_norm, ffn2_gate, ffn2_up, ffn2_down):
    pos = jnp.arange(x.shape[1])
    for l in range(DEPTH):
        x = hybrid_layer(x, ffn1_norm[l], ffn1_gate[l], ffn1_up[l], ffn1_down[l], mix_norm[l], w_in[l],
                         a_q_norm[l], a_k_norm[l], rel_bias, b_gate_up[l], b_gate_bias[l], b_out_norm[l],
                         c_dw_w[l], c_dw_b[l], c_norm[l], d_qa_norm[l], d_uq[l], d_kva_norm[l], d_ukv[l],
                         d_q_norm[l], d_k_norm[l], w_out[l], ffn2_norm[l], ffn2_gate[l], ffn2_up[l],
                         ffn2_down[l], pos)
    return x
```

```python
import contextlib
import math
import numpy as np
import ml_dtypes

import concourse.bass as bass
import concourse.mybir as mybir
from concourse.bass_utils import run_bass_kernel_spmd

F32 = mybir.dt.float32
BF16 = mybir.dt.bfloat16
AF = mybir.ActivationFunctionType
ALU = mybir.AluOpType
AX = mybir.AxisListType

D_MODEL = 1024
SEQ = 4096
BATCH = 4
DEPTH = 4
D_FF = 2816
EPS = 1e-6
IN_WIDTH = 2776
IN_WIDTH_P = 2792

SEM_ROT = 4000


class Buf:
    __slots__ = ("name", "last_w", "readers", "lane")

    def __init__(self, name):
        self.name = name
        self.last_w = None
        self.readers = []
        self.lane = None


class Lane:
    def __init__(self, name, inc):
        self.name = name
        self.inc = inc
        self.n = 0
        self.sems = []
        self.rot = SEM_ROT // inc


class Op:
    __slots__ = ("eng", "fn", "raw", "other", "lane", "sig", "signal", "is_dma")


class Prog:
    ENG_NAMES = ("pe", "act", "dve", "pool", "sp")

    def __init__(self, nc):
        self.nc = nc
        self.engs = {"pe": nc.tensor, "act": nc.scalar, "dve": nc.vector, "pool": nc.gpsimd, "sp": nc.sync}
        self.ops = []
        self.eng_lane = {e: Lane("L_" + e, 1) for e in self.ENG_NAMES}
        self.eng_last = {e: None for e in self.ENG_NAMES}
        self.bufs = []
        self.free_lanes = {"pool": [], "sp": [], "act": []}
        self.used_lanes = []
        self.nlanes = 0
        self.emitted = 0
        self.waited = {e: {} for e in self.ENG_NAMES}
        self.lane_last = {}
        self.nwait = 0

    def buf(self, name):
        b = Buf(name)
        self.bufs.append(b)
        return b

    def bufs_n(self, name, n):
        return [self.buf(f"{name}{i}") for i in range(n)]

    def _add(self, eng, fn, reads, writes, dma_buf=None):
        op = Op()
        op.eng = eng
        op.fn = fn
        op.is_dma = dma_buf is not None
        idx = len(self.ops)
        raw = set()
        other = set()
        for b in reads:
            if b.last_w is not None:
                raw.add(b.last_w)
        for b in writes:
            if b.last_w is not None:
                other.add(b.last_w)
            for r in b.readers:
                other.add(r)
        op.raw = raw
        op.other = other - raw
        if op.is_dma:
            if dma_buf.lane is None:
                fl = self.free_lanes[eng]
                if fl:
                    dma_buf.lane = fl.pop()
                else:
                    dma_buf.lane = Lane(f"D{self.nlanes}{eng}", 16)
                    dma_buf.lane.q = eng
                    self.nlanes += 1
                self.used_lanes.append(dma_buf.lane)
            assert dma_buf.lane.q == eng, (dma_buf.name, dma_buf.lane.q, eng)
            op.lane = dma_buf.lane
            self.lane_last[id(op.lane)] = idx
        else:
            op.lane = self.eng_lane[eng]
        op.sig = None
        op.signal = op.is_dma
        self.ops.append(op)
        if not op.is_dma:
            self.eng_last[eng] = idx
        for b in writes:
            b.last_w = idx
            b.readers = []
        for b in reads:
            if b.last_w != idx:
                b.readers.append(idx)
        return idx

    def op(self, eng, fn, reads=(), writes=()):
        return self._add(eng, fn, reads, writes)

    def dma(self, q, fn, reads, writes, lane_buf):
        return self._add(q, fn, reads, writes, dma_buf=lane_buf)

    def barrier(self):
        allp = set(v for v in self.eng_last.values() if v is not None) | set(self.lane_last.values())
        for e in self.ENG_NAMES:
            op = Op()
            op.eng = e
            op.fn = None
            op.is_dma = False
            op.raw = set(allp)
            op.other = set()
            op.lane = self.eng_lane[e]
            op.sig = None
            op.signal = False
            self.ops.append(op)
        for b in self.bufs:
            b.last_w = None
            b.readers = []
            b.lane = None
        for ln in self.used_lanes:
            self.free_lanes[ln.q].append(ln)
        self.used_lanes = []
        self.emit()

    def _sem(self, lane, sig):
        k = (sig - 1) // lane.rot
        while len(lane.sems) <= k:
            lane.sems.append(self.nc.alloc_semaphore(name=f"{lane.name}_{len(lane.sems)}"))
        return lane.sems[k], (sig - k * lane.rot) * lane.inc

    def emit(self):
        ops = self.ops
        lo = self.emitted
        need = {}
        for i in range(lo, len(ops)):
            o = ops[i]
            w = self.waited[o.eng]
            lst = {}
            for d in (o.raw | o.other):
                if d < lo:
                    continue
                od = ops[d]
                if (not od.is_dma) and (not o.is_dma) and od.eng == o.eng and d not in o.raw:
                    continue
                if od.eng == "pe" and o.eng == "pe" and not od.is_dma and not o.is_dma:
                    continue
                lid = id(od.lane)
                if w.get(lid, -1) >= d:
                    continue
                if lst.get(lid, (-1, None))[0] < d:
                    lst[lid] = (d, od.lane)
            need[i] = lst
            for lid, (d, lane) in lst.items():
                w[lid] = d
                ops[d].signal = True
        for i in range(lo, len(ops)):
            o = ops[i]
            if o.signal and o.sig is None:
                o.lane.n += 1
                o.sig = o.lane.n
        for i in range(lo, len(ops)):
            o = ops[i]
            eng = self.engs[o.eng]
            for lid, (d, lane) in need[i].items():
                sem, v = self._sem(lane, ops[d].sig)
                eng.wait_ge(sem, v)
                self.nwait += 1
            if o.fn is None:
                continue
            ins = o.fn()
            if o.signal:
                sem, v = self._sem(o.lane, o.sig)
                ins.then_inc(sem, o.lane.inc)
            o.fn = None
        self.emitted = len(ops)


class Ctx:
    def __init__(self, nc):
        self.nc = nc
        self.P = Prog(nc)
        self.dram = {}
        self.stack = None

    def sb(self, name, shape, dt):
        self.uid = getattr(self, "uid", 0) + 1
        t = self.stack.enter_context(self.nc.sbuf_tensor(f"{name}_u{self.uid}", list(shape), dt))
        return t

    def ps(self, name, shape, dt):
        self.uid = getattr(self, "uid", 0) + 1
        t = self.stack.enter_context(self.nc.psum_tensor(f"{name}_u{self.uid}", list(shape), dt))
        return t


def load_consts(C, st):
    nc, P = C.nc, C.P
    ident = st.enter_context(nc.sbuf_tensor("ident_sb", [128, 128], BF16))
    C.ident = ident
    C.b_ident = P.buf("ident")
    P.dma("pool", lambda: nc.gpsimd.dma_start(out=ident[:], in_=C.dram["ident"][:, :]), [], [C.b_ident], C.b_ident)
    C.eps_t = st.enter_context(nc.sbuf_tensor("eps_t", [128, 1], F32))
    C.b_cst = P.buf("cst")
    P.op("pool", lambda: nc.gpsimd.memset(C.eps_t[:], EPS), [], [C.b_cst])
    C.ones_bf = st.enter_context(nc.sbuf_tensor("ones_bf", [128, 128], BF16))
    P.op("pool", lambda: nc.gpsimd.memset(C.ones_bf[:], 1.0), [], [C.b_cst])


def phase_ffn(C, x_src, x_dst, xb_src, xb_dst, gain, wg, wu, wd):
    nc, P = C.nc, C.P
    TS = 1024
    NST = SEQ // TS
    NTB = TS // 512
    NFT = D_FF // 128
    panels = [(f0, min(512, D_FF - f0)) for f0 in range(0, D_FF, 512)]
    with contextlib.ExitStack() as st:
        C.stack = st
        gsb = C.sb("gsb", [128, 8], F32)
        xt = [C.sb(f"xt{i}", [128, 1024], F32) for i in range(3)]
        junk = C.sb("junk", [128, 1024], BF16)
        ss = [C.sb(f"ss{i}", [128, 1], F32) for i in range(3)]
        rs = [C.sb(f"rs{i}", [128, 1], F32) for i in range(3)]
        xn = [C.sb(f"xn{i}", [128, 1024], BF16) for i in range(2)]
        xnT2 = [C.sb(f"xnT{i}", [128, 8, TS], BF16) for i in range(2)]
        aT = C.sb("aT", [128, NFT, TS], BF16)
        wgp = [C.sb(f"wgp{i}", [128, 8, 512], BF16) for i in range(2)]
        wup = [C.sb(f"wup{i}", [128, 8, 512], BF16) for i in range(2)]
        wds = C.sb("wds", [128, NFT, 1024], BF16)
        sgt = [C.sb(f"sgt{i}", [128, 512], BF16) for i in range(2)]
        ot = [C.sb(f"ot{i}", [128, 1024], F32) for i in range(2)]
        pst = [C.ps(f"pst{i}", [128, 8, 128], BF16) for i in range(2)]
        pg = [C.ps(f"pg{i}", [128, 512], F32) for i in range(2)]
        pu = [C.ps(f"pu{i}", [128, 512], F32) for i in range(2)]
        pd = [C.ps(f"pd{i}", [128, 512], F32) for i in range(2)]

        b_gsb = P.buf("gsb")
        b_xt = P.bufs_n("xt", 3); b_ss = P.bufs_n("ss", 3); b_rs = P.bufs_n("rs", 3)
        b_junk = P.buf("junk")
        b_xn = P.bufs_n("xn", 2)
        b_xnT2 = [P.bufs_n(f"xnT{i}_", TS // 128) for i in range(2)]
        b_aT = [[P.buf(f"aT{f}_{tb}") for tb in range(NTB)] for f in range(NFT)]
        b_wgp = P.bufs_n("wgp", 2); b_wup = P.bufs_n("wup", 2)
        b_wds = P.bufs_n("wds", NFT)
        b_sgt = P.bufs_n("sgt", 2)
        b_ot = P.bufs_n("ot", 2)
        b_pst = P.bufs_n("pst", 2)
        b_pg = P.bufs_n("pg", 2); b_pu = P.bufs_n("pu", 2); b_pd = P.bufs_n("pd", 2)

        P.dma("sp", lambda: nc.sync.dma_start(out=gsb[:], in_=gain), [], [b_gsb], b_gsb)
        wd_v = wd.rearrange("(fc p) m -> p fc m", p=128)
        wg_v = wg.rearrange("(dc p) f -> p dc f", p=128)
        wu_v = wu.rearrange("(dc p) f -> p dc f", p=128)
        cnt = dict(x=0, xn=0, pst=0, pan=0, gu=0, sg=0, pd=0, ot=0)

        def load_wd():
            for f0 in range(0, NFT, 2):
                P.dma("pool", (lambda f0=f0: nc.gpsimd.dma_start(out=wds[:, f0:f0 + 2, :], in_=wd_v[:, f0:f0 + 2, :])),
                      [], b_wds[f0:f0 + 2], b_wds[f0])

        def load_panel(pi):
            f0, w = panels[pi]
            sl = cnt["pan"] % 2; cnt["pan"] += 1
            for h in range(2):
                P.dma("pool", (lambda sl=sl, f0=f0, w=w, h=h: nc.gpsimd.dma_start(out=wgp[sl][:, 4 * h:4 * h + 4, 0:w], in_=wg_v[:, 4 * h:4 * h + 4, f0:f0 + w])),
                      [], [b_wgp[sl]], b_wgp[sl])
            for h in range(2):
                P.dma("pool", (lambda sl=sl, f0=f0, w=w, h=h: nc.gpsimd.dma_start(out=wup[sl][:, 4 * h:4 * h + 4, 0:w], in_=wu_v[:, 4 * h:4 * h + 4, f0:f0 + w])),
                      [], [b_wup[sl]], b_wup[sl])
            return sl

        def stage1_tile(s, j):
            xnT = xnT2[s % 2]; b_xnT = b_xnT2[s % 2]
            tt = s * (TS // 128) + j
            xs = cnt["x"] % 3; cnt["x"] += 1
            ns = cnt["xn"] % 2; cnt["xn"] += 1
            pp = cnt["pst"] % 2; cnt["pst"] += 1
            P.dma("sp", (lambda: nc.sync.dma_start(out=xt[xs][:], in_=x_src[tt * 128:(tt + 1) * 128, :])),
                  [xb_src[tt]], [b_xt[xs]], b_xt[xs])
            P.op("act", (lambda: nc.scalar.activation(out=junk[:], in_=xt[xs][:], func=AF.Square, accum_out=ss[xs][:])),
                 [b_xt[xs]], [b_junk, b_ss[xs]])
            P.op("act", (lambda: nc.scalar.activation(out=ss[xs][:], in_=ss[xs][:], func=AF.Ln, scale=1.0 / D_MODEL, bias=C.eps_t[:])),
                 [b_ss[xs], C.b_cst], [b_ss[xs]])
            P.op("act", (lambda: nc.scalar.activation(out=rs[xs][:], in_=ss[xs][:], func=AF.Exp, scale=-0.5)),
                 [b_ss[xs]], [b_rs[xs]])
            P.op("dve", (lambda: nc.vector.tensor_scalar(out=xn[ns][:], in0=xt[xs][:], scalar1=rs[xs][:], scalar2=None, op0=ALU.mult)),
                 [b_xt[xs], b_rs[xs]], [b_xn[ns]])
            for dc in range(8):
                P.op("pe", (lambda dc=dc: nc.tensor.transpose(out=pst[pp][:, dc, :], in_=xn[ns][:, dc * 128:(dc + 1) * 128], identity=C.ident[:])),
                     [b_xn[ns], C.b_ident], [b_pst[pp]])
            P.op("dve", (lambda: nc.vector.tensor_tensor(out=xnT[:, :, j * 128:(j + 1) * 128], in0=pst[pp][:], in1=gsb[:, :, None].broadcast_to([128, 8, 128]), op=ALU.mult)),
                 [b_pst[pp], b_gsb], [b_xnT[j]])

        for j in range(TS // 128):
            stage1_tile(0, j)
        sched = [2, 2, 1, 1, 1, 1]
        for s in range(NST):
            xnT = xnT2[s % 2]; b_xnT = b_xnT2[s % 2]
            nxt_tile = 0
            if s == 0:
                nxt = load_panel(0)
            for pi, (f0, w) in enumerate(panels):
                sl = nxt
                if pi + 1 < len(panels):
                    nxt = load_panel(pi + 1)
                elif s + 1 < NST:
                    nxt = load_panel(0)
                if s == 0 and pi == 0:
                    load_wd()
                for fl in range(w // 128):
                    f = f0 // 128 + fl
                    for tb in range(NTB):
                        gs = cnt["gu"] % 2; cnt["gu"] += 1
                        sgs = cnt["sg"] % 2; cnt["sg"] += 1
                        rd = [b_xnT[tb * 4 + k] for k in range(4)]
                        for dc in range(8):
                            P.op("pe", (lambda sl=sl, gs=gs, fl=fl, tb=tb, dc=dc, xnT=xnT: nc.tensor.matmul(out=pg[gs][:], lhsT=wgp[sl][:, dc, fl * 128:(fl + 1) * 128], rhs=xnT[:, dc, tb * 512:(tb + 1) * 512], start=(dc == 0), stop=(dc == 7))),
                                 [b_wgp[sl]] + rd, [b_pg[gs]])
                        for dc in range(8):
                            P.op("pe", (lambda sl=sl, gs=gs, fl=fl, tb=tb, dc=dc, xnT=xnT: nc.tensor.matmul(out=pu[gs][:], lhsT=wup[sl][:, dc, fl * 128:(fl + 1) * 128], rhs=xnT[:, dc, tb * 512:(tb + 1) * 512], start=(dc == 0), stop=(dc == 7))),
                                 [b_wup[sl]] + rd, [b_pu[gs]])
                        P.op("act", (lambda gs=gs, sgs=sgs: nc.scalar.activation(out=sgt[sgs][:], in_=pg[gs][:], func=AF.Silu)),
                             [b_pg[gs]], [b_sgt[sgs]])
                        P.op("dve", (lambda gs=gs, sgs=sgs, f=f, tb=tb: nc.vector.tensor_tensor(out=aT[:, f, tb * 512:(tb + 1) * 512], in0=pu[gs][:], in1=sgt[sgs][:], op=ALU.mult)),
                             [b_pu[gs], b_sgt[sgs]], [b_aT[f][tb]])
                if s + 1 < NST:
                    for _ in range(sched[pi]):
                        stage1_tile(s + 1, nxt_tile); nxt_tile += 1
            for j in range(TS // 128):
                tt = s * (TS // 128) + j
                xs = cnt["x"] % 3; cnt["x"] += 1
                os_ = cnt["ot"] % 2; cnt["ot"] += 1
                P.dma("sp", (lambda xs=xs, tt=tt: nc.sync.dma_start(out=xt[xs][:], in_=x_src[tt * 128:(tt + 1) * 128, :])),
                      [xb_src[tt]], [b_xt[xs]], b_xt[xs])
                for mh in range(2):
                    ds = cnt["pd"] % 2; cnt["pd"] += 1
                    for fc in range(NFT):
                        P.op("pe", (lambda ds=ds, fc=fc, j=j, mh=mh: nc.tensor.matmul(out=pd[ds][:], lhsT=aT[:, fc, j * 128:(j + 1) * 128], rhs=wds[:, fc, mh * 512:(mh + 1) * 512], start=(fc == 0), stop=(fc == NFT - 1))),
                             [b_aT[fc][j // 4], b_wds[fc]], [b_pd[ds]])
                    P.op("dve", (lambda ds=ds, os_=os_, xs=xs, mh=mh: nc.vector.scalar_tensor_tensor(out=ot[os_][:, mh * 512:(mh + 1) * 512], in0=pd[ds][:], scalar=0.5, in1=xt[xs][:, mh * 512:(mh + 1) * 512], op0=ALU.mult, op1=ALU.add)),
                         [b_pd[ds], b_xt[xs]], [b_ot[os_]])
                P.dma("sp", (lambda os_=os_, tt=tt: nc.sync.dma_start(out=x_dst[tt * 128:(tt + 1) * 128, :], in_=ot[os_][:])),
                      [b_ot[os_]], [xb_dst[tt]], b_ot[os_])
        P.barrier()
    C.stack = None


FM = dict(aq=0, ak=256, iq=512, bq=768, bk=896, ca=1024, cg=1280, dcq=1536, dckv=1792, ik=1920, dkpe=1952, bg=1984)
N_FM = 2016
TM = dict(av=0, br=256, bv=512, iw=768)
N_TM = 776
ORIG = dict(aq=(0, 256), ak=(256, 512), av=(512, 768), iq=(768, 1024), ik=(1024, 1056), iw=(1056, 1064),
            bq=(1064, 1192), bk=(1192, 1320), bv=(1320, 1576), bg=(1576, 1592), br=(1592, 1848),
            cu=(1848, 2360), dcq=(2360, 2616), dckv=(2616, 2744), dkpe=(2744, 2776))


def w_in_perm():
    order = ["aq", "ak", "iq", "bq", "bk", "cu", "dcq", "dckv", "ik", "dkpe", "bg", "av", "br", "bv", "iw"]
    parts = []
    for k in order:
        parts.append(np.arange(*ORIG[k]))
        if k == "bg":
            parts.append(np.full(16, -1))
    idx = np.concatenate(parts)
    assert idx.shape[0] == IN_WIDTH_P
    return idx


def permute_w_in(w):
    idx = w_in_perm()
    out = np.zeros((w.shape[0], IN_WIDTH_P), np.float32)
    m = idx >= 0
    out[:, m] = np.asarray(w)[:, idx[m]]
    return out


def emit_norm_T(C, st_bufs, x_src, tt, j, xnT_ap_fn, b_out):
    nc, P = C.nc, C.P
    S = st_bufs
    xs = S["cnt"]["x"] % 3; S["cnt"]["x"] += 1
    ns = S["cnt"]["xn"] % 2; S["cnt"]["xn"] += 1
    pp = S["cnt"]["pst"] % 2; S["cnt"]["pst"] += 1
    xt, ss, rs, xn, pst, junk, gsb = S["xt"], S["ss"], S["rs"], S["xn"], S["pst"], S["junk"], S["gsb"]
    b = S["b"]
    P.dma("sp", (lambda: nc.sync.dma_start(out=xt[xs][:], in_=x_src[tt * 128:(tt + 1) * 128, :])),
          [], [b["xt"][xs]], b["xt"][xs])
    P.op("act", (lambda: nc.scalar.activation(out=junk[:], in_=xt[xs][:], func=AF.Square, accum_out=ss[xs][:])),
         [b["xt"][xs]], [b["junk"], b["ss"][xs]])
    P.op("act", (lambda: nc.scalar.activation(out=ss[xs][:], in_=ss[xs][:], func=AF.Ln, scale=1.0 / D_MODEL, bias=C.eps_t[:])),
         [b["ss"][xs], C.b_cst], [b["ss"][xs]])
    P.op("act", (lambda: nc.scalar.activation(out=rs[xs][:], in_=ss[xs][:], func=AF.Exp, scale=-0.5)),
         [b["ss"][xs]], [b["rs"][xs]])
    P.op("dve", (lambda: nc.vector.tensor_scalar(out=xn[ns][:], in0=xt[xs][:], scalar1=rs[xs][:], scalar2=None, op0=ALU.mult)),
         [b["xt"][xs], b["rs"][xs]], [b["xn"][ns]])
    for dc in range(8):
        P.op("pe", (lambda dc=dc: nc.tensor.transpose(out=pst[pp][:, dc, :], in_=xn[ns][:, dc * 128:(dc + 1) * 128], identity=C.ident[:])),
             [b["xn"][ns], C.b_ident], [b["pst"][pp]])
    P.op("dve", (lambda: nc.vector.tensor_tensor(out=xnT_ap_fn(j), in0=pst[pp][:], in1=gsb[:, :, None].broadcast_to([128, 8, 128]), op=ALU.mult)),
         [b["pst"][pp], b["gsb"]], [b_out])
    return xs


def norm_T_state(C, gain):
    nc, P = C.nc, C.P
    S = dict(cnt=dict(x=0, xn=0, pst=0))
    S["gsb"] = C.sb("gsb", [128, 8], F32)
    S["xt"] = [C.sb(f"xt{i}", [128, 1024], F32) for i in range(3)]
    S["junk"] = C.sb("junk", [128, 1024], BF16)
    S["ss"] = [C.sb(f"ss{i}", [128, 1], F32) for i in range(3)]
    S["rs"] = [C.sb(f"rs{i}", [128, 1], F32) for i in range(3)]
    S["xn"] = [C.sb(f"xn{i}", [128, 1024], BF16) for i in range(2)]
    S["pst"] = [C.ps(f"pst{i}", [128, 8, 128], BF16) for i in range(2)]
    S["b"] = dict(gsb=P.buf("gsb"), xt=P.bufs_n("xt", 3), ss=P.bufs_n("ss", 3), rs=P.bufs_n("rs", 3),
                  junk=P.buf("junk"), xn=P.bufs_n("xn", 2), pst=P.bufs_n("pst", 2))
    P.dma("sp", lambda: nc.sync.dma_start(out=S["gsb"][:], in_=gain), [], [S["b"]["gsb"]], S["b"]["gsb"])
    return S


def phase_proj(C, x_src, gain, w_in):
    nc, P = C.nc, C.P
    zT, ztm = C.dram["zT"], C.dram["ztm"]
    NMT = (N_FM + 127) // 128
    with contextlib.ExitStack() as st:
        C.stack = st
        S = norm_T_state(C, gain)
        win = C.sb("win", [128, 8, IN_WIDTH_P], BF16)
        NCB = (IN_WIDTH_P + 511) // 512
        b_win = P.bufs_n("win", NCB)
        w_v = w_in.rearrange("(dc p) f -> p dc f", p=128)
        for cb in range(NCB):
            c0 = cb * 512
            c1 = min(IN_WIDTH_P, c0 + 512)
            P.dma("pool", (lambda c0=c0, c1=c1: nc.gpsimd.dma_start(out=win[:, :, c0:c1], in_=w_v[:, :, c0:c1])), [], [b_win[cb]], b_win[cb])
        xnT = [C.sb(f"xnT{i}", [128, 8, 512], BF16) for i in range(2)]
        b_xnT = [P.bufs_n(f"xnT{i}_", 4) for i in range(2)]
        zsb = [C.sb(f"zsb{i}", [128, 512], BF16) for i in range(3)]
        b_zsb = P.bufs_n("zsb", 3)
        zts = [C.sb(f"zts{i}", [128, N_TM], BF16) for i in range(2)]
        b_zts = P.bufs_n("zts", 2)
        pz = [C.ps(f"pz{i}", [128, 512], F32) for i in range(6)]
        b_pz = P.bufs_n("pz", 6)
        cz = 0; cs = 0; ct = 0; cz2 = 0
        for tb in range(SEQ // 512):
            xi = tb % 2
            for j in range(4):
                emit_norm_T(C, S, x_src, tb * 4 + j, j, (lambda j, xi=xi: xnT[xi][:, :, j * 128:(j + 1) * 128]), b_xnT[xi][j])
            import os as _os
            for mt in range(NMT if not _os.environ.get('NO_FM') else 0):
                rows = min(128, N_FM - mt * 128)
                pi = cz % 4; cz += 1
                si = cs % 3; cs += 1
                for dc in range(8):
                    P.op("pe", (lambda dc=dc, pi=pi, mt=mt, rows=rows, xi=xi: nc.tensor.matmul(out=pz[pi][0:rows, :], lhsT=win[:, dc, mt * 128:mt * 128 + rows], rhs=xnT[xi][:, dc, :], start=(dc == 0), stop=(dc == 7))),
                         [b_win[(mt * 128) // 512]] + b_xnT[xi], [b_pz[pi]])
                if mt % 2 == 0:
                    P.op("act", (lambda pi=pi, si=si, rows=rows: nc.scalar.copy(out=zsb[si][0:rows, :], in_=pz[pi][0:rows, :])), [b_pz[pi]], [b_zsb[si]])
                else:
                    P.op("dve", (lambda pi=pi, si=si, rows=rows: nc.vector.tensor_copy(out=zsb[si][0:rows, :], in_=pz[pi][0:rows, :])), [b_pz[pi]], [b_zsb[si]])
                P.dma("sp", (lambda si=si, mt=mt, rows=rows, tb=tb: nc.sync.dma_start(out=zT[mt * 128:mt * 128 + rows, tb * 512:(tb + 1) * 512], in_=zsb[si][0:rows, :])),
                      [b_zsb[si]], [], b_zsb[si])
            for j in range(4 if not _os.environ.get('NO_TM') else 0):
                ti = ct % 2; ct += 1
                for (c0, n) in ((0, 512), (512, N_TM - 512)):
                    pi = 4 + cz2 % 2; cz2 += 1
                    for dc in range(8):
                        P.op("pe", (lambda dc=dc, pi=pi, j=j, c0=c0, n=n, xi=xi: nc.tensor.matmul(out=pz[pi][:, 0:n], lhsT=xnT[xi][:, dc, j * 128:(j + 1) * 128], rhs=win[:, dc, N_FM + c0:N_FM + c0 + n], start=(dc == 0), stop=(dc == 7))),
                             [b_win[cbi] for cbi in range((N_FM + c0) // 512, (N_FM + c0 + n - 1) // 512 + 1)] + [b_xnT[xi][j]], [b_pz[pi]])
                    if c0 == 0:
                        P.op("act", (lambda pi=pi, ti=ti, c0=c0, n=n: nc.scalar.copy(out=zts[ti][:, c0:c0 + n], in_=pz[pi][:, 0:n])), [b_pz[pi]], [b_zts[ti]])
                    else:
                        P.op("dve", (lambda pi=pi, ti=ti, c0=c0, n=n: nc.vector.tensor_copy(out=zts[ti][:, c0:c0 + n], in_=pz[pi][:, 0:n])), [b_pz[pi]], [b_zts[ti]])
                tt = tb * 4 + j
                P.dma("sp", (lambda ti=ti, tt=tt: nc.sync.dma_start(out=ztm[tt * 128:(tt + 1) * 128, :], in_=zts[ti][:])),
                      [b_zts[ti]], [], b_zts[ti])
        P.barrier()
    C.stack = None


def phase_conv(C, cw, cb, cg):
    nc, P = C.nc, C.P
    zT, yT = C.dram["zT"], C.dram["yT"]
    KW = 31
    with contextlib.ExitStack() as st:
        C.stack = st
        cw_sb = C.sb("cw", [128, 2, KW], F32); cb_sb = C.sb("cb", [128, 2], F32); cg_sb = C.sb("cg", [128, 2], F32)
        b_par = P.bufs_n("cpar", 3)
        P.dma("sp", lambda: nc.sync.dma_start(out=cw_sb[:], in_=cw), [], [b_par[0]], b_par[0])
        P.dma("sp", lambda: nc.sync.dma_start(out=cb_sb[:], in_=cb), [], [b_par[1]], b_par[1])
        P.dma("sp", lambda: nc.sync.dma_start(out=cg_sb[:], in_=cg), [], [b_par[2]], b_par[2])
        dg = C.sb("dg", [128, 2, KW, 128], BF16)
        b_dg = P.bufs_n("dg", 2)
        for ct in range(2):
            for j in range(KW):
                P.op("pool", (lambda ct=ct, j=j: nc.gpsimd.tensor_scalar(out=dg[:, ct, j, :], in0=C.ident[:], scalar1=cw_sb[:, ct, j:j + 1], scalar2=None, op0=ALU.mult)),
                     [C.b_ident, b_par[0]], [b_dg[ct]])
        a_sb = C.sb("a_sb", [128, SEQ], BF16); g_sb = C.sb("g_sb", [128, SEQ], BF16); sg = C.sb("sg", [128, SEQ], BF16)
        b_a = P.buf("a_sb"); b_g = P.buf("g_sb"); b_sg = P.buf("sg")
        hp = [C.sb(f"hp{i}", [128, 32 + SEQ], BF16) for i in range(2)]
        b_hp = P.bufs_n("hp", 2)
        for ct in range(2):
            P.dma("sp", (lambda ct=ct: nc.sync.dma_start(out=a_sb[:], in_=zT[FM["ca"] + ct * 128:FM["ca"] + (ct + 1) * 128, :])), [], [b_a], b_a)
            P.dma("sp", (lambda ct=ct: nc.sync.dma_start(out=g_sb[:], in_=zT[FM["cg"] + ct * 128:FM["cg"] + (ct + 1) * 128, :])), [], [b_g], b_g)
            P.op("pool", (lambda ct=ct: nc.gpsimd.memset(hp[ct][:, 0:32], 0.0)), [], [b_hp[ct]])
            P.op("act", (lambda: nc.scalar.activation(out=sg[:], in_=g_sb[:], func=AF.Sigmoid)), [b_g], [b_sg])
            P.op("dve", (lambda ct=ct: nc.vector.tensor_tensor(out=hp[ct][:, 32:], in0=a_sb[:], in1=sg[:], op=ALU.mult)), [b_a, b_sg], [b_hp[ct]])
        pc = [C.ps(f"pc{i}", [128, 512], F32) for i in range(4)]
        b_pc = P.bufs_n("pc", 4)
        pss = [C.ps(f"pss{i}", [128, 512], F32) for i in range(2)]
        b_pss = P.bufs_n("pss", 2)
        cvb = [C.sb(f"cvb{i}", [128, 512], F32) for i in range(4)]; b_cvb = P.bufs_n("cvb", 4)
        sq = [C.sb(f"sq{i}", [128, 512], BF16) for i in range(4)]; b_sq = P.bufs_n("sq", 4)
        lnt = [C.sb(f"lnt{i}", [128, 512], F32) for i in range(2)]; b_lnt = P.bufs_n("lnt", 2)
        rr = [C.sb(f"rr{i}", [128, 512], F32) for i in range(2)]; b_rr = P.bufs_n("rr", 2)
        uu = [C.sb(f"uu{i}", [128, 512], F32) for i in range(2)]; b_uu = P.bufs_n("uu", 2)
        ys = [C.sb(f"ys{i}", [128, 512], BF16) for i in range(2)]; b_ys = P.bufs_n("ys", 2)
        cu = 0
        for tb in range(SEQ // 512):
            k2 = tb % 2
            for ct in range(2):
                k = (tb % 2) * 2 + ct
                for j in range(KW):
                    c0 = 32 + tb * 512 - (KW - 1) + j
                    P.op("pe", (lambda k=k, ct=ct, j=j, c0=c0: nc.tensor.matmul(out=pc[k][:], lhsT=dg[:, ct, j, :], rhs=hp[ct][:, c0:c0 + 512], start=(j == 0), stop=(j == KW - 1))),
                         [b_dg[ct], b_hp[ct]], [b_pc[k]])
                P.op("act", (lambda k=k, ct=ct: nc.scalar.activation(out=cvb[k][:], in_=pc[k][:], func=AF.Identity, bias=cb_sb[:, ct:ct + 1])), [b_pc[k], b_par[1]], [b_cvb[k]])
                P.op("act", (lambda k=k, ct=ct: nc.scalar.activation(out=sq[k][:], in_=pc[k][:], func=AF.Square, bias=cb_sb[:, ct:ct + 1])), [b_pc[k], b_par[1]], [b_sq[k]])
            for ct in range(2):
                k = (tb % 2) * 2 + ct
                P.op("pe", (lambda k=k, ct=ct, k2=k2: nc.tensor.matmul(out=pss[k2][:], lhsT=C.ones_bf[:], rhs=sq[k][:], start=(ct == 0), stop=(ct == 1))),
                     [C.b_cst, b_sq[k]], [b_pss[k2]])
            P.op("act", (lambda k2=k2: nc.scalar.activation(out=lnt[k2][:], in_=pss[k2][:], func=AF.Ln, scale=1.0 / 256, bias=C.eps_t[:])), [b_pss[k2], C.b_cst], [b_lnt[k2]])
            P.op("act", (lambda k2=k2: nc.scalar.activation(out=rr[k2][:], in_=lnt[k2][:], func=AF.Exp, scale=-0.5)), [b_lnt[k2]], [b_rr[k2]])
            for ct in range(2):
                k = (tb % 2) * 2 + ct
                ui = cu % 2; cu += 1
                P.op("dve", (lambda k=k, ct=ct, k2=k2, ui=ui: nc.vector.scalar_tensor_tensor(out=uu[ui][:], in0=cvb[k][:], scalar=cg_sb[:, ct:ct + 1], in1=rr[k2][:], op0=ALU.mult, op1=ALU.mult)),
                     [b_cvb[k], b_par[2], b_rr[k2]], [b_uu[ui]])
                P.op("act", (lambda ui=ui: nc.scalar.activation(out=ys[ui][:], in_=uu[ui][:], func=AF.Silu)), [b_uu[ui]], [b_ys[ui]])
                P.dma("sp", (lambda ui=ui, ct=ct, tb=tb: nc.sync.dma_start(out=yT[512 + ct * 128:512 + (ct + 1) * 128, tb * 512:(tb + 1) * 512], in_=ys[ui][:])),
                      [b_ys[ui]], [], b_ys[ui])
        P.barrier()
    C.stack = None


def phase_mla(C, prm):
    nc, P = C.nc, C.P
    zT, ytm = C.dram["zT"], C.dram["ytm"]
    D = C.dram
    NB = SEQ // 512
    with contextlib.ExitStack() as outer:
        C.stack = outer
        qT = C.sb("qT", [96, 4, SEQ], BF16); kT = C.sb("kT", [96, 4, SEQ], BF16)
        v1 = C.sb("v1", [128, SEQ // 128, 4, 65], BF16)
        ctab = C.sb("ctab", [96, SEQ], F32); stab = C.sb("stab", [96, SEQ], F32)
        tri = C.sb("tri", [128, 128], BF16)
        with contextlib.ExitStack() as st:
            C.stack = st
            b_tab = P.bufs_n("tab", 3)
            P.dma("sp", lambda: nc.sync.dma_start(out=ctab[64:96, :], in_=D["rope_c"][:, :]), [], [b_tab[0]], b_tab[0])
            P.dma("sp", lambda: nc.sync.dma_start(out=stab[64:96, :], in_=D["rope_s"][:, :]), [], [b_tab[1]], b_tab[1])
            P.dma("pool", lambda: nc.gpsimd.dma_start(out=tri[:], in_=D["tri"][:, :]), [], [b_tab[2]], b_tab[2])
            wuq_f = C.sb("wuq_f", [128, 2, 384], F32); wuq_s = C.sb("wuq_s", [128, 2, 384], BF16)
            wk_f = C.sb("wk_f", [128, 4, 96], F32); wk_s = C.sb("wk_s", [128, 4, 96], BF16)
            wv_f = C.sb("wv_f", [128, 256], F32); wv_s = C.sb("wv_s", [128, 256], BF16)
            gqa = C.sb("gqa", [128, 2], F32); gkva = C.sb("gkva", [128, 1], F32)
            gq = C.sb("gq", [96, 1], F32); gk = C.sb("gk", [96, 1], F32); gqs = C.sb("gqs", [96, 1], F32)
            prot = C.sb("prot", [96, 96], BF16); emat = C.sb("emat", [32, 96], BF16)
            b_w = P.bufs_n("mlaw", 12)
            ld = [(wuq_f, prm["wuq"], "sp"), (wk_f, prm["wk"], "sp"), (wv_f, prm["wv"], "sp"), (gqa, prm["gqa"], "sp"),
                  (gkva, prm["gkva"], "sp"), (gq, prm["gq"], "sp"), (gk, prm["gk"], "sp"),
                  (prot, D["prot"], "pool"), (emat, D["emat"], "pool")]
            for n, (t, a, q) in enumerate(ld):
                if q == "sp":
                    P.dma("sp", (lambda t=t, a=a: nc.sync.dma_start(out=t[:], in_=a)), [], [b_w[n]], b_w[n])
                else:
                    P.dma("pool", (lambda t=t, a=a: nc.gpsimd.dma_start(out=t[:], in_=a)), [], [b_w[n]], b_w[n])
            P.op("dve", lambda: nc.vector.tensor_tensor(out=wuq_s[:], in0=wuq_f[:], in1=gqa[:, :, None].broadcast_to([128, 2, 384]), op=ALU.mult), [b_w[0], b_w[3]], [b_w[9]])
            P.op("dve", lambda: nc.vector.tensor_scalar(out=wk_s[:], in0=wk_f[:], scalar1=gkva[:, 0:1], scalar2=None, op0=ALU.mult), [b_w[1], b_w[4]], [b_w[10]])
            P.op("dve", lambda: nc.vector.tensor_scalar(out=wv_s[:], in0=wv_f[:], scalar1=gkva[:, 0:1], scalar2=None, op0=ALU.mult), [b_w[2], b_w[4]], [b_w[11]])
            P.op("dve", lambda: nc.vector.tensor_scalar(out=gqs[:], in0=gq[:], scalar1=float(96 ** -0.5), scalar2=None, op0=ALU.mult), [b_w[5]], [b_w[5]])
            b_v1 = P.buf("v1")
            P.op("pool", lambda: nc.gpsimd.memset(v1[:], 1.0), [], [b_v1])
            b_qT = [[P.buf(f"qT{h}_{tb}") for tb in range(NB)] for h in range(4)]
            b_kT = [[P.buf(f"kT{h}_{tb}") for tb in range(NB)] for h in range(4)]

            cq = [C.sb(f"cq{i}", [128, 2, 512], BF16) for i in range(2)]; b_cq = P.bufs_n("cq", 2)
            ckv = [C.sb(f"ckv{i}", [128, 512], BF16) for i in range(2)]; b_ckv = P.bufs_n("ckv", 2)
            kpe = [C.sb(f"kpe{i}", [32, 512], BF16) for i in range(2)]; b_kpe = P.bufs_n("kpe", 2)
            sqc = [C.sb(f"sqc{i}", [128, 3, 512], BF16) for i in range(2)]; b_sqc = P.bufs_n("sqc", 2)
            lnt = [C.sb(f"lnt{i}", [128, 512], F32) for i in range(2)]; b_lnt = P.bufs_n("lnt", 2)
            r1 = [C.sb(f"r1{i}", [128, 512], F32) for i in range(2)]; b_r1 = P.bufs_n("r1", 2)
            r2 = [C.sb(f"r2{i}", [128, 512], F32) for i in range(2)]; b_r2 = P.bufs_n("r2", 2)
            cqn = [C.sb(f"cqn{i}", [128, 2, 512], BF16) for i in range(2)]; b_cqn = P.bufs_n("cqn", 2)
            ckvn = [C.sb(f"ckvn{i}", [128, 512], BF16) for i in range(2)]; b_ckvn = P.bufs_n("ckvn", 2)
            sqh = [C.sb(f"sqh{i}", [96, 512], BF16) for i in range(2)]; b_sqh = P.bufs_n("sqh", 2)
            lnh = [C.sb(f"lnh{i}", [96, 512], F32) for i in range(2)]; b_lnh = P.bufs_n("lnh", 2)
            rh = [C.sb(f"rh{i}", [96, 512], F32) for i in range(2)]; b_rh = P.bufs_n("rh", 2)
            t1 = [C.sb(f"t1{i}", [96, 512], F32) for i in range(2)]; b_t1 = P.bufs_n("t1", 2)
            t2 = [C.sb(f"t2{i}", [96, 512], F32) for i in range(2)]; b_t2 = P.bufs_n("t2", 2)
            pss = [C.ps(f"pss{i}", [128, 512], F32) for i in range(2)]; b_pss = P.bufs_n("pss", 2)
            praw = [C.ps(f"praw{i}", [128, 512], F32) for i in range(3)]; b_praw = P.bufs_n("praw", 3)
            prt = [C.ps(f"prt{i}", [128, 512], F32) for i in range(2)]; b_prt = P.bufs_n("prt", 2)
            pvv = C.ps("pvv", [128, 256], F32); b_pvv = P.buf("pvv")
            cn = dict(ss=0, raw=0, h=0)

            def nr_stages(mm_fn, gvec, b_g, dst, b_dst, cols, bidx):
                k = bidx % 2
                st8 = {}

                def s1():
                    ri = cn["raw"] % 3; cn["raw"] += 1
                    st8["ri"] = ri
                    mm_fn(ri)
                    P.op("act", (lambda: nc.scalar.activation(out=sqh[k][:], in_=praw[ri][0:96, :], func=AF.Square)), [b_praw[ri]], [b_sqh[k]])

                def s2():
                    ri = st8["ri"]
                    s2i = cn["ss"] % 2; cn["ss"] += 1
                    P.op("pe", (lambda: nc.tensor.matmul(out=pss[s2i][0:96, :], lhsT=C.ones_bf[0:96, 0:96], rhs=sqh[k][:], start=True, stop=True)), [C.b_cst, b_sqh[k]], [b_pss[s2i]])
                    P.op("act", (lambda: nc.scalar.activation(out=lnh[k][:], in_=pss[s2i][0:96, :], func=AF.Ln, scale=1.0 / 96, bias=C.eps_t[0:96, :])), [b_pss[s2i], C.b_cst], [b_lnh[k]])
                    P.op("act", (lambda: nc.scalar.activation(out=rh[k][:], in_=lnh[k][:], func=AF.Exp, scale=-0.5)), [b_lnh[k]], [b_rh[k]])
                    P.op("dve", (lambda: nc.vector.scalar_tensor_tensor(out=dst, in0=praw[ri][0:96, :], scalar=gvec[:, 0:1], in1=rh[k][:], op0=ALU.mult, op1=ALU.mult)), [b_praw[ri], b_g, b_rh[k]], [b_dst])

                def s3():
                    P.op("pe", (lambda: nc.tensor.matmul(out=prt[k][0:96, :], lhsT=prot[:], rhs=dst, start=True, stop=True)), [b_w[7], b_dst], [b_prt[k]])
                    P.op("dve", (lambda: nc.vector.tensor_tensor(out=t1[k][64:96, :], in0=dst[64:96, :], in1=ctab[64:96, cols], op=ALU.mult)), [b_dst, b_tab[0]], [b_t1[k]])
                    P.op("dve", (lambda: nc.vector.tensor_tensor(out=t2[k][64:96, :], in0=prt[k][64:96, :], in1=stab[64:96, cols], op=ALU.mult)), [b_prt[k], b_tab[1]], [b_t2[k]])
                    P.op("dve", (lambda: nc.vector.tensor_tensor(out=dst[64:96, :], in0=t1[k][64:96, :], in1=t2[k][64:96, :], op=ALU.add)), [b_t1[k], b_t2[k]], [b_dst])
                return (s1, s2, s3)

            for tb in range(NB):
                i2 = tb % 2
                cols = slice(tb * 512, (tb + 1) * 512)
                P.dma("sp", (lambda i2=i2, cols=cols: nc.sync.dma_start(out=cq[i2][:], in_=zT[FM["dcq"]:FM["dcq"] + 256, cols].rearrange("(ct p) t -> p ct t", p=128))), [], [b_cq[i2]], b_cq[i2])
                P.dma("sp", (lambda i2=i2, cols=cols: nc.sync.dma_start(out=ckv[i2][:], in_=zT[FM["dckv"]:FM["dckv"] + 128, cols])), [], [b_ckv[i2]], b_ckv[i2])
                P.dma("sp", (lambda i2=i2, cols=cols: nc.sync.dma_start(out=kpe[i2][:], in_=zT[FM["dkpe"]:FM["dkpe"] + 32, cols])), [], [b_kpe[i2]], b_kpe[i2])
                P.op("act", (lambda i2=i2: nc.scalar.activation(out=sqc[i2][:, 0:2, :], in_=cq[i2][:], func=AF.Square)), [b_cq[i2]], [b_sqc[i2]])
                P.op("act", (lambda i2=i2: nc.scalar.activation(out=sqc[i2][:, 2, :], in_=ckv[i2][:], func=AF.Square)), [b_ckv[i2]], [b_sqc[i2]])
                s2 = cn["ss"] % 2; cn["ss"] += 1
                for ct in range(2):
                    P.op("pe", (lambda i2=i2, ct=ct, s2=s2: nc.tensor.matmul(out=pss[s2][:], lhsT=C.ones_bf[:], rhs=sqc[i2][:, ct, :], start=(ct == 0), stop=(ct == 1))), [C.b_cst, b_sqc[i2]], [b_pss[s2]])
                P.op("act", (lambda i2=i2, s2=s2: nc.scalar.activation(out=lnt[i2][:], in_=pss[s2][:], func=AF.Ln, scale=1.0 / 256, bias=C.eps_t[:])), [b_pss[s2], C.b_cst], [b_lnt[i2]])
                P.op("act", (lambda i2=i2: nc.scalar.activation(out=r1[i2][:], in_=lnt[i2][:], func=AF.Exp, scale=-0.5)), [b_lnt[i2]], [b_r1[i2]])
                P.op("dve", (lambda i2=i2: nc.vector.tensor_tensor(out=cqn[i2][:], in0=cq[i2][:], in1=r1[i2][:, None, :].broadcast_to([128, 2, 512]), op=ALU.mult)), [b_cq[i2], b_r1[i2]], [b_cqn[i2]])
                s2 = cn["ss"] % 2; cn["ss"] += 1
                P.op("pe", (lambda i2=i2, s2=s2: nc.tensor.matmul(out=pss[s2][:], lhsT=C.ones_bf[:], rhs=sqc[i2][:, 2, :], start=True, stop=True)), [C.b_cst, b_sqc[i2]], [b_pss[s2]])
                P.op("act", (lambda i2=i2, s2=s2: nc.scalar.activation(out=lnt[i2][:], in_=pss[s2][:], func=AF.Ln, scale=1.0 / 128, bias=C.eps_t[:])), [b_pss[s2], C.b_cst], [b_lnt[i2]])
                P.op("act", (lambda i2=i2: nc.scalar.activation(out=r2[i2][:], in_=lnt[i2][:], func=AF.Exp, scale=-0.5)), [b_lnt[i2]], [b_r2[i2]])
                P.op("dve", (lambda i2=i2: nc.vector.tensor_tensor(out=ckvn[i2][:], in0=ckv[i2][:], in1=r2[i2][:], op=ALU.mult)), [b_ckv[i2], b_r2[i2]], [b_ckvn[i2]])
                blks = []
                for h in range(4):
                    def mmq(ri, h=h, i2=i2):
                        for ct in range(2):
                            P.op("pe", (lambda ct=ct: nc.tensor.matmul(out=praw[ri][0:96, :], lhsT=wuq_s[:, ct, h * 96:(h + 1) * 96], rhs=cqn[i2][:, ct, :], start=(ct == 0), stop=(ct == 1))),
                                 [b_w[9], b_cqn[i2]], [b_praw[ri]])
                    blks.append(nr_stages(mmq, gqs, b_w[5], qT[:, h, cols], b_qT[h][tb], cols, len(blks)))
                for h in range(4):
                    def mmk(ri, h=h, i2=i2):
                        P.op("pe", (lambda: nc.tensor.matmul(out=praw[ri][0:96, :], lhsT=wk_s[:, h, :], rhs=ckvn[i2][:], start=True, stop=False)), [b_w[10], b_ckvn[i2]], [b_praw[ri]])
                        P.op("pe", (lambda: nc.tensor.matmul(out=praw[ri][0:96, :], lhsT=emat[:], rhs=kpe[i2][:], start=False, stop=True)), [b_w[8], b_kpe[i2]], [b_praw[ri]])
                    blks.append(nr_stages(mmk, gk, b_w[6], kT[:, h, cols], b_kT[h][tb], cols, len(blks)))
                nb_ = len(blks)
                for idx in range(nb_ + 2):
                    if idx < nb_:
                        blks[idx][0]()
                    if 0 <= idx - 1 < nb_:
                        blks[idx - 1][1]()
                    if 0 <= idx - 2 < nb_:
                        blks[idx - 2][2]()
                for j in range(4):
                    P.op("pe", (lambda i2=i2, j=j: nc.tensor.matmul(out=pvv[:], lhsT=ckvn[i2][:, j * 128:(j + 1) * 128], rhs=wv_s[:], start=True, stop=True)), [b_ckvn[i2], b_w[11]], [b_pvv])
                    P.op("act", (lambda tb=tb, j=j: nc.scalar.copy(out=v1[:, tb * 4 + j, :, 0:64], in_=pvv[:].rearrange("p (h d) -> p h d", h=4))), [b_pvv], [b_v1])
            P.barrier()
        with contextlib.ExitStack() as st:
            C.stack = st
            psS = [C.ps(f"psS{i}", [128, 512], F32) for i in range(3)]; b_psS = P.bufs_n("psS", 3)
            po = [C.ps(f"po{i}", [128, 512], F32) for i in range(2)]; b_po = P.bufs_n("po", 2)
            pT = [C.sb(f"pT{i}", [128, 512], BF16) for i in range(3)]; b_pT = P.bufs_n("pT", 3)
            rc = [C.sb(f"rc{i}", [128, 4], F32) for i in range(2)]; b_rc = P.bufs_n("rc", 2)
            yd = [C.sb(f"yd{i}", [128, 4, 4, 64], BF16) for i in range(2)]; b_yd = P.bufs_n("yd", 2)
            cS = [0]
            for tt in range(NB):
                t0 = tt * 512
                yi = tt % 2
                for h in range(4):
                    oi = (tt * 4 + h) % 2
                    nch = 4 * tt + 4
                    slots = {}

                    def qk(c, h=h, t0=t0, tt=tt):
                        d = c - 4 * tt
                        off = max(d, 0) * 128
                        N = 512 - off
                        si = cS[0] % 3; cS[0] += 1
                        slots[c] = (si, d, off, N)
                        P.op("pe", (lambda: nc.tensor.matmul(out=psS[si][:, 0:N], lhsT=kT[:, h, c * 128:(c + 1) * 128], rhs=qT[:, h, t0 + off:t0 + 512], start=True, stop=True)), [], [b_psS[si]])

                    def ex(c):
                        si, d, off, N = slots[c]
                        P.op("act", (lambda: nc.scalar.activation(out=pT[si][:, 0:N], in_=psS[si][:, 0:N], func=AF.Exp)), [b_psS[si]], [b_pT[si]])
                        if d >= 0:
                            P.op("pool", (lambda: nc.gpsimd.tensor_tensor(out=pT[si][:, 0:128], in0=pT[si][:, 0:128], in1=tri[:], op=ALU.mult)), [b_pT[si]], [b_pT[si]])

                    def pv(c, h=h, tt=tt, oi=oi):
                        si, d, off, N = slots[c]
                        for j in range(max(d, 0), 4):
                            P.op("pe", (lambda j=j: nc.tensor.matmul(out=po[oi][:, j * 65:(j + 1) * 65], lhsT=pT[si][:, j * 128 - off:j * 128 - off + 128], rhs=v1[:, c, h, :], start=(c == 0 and j == 0), stop=(c == 4 * tt + j), skip_group_check=True)),
                                 [b_pT[si]], [b_po[oi]])

                    for idx in range(nch + 2):
                        if idx < nch:
                            qk(idx)
                        if 0 <= idx - 1 < nch:
                            ex(idx - 1)
                        if 0 <= idx - 2 < nch:
                            pv(idx - 2)
                    P.op("dve", (lambda oi=oi: nc.vector.reciprocal(out=rc[oi][:], in_=po[oi][:, 0:260].rearrange("p (j e) -> p j e", e=65)[:, :, 64])), [b_po[oi]], [b_rc[oi]])
                    P.op("dve", (lambda oi=oi, yi=yi, h=h: nc.vector.tensor_tensor(out=yd[yi][:, :, h, :], in0=po[oi][:, 0:260].rearrange("p (j e) -> p j e", e=65)[:, :, 0:64], in1=rc[oi][:, :, None].broadcast_to([128, 4, 64]), op=ALU.mult)),
                         [b_po[oi], b_rc[oi]], [b_yd[yi]])
                P.dma("sp", (lambda yi=yi, t0=t0: nc.sync.dma_start(out=ytm[t0:t0 + 512, 768:1024].rearrange("(j p) (h d) -> p j h d", p=128, h=4), in_=yd[yi][:])), [b_yd[yi]], [], b_yd[yi])
            P.barrier()
    C.stack = None


def phase_out(C, x_src, x_dst, w_out):
    nc, P = C.nc, C.P
    ytm, yT = C.dram["ytm"], C.dram["yT"]
    with contextlib.ExitStack() as st:
        C.stack = st
        wo = C.sb("wo", [128, 8, 1024], BF16); b_wo = P.bufs_n("wo", 8)
        w_v = w_out.rearrange("(c p) m -> p c m", p=128)
        for c in range(0, 8, 2):
            P.dma("pool", (lambda c=c: nc.gpsimd.dma_start(out=wo[:, c:c + 2, :], in_=w_v[:, c:c + 2, :])), [], b_wo[c:c + 2], b_wo[c])
        NS = 3
        yt = [C.sb(f"yt{i}", [128, 1024], BF16) for i in range(NS)]; b_yt = P.bufs_n("yt", NS)
        yTt = [C.sb(f"yTt{i}", [128, 8, 128], BF16) for i in range(NS)]
        b_yTa = P.bufs_n("yTa", NS); b_yTc = P.bufs_n("yTc", NS)
        xt = [C.sb(f"xt{i}", [128, 1024], F32) for i in range(NS)]; b_xt = P.bufs_n("xt", NS)
        ot = [C.sb(f"ot{i}", [128, 1024], F32) for i in range(2)]; b_ot = P.bufs_n("ot", 2)
        pst = [C.ps(f"pst{i}", [128, 8, 128], BF16) for i in range(2)]; b_pst = P.bufs_n("pst", 2)
        pd = [C.ps(f"pd{i}", [128, 512], F32) for i in range(4)]; b_pd = P.bufs_n("pd", 4)
        cd = 0
        NT = SEQ // 128

        def loads(tt):
            k = tt % NS
            rows = slice(tt * 128, (tt + 1) * 128)
            P.dma("sp", (lambda: nc.sync.dma_start(out=yt[k][:, 0:512], in_=ytm[rows, 0:512])), [], [b_yt[k]], b_yt[k])
            P.dma("sp", (lambda: nc.sync.dma_start(out=yt[k][:, 768:1024], in_=ytm[rows, 768:1024])), [], [b_yt[k]], b_yt[k])
            P.dma("sp", (lambda: nc.sync.dma_start(out=yTt[k][:, 4:6, :], in_=yT[512:768, rows].rearrange("(ct p) t -> p ct t", p=128))), [], [b_yTc[k]], b_yTc[k])
            P.dma("sp", (lambda: nc.sync.dma_start(out=xt[k][:], in_=x_src[rows, :])), [], [b_xt[k]], b_xt[k])

        loads(0)
        loads(1)
        for tt in range(NT):
            k = tt % NS
            k2 = tt % 2
            rows = slice(tt * 128, (tt + 1) * 128)
            if tt + 2 < NT:
                loads(tt + 2)
            for c in (0, 1, 2, 3, 6, 7):
                P.op("pe", (lambda k=k, k2=k2, c=c: nc.tensor.transpose(out=pst[k2][:, c, :], in_=yt[k][:, c * 128:(c + 1) * 128], identity=C.ident[:])), [b_yt[k], C.b_ident], [b_pst[k2]])
            P.op("act", (lambda k=k, k2=k2: nc.scalar.copy(out=yTt[k][:, 0:4, :], in_=pst[k2][:, 0:4, :])), [b_pst[k2]], [b_yTa[k]])
            P.op("dve", (lambda k=k, k2=k2: nc.vector.tensor_copy(out=yTt[k][:, 6:8, :], in_=pst[k2][:, 6:8, :])), [b_pst[k2]], [b_yTa[k]])
            for mh in range(2):
                di = cd % 4; cd += 1
                for c in range(8):
                    P.op("pe", (lambda k=k, c=c, mh=mh, di=di: nc.tensor.matmul(out=pd[di][:], lhsT=yTt[k][:, c, :], rhs=wo[:, c, mh * 512:(mh + 1) * 512], start=(c == 0), stop=(c == 7))),
                         [b_yTa[k], b_yTc[k], b_wo[c]], [b_pd[di]])
                P.op("dve", (lambda k=k, k2=k2, mh=mh, di=di: nc.vector.tensor_tensor(out=ot[k2][:, mh * 512:(mh + 1) * 512], in0=pd[di][:], in1=xt[k][:, mh * 512:(mh + 1) * 512], op=ALU.add)),
                     [b_pd[di], b_xt[k]], [b_ot[k2]])
            P.dma("sp", (lambda k2=k2, rows=rows: nc.sync.dma_start(out=x_dst[rows, :], in_=ot[k2][:])), [b_ot[k2]], [], b_ot[k2])
        P.barrier()
    C.stack = None


def phase_gla(C, prm):
    nc, P = C.nc, C.P
    zT, ztm, ytm, glo = C.dram["zT"], C.dram["ztm"], C.dram["ytm"], C.dram["gla_o"]
    D = C.dram
    NCH = SEQ // 64
    with contextlib.ExitStack() as st:
        C.stack = st
        qT = C.sb("gq", [128, SEQ], BF16); kT = C.sb("gk", [128, SEQ], BF16); bgT = C.sb("bgT", [16, SEQ], BF16)
        vv = C.sb("gv", [128, SEQ // 128, 256], BF16)
        b_in = P.bufs_n("gin", 4)
        P.dma("sp", lambda: nc.sync.dma_start(out=qT[:], in_=zT[FM["bq"]:FM["bq"] + 128, :]), [], [b_in[0]], b_in[0])
        P.dma("sp", lambda: nc.sync.dma_start(out=kT[:], in_=zT[FM["bk"]:FM["bk"] + 128, :]), [], [b_in[1]], b_in[1])
        P.dma("sp", lambda: nc.sync.dma_start(out=bgT[:], in_=zT[FM["bg"]:FM["bg"] + 16, :]), [], [b_in[2]], b_in[2])
        P.dma("sp", lambda: nc.sync.dma_start(out=vv[:], in_=ztm[:, TM["bv"]:TM["bv"] + 256].rearrange("(n p) c -> p n c", p=128)), [], [b_in[3]], b_in[3])
        wgu = C.sb("wgu", [16, 128], BF16); gb = C.sb("gb", [128, 1], F32); ngb = C.sb("ngb", [128, 1], F32)
        gout = C.sb("gout", [128, 64], F32)
        bmask = C.sb("bmask", [128, 4, 64], BF16); tri64 = C.sb("tri64", [128, 4, 64], BF16); rmask = C.sb("rmask", [128, 512], F32)
        one_c = C.sb("one_c", [128, 1], F32)
        b_p = P.bufs_n("gpar", 8)
        P.dma("pool", lambda: nc.gpsimd.dma_start(out=wgu[:], in_=prm["wgu"]), [], [b_p[0]], b_p[0])
        P.dma("sp", lambda: nc.sync.dma_start(out=gb[:], in_=prm["gbias"]), [], [b_p[1]], b_p[1])
        P.dma("sp", lambda: nc.sync.dma_start(out=gout[:], in_=prm["gout"].partition_broadcast(128)), [], [b_p[2]], b_p[2])
        P.dma("pool", lambda: nc.gpsimd.dma_start(out=bmask[:], in_=D["bmask"].rearrange("p (h d) -> p h d", h=4)), [], [b_p[3]], b_p[3])
        P.dma("pool", lambda: nc.gpsimd.dma_start(out=tri64[:], in_=D["tri64"].rearrange("p (h d) -> p h d", h=4)), [], [b_p[4]], b_p[4])
        P.dma("sp", lambda: nc.sync.dma_start(out=rmask[:], in_=D["rmask"][:, :]), [], [b_p[5]], b_p[5])
        P.op("dve", lambda: nc.vector.tensor_scalar(out=ngb[:], in0=gb[:], scalar1=-1.0, scalar2=None, op0=ALU.mult), [b_p[1]], [b_p[6]])
        P.op("pool", lambda: nc.gpsimd.memset(one_c[:], 1.0), [], [b_p[7]])
        la = C.sb("la", [128, SEQ], F32); cs = C.sb("cs", [128, SEQ], F32); tmpf = C.sb("tmpf", [128, SEQ], F32)
        b_la = P.bufs_n("la", 8); b_cs = P.bufs_n("cs", 8)
        b_tmp = P.buf("tmpf")
        pgt = [C.ps(f"pgt{i}", [128, 512], F32) for i in range(1)]; b_pgt = P.bufs_n("pgt", 1)
        for tb in range(8):
            cols = slice(tb * 512, (tb + 1) * 512)
            k = 0
            P.op("pe", (lambda k=k, cols=cols: nc.tensor.matmul(out=pgt[k][:], lhsT=wgu[:], rhs=bgT[:, cols], start=True, stop=True)), [b_p[0], b_in[2]], [b_pgt[k]])
            P.op("act", (lambda k=k, cols=cols: nc.scalar.activation(out=la[:, cols], in_=pgt[k][:], func=AF.Exp, scale=-1.0, bias=ngb[:])), [b_pgt[k], b_p[6]], [b_la[tb]])
            P.op("act", (lambda cols=cols: nc.scalar.activation(out=la[:, cols], in_=la[:, cols], func=AF.Ln, bias=one_c[:])), [b_la[tb], b_p[7]], [b_la[tb]])
            P.op("dve", (lambda cols=cols: nc.vector.tensor_tensor_scan(out=cs[:, cols], data0=rmask[:], data1=la[:, cols], initial=0.0, op0=ALU.mult, op1=ALU.add)), [b_la[tb], b_p[5]], [b_cs[tb]])
        qe = C.sb("qe", [128, SEQ], BF16); ke = C.sb("ke", [128, SEQ], BF16); kdT = C.sb("kdT", [128, SEQ], BF16)
        ebl = C.sb("ebl", [128, NCH], F32)
        b_qe = P.buf("qe"); b_ke = P.buf("ke"); b_kd = P.buf("kdT"); b_ebl = P.buf("ebl")
        csl = cs[:].rearrange("p (c i) -> p c i", i=64)[:, :, 63]
        P.op("act", lambda: nc.scalar.activation(out=tmpf[:], in_=cs[:], func=AF.Exp, scale=-1.0 / 16), b_cs, [b_tmp])
        P.op("dve", lambda: nc.vector.scalar_tensor_tensor(out=qe[:], in0=qT[:], scalar=float(32 ** -0.5), in1=tmpf[:], op0=ALU.mult, op1=ALU.mult), [b_in[0], b_tmp], [b_qe])
        P.op("act", lambda: nc.scalar.activation(out=tmpf[:], in_=cs[:], func=AF.Exp, scale=1.0 / 16), b_cs, [b_tmp])
        P.op("dve", lambda: nc.vector.tensor_tensor(out=ke[:], in0=kT[:], in1=tmpf[:], op=ALU.mult), [b_in[1], b_tmp], [b_ke])
        P.op("act", lambda: nc.scalar.activation(out=ebl[:], in_=csl, func=AF.Exp, scale=-1.0 / 16), b_cs, [b_ebl])
        P.op("dve", lambda: nc.vector.tensor_tensor(out=tmpf[:].rearrange("p (c i) -> p c i", i=64), in0=cs[:].rearrange("p (c i) -> p c i", i=64), in1=csl.unsqueeze(2).broadcast_to([128, NCH, 64]), op=ALU.subtract), b_cs, [b_tmp])
        P.op("act", lambda: nc.scalar.activation(out=tmpf[:], in_=tmpf[:], func=AF.Exp, scale=1.0 / 16), [b_tmp], [b_tmp])
        P.op("dve", lambda: nc.vector.tensor_tensor(out=kdT[:], in0=kT[:], in1=tmpf[:], op=ALU.mult), [b_in[1], b_tmp], [b_kd])
        S = C.sb("S", [128, 4, 64], F32); Sbf = C.sb("Sbf", [128, 256], BF16)
        b_S = P.buf("S"); b_Sbf = P.buf("Sbf")
        P.op("dve", lambda: nc.vector.memset(S[:], 0.0), [], [b_S])
        P.op("pool", lambda: nc.gpsimd.memset(Sbf[:], 0.0), [], [b_Sbf])
        Qbd = [C.sb(f"Qbd{i}", [128, 4, 64], BF16) for i in range(2)]; b_Qbd = P.bufs_n("Qbd", 2)
        Am = [C.sb(f"Am{i}", [128, 256], BF16) for i in range(2)]; b_Am = P.bufs_n("Am", 2)
        kdm = [C.sb(f"kdm{i}", [128, 128], BF16) for i in range(2)]; b_kdm = P.bufs_n("kdm", 2)
        tS = [C.sb(f"tS{i}", [128, 4, 64], F32) for i in range(2)]; b_tS = P.bufs_n("tS", 2)
        osb = [C.sb(f"osb{i}", [64, 256], F32) for i in range(2)]; b_osb = P.bufs_n("osb", 2)
        pA = [C.ps(f"pA{i}", [128, 256], F32) for i in range(2)]; b_pA = P.bufs_n("pA", 2)
        pO = [C.ps(f"pO{i}", [64, 256], F32) for i in range(2)]; b_pO = P.bufs_n("pO", 2)
        pK = C.ps("pK", [128, 128], BF16); b_pK = P.buf("pK")
        pS = [C.ps(f"pS{i}", [128, 256], F32) for i in range(2)]; b_pS = P.bufs_n("pS", 2)
        def pre(c):
            k = c % 2
            pr = c // 2
            base = k * 64
            cols = slice(c * 64, (c + 1) * 64)
            pcols = slice(pr * 128, (pr + 1) * 128)
            kk = pr % 2
            if k == 0:
                P.op("pe", (lambda: nc.tensor.transpose(out=pK[:], in_=kdT[:, pcols], identity=C.ident[:])), [b_kd, C.b_ident], [b_pK])
                P.op("act", (lambda: nc.scalar.copy(out=kdm[kk][:], in_=pK[:])), [b_pK], [b_kdm[kk]])
            P.op("pool", (lambda: nc.gpsimd.tensor_tensor(out=Qbd[k][:], in0=qe[:, cols].unsqueeze(1).broadcast_to([128, 4, 64]), in1=bmask[:], op=ALU.mult)), [b_qe, b_p[3]], [b_Qbd[k]])
            P.op("pe", (lambda: nc.tensor.matmul(out=pA[k][:], lhsT=ke[:, pcols], rhs=Qbd[k][:].rearrange("p h d -> p (h d)"), start=True, stop=True)), [b_ke, b_Qbd[k]], [b_pA[k]])
            P.op("dve", (lambda: nc.vector.tensor_tensor(out=Am[k][base:base + 64, :], in0=pA[k][base:base + 64, :], in1=tri64[base:base + 64, :, :].rearrange("p h d -> p (h d)"), op=ALU.mult)), [b_pA[k], b_p[4]], [b_Am[k]])
            P.op("pe", (lambda: nc.tensor.matmul(out=pS[k][:], lhsT=kdm[kk][base:base + 64, :], rhs=vv[base:base + 64, pr, :], start=True, stop=True)), [b_kdm[kk], b_in[3]], [b_pS[k]])
            P.op("dve", (lambda: nc.vector.tensor_tensor(out=tS[k][:], in0=pS[k][:].rearrange("p (h d) -> p h d", h=4), in1=bmask[:], op=ALU.mult)), [b_pS[k], b_p[3]], [b_tS[k]])

        def post(c):
            k = c % 2
            pr = c // 2
            base = k * 64
            cols = slice(c * 64, (c + 1) * 64)
            P.op("pe", (lambda: nc.tensor.matmul(out=pO[k][:], lhsT=qe[:, cols], rhs=Sbf[:], start=True, stop=False, skip_group_check=True)), [b_qe, b_Sbf], [b_pO[k]])
            for h in range(4):
                P.op("pe", (lambda h=h: nc.tensor.matmul(out=pO[k][:, h * 64:(h + 1) * 64], lhsT=Am[k][base:base + 64, h * 64:(h + 1) * 64], rhs=vv[base:base + 64, pr, h * 64:(h + 1) * 64], start=False, stop=(h == 3), skip_group_check=True)),
                     [b_Am[k], b_in[3]], [b_pO[k]])
            P.op("dve", (lambda: nc.vector.scalar_tensor_tensor(out=S[:], in0=S[:], scalar=ebl[:, c:c + 1], in1=tS[k][:], op0=ALU.mult, op1=ALU.add)), [b_S, b_ebl, b_tS[k]], [b_S])
            P.op("act", (lambda: nc.scalar.copy(out=Sbf[:], in_=S[:].rearrange("p h d -> p (h d)"))), [b_S], [b_Sbf])
            P.op("act", (lambda: nc.scalar.copy(out=osb[k][:], in_=pO[k][:])), [b_pO[k]], [b_osb[k]])
            P.dma("sp", (lambda: nc.sync.dma_start(out=glo[c * 64:(c + 1) * 64, :], in_=osb[k][:])), [b_osb[k]], [], b_osb[k])

        pre(0)
        for c in range(NCH):
            if c + 1 < NCH:
                pre(c + 1)
            post(c)
        P.barrier()
    with contextlib.ExitStack() as st:
        C.stack = st
        gout = C.sb("gout", [128, 64], F32); b_g = P.buf("gout")
        P.dma("sp", lambda: nc.sync.dma_start(out=gout[:], in_=prm["gout"].partition_broadcast(128)), [], [b_g], b_g)
        ot = [C.sb(f"got{i}", [128, 4, 64], F32) for i in range(3)]; b_ot = P.bufs_n("got", 3)
        junk = C.sb("gjunk", [128, 64], BF16); b_junk = P.buf("gjunk")
        ssa = C.sb("ssa", [128, SEQ // 128, 4], F32); b_ssa = P.buf("ssa")
        rsa = C.sb("rsa", [128, SEQ // 128, 4], F32); b_rsa = P.buf("rsa")
        NT = SEQ // 128
        for tt in range(NT):
            k = tt % 3
            P.dma("sp", (lambda k=k, tt=tt: nc.sync.dma_start(out=ot[k][:], in_=glo[tt * 128:(tt + 1) * 128, :].rearrange("p (h d) -> p h d", h=4))), [], [b_ot[k]], b_ot[k])
            for h in range(4):
                P.op("act", (lambda k=k, tt=tt, h=h: nc.scalar.activation(out=junk[:], in_=ot[k][:, h, :], func=AF.Square, accum_out=ssa[:, tt, h:h + 1])), [b_ot[k]], [b_junk, b_ssa])
        P.op("act", lambda: nc.scalar.activation(out=rsa[:], in_=ssa[:], func=AF.Ln, scale=1.0 / 64, bias=C.eps_t[:]), [b_ssa, C.b_cst], [b_rsa])
        P.op("act", lambda: nc.scalar.activation(out=rsa[:], in_=rsa[:], func=AF.Exp, scale=-0.5), [b_rsa], [b_rsa])
        rt = [C.sb(f"grt{i}", [128, 256], BF16) for i in range(2)]; b_rt = P.bufs_n("grt", 2)
        sr = [C.sb(f"gsr{i}", [128, 4, 64], F32) for i in range(2)]; b_sr = P.bufs_n("gsr", 2)
        on = [C.sb(f"gon{i}", [128, 4, 64], F32) for i in range(2)]; b_on = P.bufs_n("gon", 2)
        yb = [C.sb(f"gyb{i}", [128, 4, 64], BF16) for i in range(2)]; b_yb = P.bufs_n("gyb", 2)
        for tt in range(NT):
            k = tt % 3
            k2 = tt % 2
            rows = slice(tt * 128, (tt + 1) * 128)
            P.dma("sp", (lambda k=k, rows=rows: nc.sync.dma_start(out=ot[k][:], in_=glo[rows, :].rearrange("p (h d) -> p h d", h=4))), [], [b_ot[k]], b_ot[k])
            P.dma("sp", (lambda k2=k2, rows=rows: nc.sync.dma_start(out=rt[k2][:], in_=ztm[rows, TM["br"]:TM["br"] + 256])), [], [b_rt[k2]], b_rt[k2])
            P.op("act", (lambda k2=k2: nc.scalar.activation(out=sr[k2][:].rearrange("p h d -> p (h d)"), in_=rt[k2][:], func=AF.Silu)), [b_rt[k2]], [b_sr[k2]])
            P.op("dve", (lambda k=k, k2=k2, tt=tt: nc.vector.tensor_tensor(out=on[k2][:], in0=ot[k][:], in1=rsa[:, tt, :].unsqueeze(2).broadcast_to([128, 4, 64]), op=ALU.mult)), [b_ot[k], b_rsa], [b_on[k2]])
            P.op("dve", (lambda k2=k2: nc.vector.tensor_tensor(out=on[k2][:], in0=on[k2][:], in1=gout[:].unsqueeze(1).broadcast_to([128, 4, 64]), op=ALU.mult)), [b_on[k2], b_g], [b_on[k2]])
            P.op("dve", (lambda k2=k2: nc.vector.tensor_tensor(out=yb[k2][:], in0=on[k2][:], in1=sr[k2][:], op=ALU.mult)), [b_on[k2], b_sr[k2]], [b_yb[k2]])
            P.dma("sp", (lambda k2=k2, rows=rows: nc.sync.dma_start(out=ytm[rows, 256:512], in_=yb[k2][:].rearrange("p h d -> p (h d)"))), [b_yb[k2]], [], b_yb[k2])
        P.barrier()
    C.stack = None


NBIS = 16
TOPK = 256
NEG_MASK = -30000.0


def phase_dsa(C, prm, eng_bis="dve"):
    nc, P = C.nc, C.P
    zT, ztm, ytm = C.dram["zT"], C.dram["ztm"], C.dram["ytm"]
    D = C.dram
    NB = SEQ // 128
    with contextlib.ExitStack() as outer:
        C.stack = outer
        qT2 = C.sb("aqT", [64, 4, SEQ], BF16); kT2 = C.sb("akT", [64, 4, SEQ], BF16)
        v1 = C.sb("av1", [128, NB, 4, 65], BF16)
        ikT = C.sb("ikT", [32, SEQ], BF16)
        with contextlib.ExitStack() as st:
            C.stack = st
            raw = C.sb("araw", [64, 4, SEQ], BF16); b_raw = P.bufs_n("araw", 2)
            vst = C.sb("avst", [128, NB, 256], BF16); b_vst = P.buf("avst")
            g2 = C.sb("ag2", [128, 2], F32); g2s = C.sb("ag2s", [128, 1], F32); b_g2 = P.bufs_n("ag2", 2)
            bd64 = C.sb("bd64", [128, 128], BF16); b_bd = P.buf("bd64")
            b_v1 = P.buf("av1"); b_ik = P.buf("ikT")
            P.dma("sp", lambda: nc.sync.dma_start(out=g2[:], in_=prm["g2"]), [], [b_g2[0]], b_g2[0])
            P.dma("pool", lambda: nc.gpsimd.dma_start(out=bd64[:], in_=D["bd64"][:, :]), [], [b_bd], b_bd)
            P.dma("sp", lambda: nc.sync.dma_start(out=ikT[:], in_=zT[FM["ik"]:FM["ik"] + 32, :]), [], [b_ik], b_ik)
            P.dma("sp", lambda: nc.sync.dma_start(out=vst[:], in_=ztm[:, TM["av"]:TM["av"] + 256].rearrange("(n p) c -> p n c", p=128)), [], [b_vst], b_vst)
            P.op("pool", lambda: nc.gpsimd.memset(v1[:], 1.0), [], [b_v1])
            P.op("dve", lambda: nc.vector.tensor_scalar(out=g2s[:], in0=g2[:, 0:1], scalar1=0.125, scalar2=None, op0=ALU.mult), [b_g2[0]], [b_g2[1]])
            for n in range(0, NB, 8):
                P.op("act", (lambda n=n: nc.scalar.copy(out=v1[:, n:n + 8, :, 0:64], in_=vst[:, n:n + 8, :].rearrange("p n (h d) -> p n h d", h=4))), [b_vst, b_v1], [b_v1])
            b_qk = P.buf("aqk")
            NSL = 3
            sq = [C.sb(f"asq{i}", [128, 512], BF16) for i in range(NSL)]; b_sq = P.bufs_n("asq", NSL)
            lnt = [C.sb(f"alnt{i}", [128, 512], F32) for i in range(NSL)]; b_lnt = P.bufs_n("alnt", NSL)
            rr = [C.sb(f"arr{i}", [128, 512], F32) for i in range(NSL)]; b_rr = P.bufs_n("arr", NSL)
            pss = [C.ps(f"apss{i}", [128, 512], F32) for i in range(NSL)]; b_pss = P.bufs_n("apss", NSL)
            for which, (dst, row0, gv, bg) in enumerate(((qT2, FM["aq"], g2s, b_g2[1]), (kT2, FM["ak"], g2, b_g2[0]))):
                P.dma("sp", (lambda row0=row0: nc.sync.dma_start(out=raw[:], in_=zT[row0:row0 + 256, :].rearrange("(hp p) t -> p hp t", p=64))), [], b_raw, b_raw[0])
                gcol = gv[0:64, 0:1] if which == 0 else gv[0:64, 1:2]
                blks = [(hp, tb) for hp in range(4) for tb in range(SEQ // 512)]

                def stg(n, stage, dst=dst, gcol=gcol, bg=bg):
                    hp, tb = blks[n]
                    k = n % NSL
                    cols = slice(tb * 512, (tb + 1) * 512)
                    if stage == 0:
                        P.op("act", (lambda: nc.scalar.activation(out=sq[k][0:64, :], in_=raw[:, hp, cols], func=AF.Square)), b_raw, [b_sq[k]])
                    elif stage == 1:
                        P.op("pe", (lambda: nc.tensor.matmul(out=pss[k][0:64, :], lhsT=C.ones_bf[0:64, 0:64], rhs=sq[k][0:64, :], start=True, stop=True)), [C.b_cst, b_sq[k]], [b_pss[k]])
                        P.op("act", (lambda: nc.scalar.activation(out=lnt[k][0:64, :], in_=pss[k][0:64, :], func=AF.Ln, scale=1.0 / 64, bias=C.eps_t[0:64, :])), [b_pss[k], C.b_cst], [b_lnt[k]])
                    elif stage == 2:
                        P.op("act", (lambda: nc.scalar.activation(out=rr[k][0:64, :], in_=lnt[k][0:64, :], func=AF.Exp, scale=-0.5)), [b_lnt[k]], [b_rr[k]])
                    else:
                        P.op("dve", (lambda: nc.vector.scalar_tensor_tensor(out=dst[:, hp, cols], in0=raw[:, hp, cols], scalar=gcol, in1=rr[k][0:64, :], op0=ALU.mult, op1=ALU.mult)),
                             b_raw + [bg, b_rr[k]], [b_qk])
                nb_ = len(blks)
                for idx in range(nb_ + 3):
                    for stage in range(4):
                        n = idx - stage
                        if 0 <= n < nb_:
                            stg(n, stage)
            P.barrier()
        with contextlib.ExitStack() as st:
            C.stack = st
            Bn = C.sb("Bn", [128, 4, 256], BF16); I4 = C.sb("I4", [128, 4, 128], BF16); cneg = C.sb("cneg", [128, 128], F32)
            cfrow = C.sb("cfrow", [1, 512], BF16); onesrow = C.sb("onesrow", [1, 128], BF16)
            p2 = C.sb("p2", [128, NBIS + 1], F32); halfc = C.sb("halfc", [128, 1], F32)
            b_c = P.bufs_n("dcst", 8)
            P.dma("pool", lambda: nc.gpsimd.dma_start(out=Bn[:], in_=D["a_bn"].rearrange("p (h s) -> p h s", h=4)), [], [b_c[0]], b_c[0])
            P.dma("pool", lambda: nc.gpsimd.dma_start(out=I4[:], in_=D["i4"].rearrange("p (h s) -> p h s", h=4)), [], [b_c[1]], b_c[1])
            P.dma("sp", lambda: nc.sync.dma_start(out=cneg[:], in_=D["cneg"][:, :]), [], [b_c[2]], b_c[2])
            P.dma("pool", lambda: nc.gpsimd.dma_start(out=cfrow[:], in_=D["a_cf"][:, :]), [], [b_c[3]], b_c[3])
            P.dma("sp", lambda: nc.sync.dma_start(out=p2[:], in_=D["pow2"][:, :]), [], [b_c[4]], b_c[4])
            P.op("pool", lambda: nc.gpsimd.memset(onesrow[:], 1.0), [], [b_c[5]])
            score = [C.sb(f"score{i}", [128, SEQ], F32) for i in range(2)]; b_score = P.bufs_n("score", 2)
            maskb = [C.sb(f"maskb{i}", [128, SEQ], BF16) for i in range(2)]; b_maskb = P.bufs_n("maskb", 2)
            junk = C.sb("bjunk", [128, SEQ], BF16); b_junk = P.buf("bjunk")
            junk2 = C.sb("bjunk2", [128, SEQ], BF16); b_junk2 = P.buf("bjunk2")
            b_cnt = P.bufs_n("bcnt", 2); b_sgd = P.bufs_n("bsgd", 2)
            b_scp = [P.bufs_n(f"scp{kk}_", SEQ // 512) for kk in range(2)]
            iqb = [C.sb(f"iqb{i}", [32, 8, 128], BF16) for i in range(2)]; b_iqb = P.bufs_n("iqb", 2)
            iwb = [C.sb(f"iwb{i}", [128, 8], BF16) for i in range(2)]; b_iwb = P.bufs_n("iwb", 2)
            iwf = [C.sb(f"iwf{i}", [128, 8], F32) for i in range(2)]; b_iwf = P.bufs_n("iwf", 2)
            qb = [C.sb(f"qb{i}", [128, 2, 128], BF16) for i in range(2)]; b_qb = P.bufs_n("qb", 2)
            rsb = [C.sb(f"rsb{i}", [128, 512], F32) for i in range(3)]; b_rsb = P.bufs_n("rsb", 3)
            pT = [C.sb(f"apT{i}", [128, 512], BF16) for i in range(3)]; b_pT = P.bufs_n("apT", 3)
            st_ = [dict(amax=C.sb(f"amax{i}", [128, 1], F32), dt=C.sb(f"dt{i}", [128, NBIS + 1], F32), d2=C.sb(f"d2{i}", [128, NBIS + 1], F32),
                        mid=C.sb(f"mid{i}", [128, 1], F32), cnt=C.sb(f"cnt{i}", [128, 1], F32), sgd=C.sb(f"sgd{i}", [128, 1], F32),
                        thr=C.sb(f"thr{i}", [128, 1], F32)) for i in range(2)]
            b_st = [P.buf(f"bst{i}") for i in range(2)]
            rc = [C.sb(f"arc{i}", [128, 4], F32) for i in range(2)]; b_rc = P.bufs_n("arc", 2)
            ya = [C.sb(f"aya{i}", [128, 4, 64], BF16) for i in range(2)]; b_ya = P.bufs_n("aya", 2)
            psc = [C.ps(f"psc{i}", [128, 512], F32) for i in range(3)]; b_psc = P.bufs_n("psc", 3)
            psA = [C.ps(f"psA{i}", [128, 512], F32) for i in range(3)]; b_psA = P.bufs_n("psA", 3)
            po = [C.ps(f"apo{i}", [128, 512], F32) for i in range(2)]; b_po = P.bufs_n("apo", 2)
            ce = dict(sc=0, A=0)
            sgn = [C.sb(f"sgn{i}", [128, 1], F32) for i in range(2)]
            nthr = C.sb("nthr", [128, NB], F32)
            P.dma("sp", lambda: nc.sync.dma_start(out=nthr[:], in_=D["a_nthr"][:, :]), [], [b_c[6]], b_c[6])

            def emit_S2(blocks):
                units = []
                for i in blocks:
                    k = i % 2
                    rows = slice(i * 128, (i + 1) * 128)
                    N = 128 * (i + 1)
                    P.dma("sp", (lambda k=k, rows=rows: nc.sync.dma_start(out=iqb[k][:], in_=zT[FM["iq"]:FM["iq"] + 256, rows].rearrange("(h d) t -> d h t", d=32))), [], [b_iqb[k]], b_iqb[k])
                    P.dma("sp", (lambda k=k, rows=rows: nc.sync.dma_start(out=iwb[k][:], in_=ztm[rows, TM["iw"]:TM["iw"] + 8])), [], [b_iwb[k]], b_iwb[k])
                    P.op("act", (lambda k=k: nc.scalar.copy(out=iwf[k][:], in_=iwb[k][:])), [b_iwb[k]], [b_iwf[k]])
                    for pi_, p0 in enumerate(range(0, N, 512)):
                        units.append((k, pi_, p0, min(512, N - p0)))
                for hh in range(8):
                    for (k, pi_, p0, w) in units:
                        si = ce["sc"] % 3; ce["sc"] += 1
                        bs = b_scp[k][pi_]
                        P.op("pe", (lambda k=k, hh=hh, si=si, p0=p0, w=w: nc.tensor.matmul(out=psc[si][:, 0:w], lhsT=iqb[k][:, hh, :], rhs=ikT[:, p0:p0 + w], start=True, stop=True)), [b_iqb[k]], [b_psc[si]])
                        P.op("act", (lambda si=si, w=w: nc.scalar.activation(out=rsb[si][:, 0:w], in_=psc[si][:, 0:w], func=AF.Relu)), [b_psc[si]], [b_rsb[si]])
                        if hh == 0:
                            P.op("dve", (lambda k=k, si=si, p0=p0, w=w: nc.vector.tensor_scalar(out=score[k][:, p0:p0 + w], in0=rsb[si][:, 0:w], scalar1=iwf[k][:, 0:1], scalar2=None, op0=ALU.mult)), [b_rsb[si], b_iwf[k]], [bs, b_score[k]])
                        else:
                            P.op("dve", (lambda k=k, si=si, p0=p0, w=w, hh=hh: nc.vector.scalar_tensor_tensor(out=score[k][:, p0:p0 + w], in0=rsb[si][:, 0:w], scalar=iwf[k][:, hh:hh + 1], in1=score[k][:, p0:p0 + w], op0=ALU.mult, op1=ALU.add)), [b_rsb[si], b_iwf[k], bs], [bs])
                for i in blocks:
                    k = i % 2
                    S_ = st_[k]
                    N = 128 * (i + 1)
                    npc = (N + 511) // 512
                    if i >= 2:
                        P.op("dve", (lambda k=k, N=N, S_=S_: nc.vector.tensor_reduce(out=S_["amax"][:], in_=score[k][:, 0:N], axis=AX.X, op=ALU.max, apply_absolute_value=True)), b_scp[k][0:npc], [b_st[k]])
                    P.op("dve", (lambda k=k, i=i: nc.vector.tensor_tensor(out=score[k][:, i * 128:(i + 1) * 128], in0=score[k][:, i * 128:(i + 1) * 128], in1=cneg[:], op=ALU.add)), b_scp[k][0:npc] + [b_c[2]], [b_score[k]])

            def steps_B(i0):
                steps = []
                blocks = (i0, i0 + 1)
                if i0 >= 2:
                    def init():
                        for i in blocks:
                            k = i % 2; S_ = st_[k]
                            P.op("dve", (lambda S_=S_: nc.vector.tensor_scalar(out=S_["amax"][:], in0=S_["amax"][:], scalar1=1.001, scalar2=1e-30, op0=ALU.mult, op1=ALU.add)), [b_st[k]], [b_st[k]])
                            P.op("dve", (lambda S_=S_: nc.vector.tensor_scalar(out=S_["dt"][:], in0=p2[:], scalar1=S_["amax"][:, 0:1], scalar2=None, op0=ALU.mult)), [b_st[k], b_c[4]], [b_st[k]])
                            P.op("dve", (lambda S_=S_: nc.vector.tensor_scalar(out=S_["d2"][:], in0=S_["dt"][:], scalar1=2.0, scalar2=None, op0=ALU.mult)), [b_st[k]], [b_st[k]])
                            P.op("dve", (lambda S_=S_: nc.vector.memset(S_["mid"][:], 0.0)), [], [b_st[k]])
                    steps.append(init)
                    for it in range(NBIS):
                        def one(it=it):
                            i = blocks[0]; k = i % 2; S_ = st_[k]; N = 128 * (i + 1)
                            P.op("dve", (lambda k=k, N=N, S_=S_: nc.vector.tensor_scalar(out=junk[:, 0:N], in0=score[k][:, 0:N], scalar1=S_["mid"][:, 0:1], scalar2=0.0, op0=ALU.is_ge, op1=ALU.add, accum_out=S_["cnt"][:])),
                                 [b_score[k], b_st[k]], [b_junk, b_cnt[k]])
                            i = blocks[1]; k1 = i % 2; S1 = st_[k1]; N1 = 128 * (i + 1)
                            P.op("act", (lambda k1=k1, N1=N1, S1=S1: nc.scalar.activation(out=junk2[:, 0:N1], in_=score[k1][:, 0:N1], func=AF.Sign, bias=S1["mid"][:, 0:1], accum_out=S1["cnt"][:])),
                                 [b_score[k1], b_st[k1]], [b_junk2, b_cnt[k1]])
                            P.op("dve", (lambda S1=S1, i=i: nc.vector.scalar_tensor_tensor(out=S1["sgd"][:], in0=S1["cnt"][:], scalar=nthr[:, i:i + 1], in1=S1["d2"][:, it + 1:it + 2], op0=ALU.is_ge, op1=ALU.mult)), [b_cnt[k1], b_st[k1], b_c[6]], [b_sgd[k1]])
                            P.op("dve", (lambda S_=S_: nc.vector.scalar_tensor_tensor(out=S_["sgd"][:], in0=S_["cnt"][:], scalar=TOPK - 0.5, in1=S_["d2"][:, it + 1:it + 2], op0=ALU.is_ge, op1=ALU.mult)), [b_cnt[k], b_st[k]], [b_sgd[k]])
                            P.op("dve", (lambda S1=S1: nc.vector.scalar_tensor_tensor(out=S1["mid"][:], in0=S1["dt"][:, it + 1:it + 2], scalar=S1["sgd"][:, 0:1], in1=S1["mid"][:], op0=ALU.subtract, op1=ALU.add)), [b_sgd[k1], b_st[k1]], [b_st[k1]])
                            P.op("dve", (lambda S_=S_: nc.vector.scalar_tensor_tensor(out=S_["mid"][:], in0=S_["sgd"][:], scalar=S_["dt"][:, it + 1:it + 2], in1=S_["mid"][:], op0=ALU.subtract, op1=ALU.add)), [b_sgd[k], b_st[k]], [b_st[k]])
                        steps.append(one)

                    def fin():
                        i = blocks[0]; k = i % 2; S_ = st_[k]
                        P.op("dve", (lambda S_=S_: nc.vector.tensor_tensor(out=S_["thr"][:], in0=S_["mid"][:], in1=S_["dt"][:, NBIS:NBIS + 1], op=ALU.subtract)), [b_st[k]], [b_st[k]])
                        i = blocks[1]; k = i % 2; S_ = st_[k]
                        P.op("dve", (lambda S_=S_: nc.vector.scalar_tensor_tensor(out=S_["thr"][:], in0=S_["mid"][:], scalar=-1.0, in1=S_["dt"][:, NBIS:NBIS + 1], op0=ALU.mult, op1=ALU.subtract)), [b_st[k]], [b_st[k]])
                    steps.append(fin)
                else:
                    def init0():
                        for i in blocks:
                            k = i % 2; S_ = st_[k]
                            P.op("dve", (lambda S_=S_: nc.vector.memset(S_["thr"][:], -1e29)), [], [b_st[k]])
                    steps.append(init0)

                def mk():
                    for i in blocks:
                        k = i % 2; S_ = st_[k]; N = 128 * (i + 1)
                        P.op("dve", (lambda k=k, N=N, S_=S_: nc.vector.tensor_scalar(out=maskb[k][:, 0:N], in0=score[k][:, 0:N], scalar1=S_["thr"][:, 0:1], scalar2=NEG_MASK, op0=ALU.is_lt, op1=ALU.mult)), [b_score[k], b_st[k]], [b_maskb[k]])
                steps.append(mk)
                return steps

            def steps_A(i):
                k = i % 2
                oi = i % 2
                rows = slice(i * 128, (i + 1) * 128)
                slots = {}

                def qk(c):
                    ai = ce["A"] % 3; ce["A"] += 1
                    slots[c] = ai
                    ccols = slice(c * 128, (c + 1) * 128)
                    for h in range(4):
                        P.op("pe", (lambda h=h: nc.tensor.matmul(out=psA[ai][:, h * 128:(h + 1) * 128], lhsT=kT2[:, h, ccols], rhs=qT2[:, h, rows], start=(h == 0), stop=False, skip_group_check=True)), [], [b_psA[ai]])
                    P.op("pe", (lambda: nc.tensor.matmul(out=psA[ai][:], lhsT=maskb[k][:, ccols], rhs=I4[:].rearrange("p h s -> p (h s)"), start=False, stop=False, skip_group_check=True)), [b_maskb[k], b_c[1]], [b_psA[ai]])
                    if c >= i - 1:
                        o0 = 128 if c == i else 0
                        for h in range(4):
                            P.op("pe", (lambda h=h: nc.tensor.matmul(out=psA[ai][:, h * 128:(h + 1) * 128], lhsT=Bn[:, h, o0:o0 + 128], rhs=C.ident[:], start=False, stop=(h == 3), skip_group_check=True)), [b_c[0], C.b_ident], [b_psA[ai]])
                    else:
                        P.op("pe", (lambda: nc.tensor.matmul(out=psA[ai][:], lhsT=onesrow[:], rhs=cfrow[:], start=False, stop=True, skip_group_check=True)), [b_c[3], b_c[5]], [b_psA[ai]])

                def ex(c):
                    ai = slots[c]
                    P.op("act", (lambda: nc.scalar.activation(out=pT[ai][:], in_=psA[ai][:], func=AF.Exp)), [b_psA[ai]], [b_pT[ai]])

                def pv(c):
                    ai = slots[c]
                    for h in range(4):
                        P.op("pe", (lambda h=h: nc.tensor.matmul(out=po[oi][:, h * 65:(h + 1) * 65], lhsT=pT[ai][:, h * 128:(h + 1) * 128], rhs=v1[:, c, h, :], start=(c == 0 and h == 0), stop=(c == i), skip_group_check=True)), [b_pT[ai]], [b_po[oi]])

                def fin():
                    P.op("dve", (lambda: nc.vector.reciprocal(out=rc[oi][:], in_=po[oi][:, 0:260].rearrange("p (j e) -> p j e", e=65)[:, :, 64])), [b_po[oi]], [b_rc[oi]])
                    P.op("dve", (lambda: nc.vector.tensor_tensor(out=ya[oi][:], in0=po[oi][:, 0:260].rearrange("p (j e) -> p j e", e=65)[:, :, 0:64], in1=rc[oi][:, :].unsqueeze(2).broadcast_to([128, 4, 64]), op=ALU.mult)), [b_po[oi], b_rc[oi]], [b_ya[oi]])
                    P.dma("sp", (lambda: nc.sync.dma_start(out=ytm[rows, 0:256], in_=ya[oi][:].rearrange("p h d -> p (h d)"))), [b_ya[oi]], [], b_ya[oi])

                n = i + 1
                steps = []
                for idx in range(n + 2):
                    def st(idx=idx):
                        if idx < n:
                            qk(idx)
                        if 0 <= idx - 1 < n:
                            ex(idx - 1)
                        if 0 <= idx - 2 < n:
                            pv(idx - 2)
                        if idx == n + 1:
                            fin()
                    steps.append(st)
                return steps

            def merge(a, b):
                na, nb = len(a), len(b)
                ia = ib = 0
                while ia < na or ib < nb:
                    if ib >= nb or (ia < na and ia * nb <= ib * na):
                        a[ia](); ia += 1
                    else:
                        b[ib](); ib += 1

            NP = NB // 2
            emit_S2((0, 1))
            for f in steps_B(0):
                f()
            for p in range(NP):
                i0 = 2 * p
                sa = steps_A(i0) + steps_A(i0 + 1)
                if p + 1 < NP:
                    emit_S2((i0 + 2, i0 + 3))
                    sb_ = steps_B(i0 + 2)
                else:
                    sb_ = []
                merge(sa, sb_[:-1])
                if sb_:
                    sb_[-1]()
            P.barrier()
    C.stack = None


def build_program(depth, phases=("f1", "proj", "conv", "mla", "gla", "dsa", "out", "f2"), debug=False):
    nc = bass.Bass("TRN2", target_bir_lowering=False)
    C = Ctx(nc)
    P = C.P
    D = C.dram

    def din(name, shape, dt=F32):
        D[name] = nc.dram_tensor(name, list(shape), dt, kind="ExternalInput").ap()
        return D[name]

    def dscr(name, shape, dt):
        D[name] = nc.dram_tensor(name, list(shape), dt, kind=("ExternalOutput" if debug else "Internal")).ap()
        return D[name]

    din("x", [SEQ, D_MODEL])
    din("ident", [128, 128])
    din("ffn_g", [2 * depth, 128, 8])
    din("ffn_wg", [2 * depth, D_MODEL, D_FF])
    din("ffn_wu", [2 * depth, D_MODEL, D_FF])
    din("ffn_wd", [2 * depth, D_FF, D_MODEL])
    din("mix_g", [depth, 128, 8])
    din("w_in", [depth, D_MODEL, IN_WIDTH_P])
    dscr("zT", [N_FM, SEQ], BF16)
    dscr("ztm", [SEQ, N_TM], BF16)
    dscr("yT", [D_MODEL, SEQ], BF16)
    dscr("ytm", [SEQ, D_MODEL], BF16)
    din("rope_c", [32, SEQ]); din("rope_s", [32, SEQ]); din("tri", [128, 128]); din("prot", [96, 96]); din("emat", [32, 96])
    din("d_gqa", [depth, 128, 2]); din("d_wuq", [depth, 128, 2, 384]); din("d_gkva", [depth, 128, 1])
    din("d_wk", [depth, 128, 4, 96]); din("d_wv", [depth, 128, 256]); din("d_gq", [depth, 96, 1]); din("d_gk", [depth, 96, 1])
    din("w_out", [depth, D_MODEL, D_MODEL])
    din("bd64", [128, 128]); din("a_bn", [128, 1024]); din("i4", [128, 512]); din("cneg", [128, 128]); din("a_cf", [1, 512]); din("pow2", [128, NBIS + 1])
    din("a_g2", [depth, 128, 2]); din("a_nthr", [128, SEQ // 128])
    din("bmask", [128, 256]); din("tri64", [128, 256]); din("rmask", [128, 512])
    din("b_wgu", [depth, 16, 128]); din("b_gbias", [depth, 128, 1]); din("b_gout", [depth, 1, 64])
    D["gla_o"] = nc.dram_tensor("gla_o", [SEQ, 256], F32, kind="Internal").ap()
    din("c_w", [depth, 128, 2, 31])
    din("c_b", [depth, 128, 2])
    din("c_g", [depth, 128, 2])
    if debug:
        D["dbg"] = nc.dram_tensor("dbg", [SEQ // 128, 128, 4], F32, kind="ExternalOutput").ap()
    y = nc.dram_tensor("y", [SEQ, D_MODEL], F32, kind="ExternalOutput").ap()
    D["y"] = y
    xb_in = P.bufs_n("xin", SEQ // 128)
    xb_y = P.bufs_n("xy", SEQ // 128)

    with contextlib.ExitStack() as cst:
        load_consts(C, cst)
        src, sb = D["x"], xb_in
        for l in range(depth):
            if "f1" in phases:
                phase_ffn(C, src, y, sb, xb_y, D["ffn_g"][2 * l], D["ffn_wg"][2 * l], D["ffn_wu"][2 * l], D["ffn_wd"][2 * l])
                src, sb = y, xb_y
            if "proj" in phases:
                phase_proj(C, src, D["mix_g"][l], D["w_in"][l])
            if "conv" in phases:
                phase_conv(C, D["c_w"][l], D["c_b"][l], D["c_g"][l])
            if "mla" in phases:
                phase_mla(C, dict(gqa=D["d_gqa"][l], wuq=D["d_wuq"][l], gkva=D["d_gkva"][l], wk=D["d_wk"][l], wv=D["d_wv"][l], gq=D["d_gq"][l], gk=D["d_gk"][l]))
            if "dsa" in phases:
                phase_dsa(C, dict(g2=D["a_g2"][l]))
            if "gla" in phases:
                phase_gla(C, dict(wgu=D["b_wgu"][l], gbias=D["b_gbias"][l], gout=D["b_gout"][l]))
            if "out" in phases:
                phase_out(C, src, y, D["w_out"][l])
                src, sb = y, xb_y
            if "f2" in phases:
                phase_ffn(C, src, y, sb, xb_y, D["ffn_g"][2 * l + 1], D["ffn_wg"][2 * l + 1], D["ffn_wu"][2 * l + 1], D["ffn_wd"][2 * l + 1])
                src, sb = y, xb_y
        P.barrier()
    return nc, C


def arrange_gain(g):
    return np.ascontiguousarray(np.asarray(g, np.float32).reshape(8, 128).T)


def make_inputs(inp, depth=DEPTH, batch=0, x_override=None, layer0=0):
    f32 = np.float32
    L = range(layer0, layer0 + depth)
    m = {}
    m["x"] = np.ascontiguousarray(inp["x"][batch] if x_override is None else x_override, dtype=f32)
    m["ident"] = np.eye(128, dtype=f32)

    def il(a, b):
        return np.ascontiguousarray(np.stack([v for l in L for v in (inp[a][l], inp[b][l])]).astype(f32, copy=False))

    m["ffn_g"] = np.stack([arrange_gain(v) for l in L for v in (inp["ffn1_norm"][l], inp["ffn2_norm"][l])])
    m["ffn_wg"] = il("ffn1_gate", "ffn2_gate")
    m["ffn_wu"] = il("ffn1_up", "ffn2_up")
    m["ffn_wd"] = il("ffn1_down", "ffn2_down")
    m["mix_g"] = np.stack([arrange_gain(inp["mix_norm"][l]) for l in L])
    m["w_in"] = np.stack([permute_w_in(inp["w_in"][l]) for l in L])

    def pc2(v):
        return np.ascontiguousarray(np.asarray(v, f32).reshape(2, 128).T)


    half = 16
    freqs = (np.float32(10000.0) ** (-np.arange(half, dtype=f32) / np.float32(half))).astype(f32)
    ang = np.arange(SEQ, dtype=f32)[None, :] * freqs[:, None]
    m["rope_c"] = np.concatenate([np.cos(ang), np.cos(ang)], 0).astype(f32)
    m["rope_s"] = np.concatenate([np.sin(ang), np.sin(ang)], 0).astype(f32)
    m["tri"] = np.triu(np.ones((128, 128), f32))
    pr = np.zeros((96, 96), f32)
    for i in range(16):
        pr[80 + i, 64 + i] = -1.0
        pr[64 + i, 80 + i] = 1.0
    m["prot"] = pr
    em = np.zeros((32, 96), f32)
    em[np.arange(32), 64 + np.arange(32)] = 1.0
    m["emat"] = em
    m["d_gqa"] = np.stack([pc2(inp["d_qa_norm"][l]) for l in L])
    m["d_wuq"] = np.stack([np.ascontiguousarray(np.asarray(inp["d_uq"][l], f32).reshape(2, 128, 384).transpose(1, 0, 2)) for l in L])
    m["d_gkva"] = np.stack([np.asarray(inp["d_kva_norm"][l], f32).reshape(128, 1) for l in L])
    wk = []
    wv = []
    for l in L:
        w = np.asarray(inp["d_ukv"][l], f32).reshape(128, 4, 128)
        k96 = np.zeros((128, 4, 96), f32)
        k96[:, :, 0:64] = w[:, :, 0:64]
        wk.append(k96)
        wv.append(np.ascontiguousarray(w[:, :, 64:128].reshape(128, 256)))
    m["d_wk"] = np.stack(wk)
    m["d_wv"] = np.stack(wv)
    m["d_gq"] = np.stack([np.asarray(inp["d_q_norm"][l], f32).reshape(96, 1) for l in L])
    m["d_gk"] = np.stack([np.asarray(inp["d_k_norm"][l], f32).reshape(96, 1) for l in L])
    m["bd64"] = np.kron(np.eye(2, dtype=f32), np.ones((64, 64), f32))
    rb = np.asarray(inp["rel_bias"], f32)
    tq = np.arange(128)[:, None]
    sp = np.arange(256)[None, :]
    dist = tq - (sp - 128)
    dpos = np.maximum(dist, 0)
    df = np.maximum(dpos, 1).astype(f32)
    large = 16 + (np.log(df / f32(16)) / f32(math.log(128 / 16)) * f32(16)).astype(np.int32)
    large = np.minimum(large, 31)
    bucket = np.where(dpos < 16, dpos, large)
    m["a_bn"] = np.ascontiguousarray(rb[bucket].transpose(0, 2, 1).reshape(128, 1024))
    m["i4"] = np.ascontiguousarray(np.tile(np.eye(128, dtype=f32), (1, 4)))
    m["cneg"] = np.where(np.arange(128)[None, :] <= np.arange(128)[:, None], f32(0), f32(-1e30)).astype(f32)
    m["a_cf"] = np.ascontiguousarray(np.repeat(rb[31, :], 128)[None, :].astype(f32))
    m["pow2"] = np.ascontiguousarray(np.broadcast_to((2.0 ** -np.arange(NBIS + 1)).astype(f32)[None, :], (128, NBIS + 1)))
    m["a_nthr"] = np.ascontiguousarray(np.broadcast_to((2 * TOPK - 0.5 - 128.0 * (np.arange(SEQ // 128) + 1)).astype(f32)[None, :], (128, SEQ // 128)))
    m["a_g2"] = np.stack([np.stack([np.tile(np.asarray(inp["a_q_norm"][l], f32), 2), np.tile(np.asarray(inp["a_k_norm"][l], f32), 2)], 1) for l in L])
    bm = np.zeros((128, 4, 64), f32)
    for h in range(4):
        bm[h * 32:(h + 1) * 32, h, :] = 1.0
    m["bmask"] = bm.reshape(128, 256)
    t64 = (np.arange(128)[:, None] % 64 <= np.arange(64)[None, :]).astype(f32)
    m["tri64"] = np.ascontiguousarray(np.broadcast_to(t64[:, None, :], (128, 4, 64)).reshape(128, 256))
    rm = np.ones((128, 512), f32)
    rm[:, ::64] = 0.0
    m["rmask"] = rm
    m["b_wgu"] = np.stack([np.asarray(inp["b_gate_up"][l], f32) for l in L])
    m["b_gbias"] = np.stack([np.asarray(inp["b_gate_bias"][l], f32).reshape(128, 1) for l in L])
    m["b_gout"] = np.stack([np.asarray(inp["b_out_norm"][l], f32).reshape(1, 64) for l in L])
    m["w_out"] = np.ascontiguousarray(np.stack([np.asarray(inp["w_out"][l], f32) for l in L]))
    m["c_w"] = np.stack([np.ascontiguousarray(np.asarray(inp["c_dw_w"][l], f32)[:, 0, :].reshape(31, 2, 128).transpose(2, 1, 0)) for l in L])
    m["c_b"] = np.stack([pc2(inp["c_dw_b"][l]) for l in L])
    m["c_g"] = np.stack([pc2(inp["c_norm"][l]) for l in L])
    return m


N_CORES_USED = 4


def kernel(**inputs):
    inp = {k: np.asarray(v) for k, v in inputs.items()}
    nc, C = build_program(DEPTH)
    shared = make_inputs(inp, depth=DEPTH, batch=0)
    in_maps = []
    for b in range(BATCH):
        m = dict(shared)
        m["x"] = np.ascontiguousarray(inp["x"][b], dtype=np.float32)
        in_maps.append(m)
    res = run_bass_kernel_spmd(nc, in_maps, core_ids=list(range(N_CORES_USED)))
    out = np.stack([np.asarray(res.results[b]["y"], dtype=np.float32) for b in range(BATCH)], axis=0)
    return out
```

```python
import contextlib
import math
import numpy as np
import ml_dtypes

import concourse.bass as bass
import concourse.mybir as mybir
from concourse.bass_utils import run_bass_kernel_spmd

F32 = mybir.dt.float32
BF16 = mybir.dt.bfloat16
AF = mybir.ActivationFunctionType
ALU = mybir.AluOpType
AX = mybir.AxisListType

D_MODEL = 1024
SEQ = 4096
BATCH = 4
DEPTH = 4
D_FF = 2816
EPS = 1e-6
IN_WIDTH = 2776
IN_WIDTH_P = 2792

SEM_ROT = 4000


class Buf:
    __slots__ = ("name", "last_w", "readers", "lane")

    def __init__(self, name):
        self.name = name
        self.last_w = None
        self.readers = []
        self.lane = None


class Lane:
    def __init__(self, name, inc):
        self.name = name
        self.inc = inc
        self.n = 0
        self.sems = []
        self.rot = SEM_ROT // inc


class Op:
    __slots__ = ("eng", "fn", "raw", "other", "lane", "sig", "signal", "is_dma")


class Prog:
    ENG_NAMES = ("pe", "act", "dve", "pool", "sp")

    def __init__(self, nc):
        self.nc = nc
        self.engs = {"pe": nc.tensor, "act": nc.scalar, "dve": nc.vector, "pool": nc.gpsimd, "sp": nc.sync}
        self.ops = []
        self.eng_lane = {e: Lane("L_" + e, 1) for e in self.ENG_NAMES}
        self.eng_last = {e: None for e in self.ENG_NAMES}
        self.bufs = []
        self.free_lanes = {"pool": [], "sp": [], "act": []}
        self.used_lanes = []
        self.nlanes = 0
        self.emitted = 0
        self.waited = {e: {} for e in self.ENG_NAMES}
        self.lane_last = {}
        self.nwait = 0

    def buf(self, name):
        b = Buf(name)
        self.bufs.append(b)
        return b

    def bufs_n(self, name, n):
        return [self.buf(f"{name}{i}") for i in range(n)]

    def _add(self, eng, fn, reads, writes, dma_buf=None):
        op = Op()
        op.eng = eng
        op.fn = fn
        op.is_dma = dma_buf is not None
        idx = len(self.ops)
        raw = set()
        other = set()
        for b in reads:
            if b.last_w is not None:
                raw.add(b.last_w)
        for b in writes:
            if b.last_w is not None:
                other.add(b.last_w)
            for r in b.readers:
                other.add(r)
        op.raw = raw
        op.other = other - raw
        if op.is_dma:
            if dma_buf.lane is None:
                fl = self.free_lanes[eng]
                if fl:
                    dma_buf.lane = fl.pop()
                else:
                    dma_buf.lane = Lane(f"D{self.nlanes}{eng}", 16)
                    dma_buf.lane.q = eng
                    self.nlanes += 1
                self.used_lanes.append(dma_buf.lane)
            assert dma_buf.lane.q == eng, (dma_buf.name, dma_buf.lane.q, eng)
            op.lane = dma_buf.lane
            self.lane_last[id(op.lane)] = idx
        else:
            op.lane = self.eng_lane[eng]
        op.sig = None
        op.signal = op.is_dma
        self.ops.append(op)
        if not op.is_dma:
            self.eng_last[eng] = idx
        for b in writes:
            b.last_w = idx
            b.readers = []
        for b in reads:
            if b.last_w != idx:
                b.readers.append(idx)
        return idx

    def op(self, eng, fn, reads=(), writes=()):
        return self._add(eng, fn, reads, writes)

    def dma(self, q, fn, reads, writes, lane_buf):
        return self._add(q, fn, reads, writes, dma_buf=lane_buf)

    def barrier(self):
        allp = set(v for v in self.eng_last.values() if v is not None) | set(self.lane_last.values())
        for e in self.ENG_NAMES:
            op = Op()
            op.eng = e
            op.fn = None
            op.is_dma = False
            op.raw = set(allp)
            op.other = set()
            op.lane = self.eng_lane[e]
            op.sig = None
            op.signal = False
            self.ops.append(op)
        for b in self.bufs:
            b.last_w = None
            b.readers = []
            b.lane = None
        for ln in self.used_lanes:
            self.free_lanes[ln.q].append(ln)
        self.used_lanes = []
        self.emit()

    def _sem(self, lane, sig):
        k = (sig - 1) // lane.rot
        while len(lane.sems) <= k:
            lane.sems.append(self.nc.alloc_semaphore(name=f"{lane.name}_{len(lane.sems)}"))
        return lane.sems[k], (sig - k * lane.rot) * lane.inc

    def emit(self):
        ops = self.ops
        lo = self.emitted
        need = {}
        for i in range(lo, len(ops)):
            o = ops[i]
            w = self.waited[o.eng]
            lst = {}
            for d in (o.raw | o.other):
                if d < lo:
                    continue
                od = ops[d]
                if (not od.is_dma) and (not o.is_dma) and od.eng == o.eng and d not in o.raw:
                    continue
                if od.eng == "pe" and o.eng == "pe" and not od.is_dma and not o.is_dma:
                    continue
                lid = id(od.lane)
                if w.get(lid, -1) >= d:
                    continue
                if lst.get(lid, (-1, None))[0] < d:
                    lst[lid] = (d, od.lane)
            need[i] = lst
            for lid, (d, lane) in lst.items():
                w[lid] = d
                ops[d].signal = True
        for i in range(lo, len(ops)):
            o = ops[i]
            if o.signal and o.sig is None:
                o.lane.n += 1
                o.sig = o.lane.n
        for i in range(lo, len(ops)):
            o = ops[i]
            eng = self.engs[o.eng]
            for lid, (d, lane) in need[i].items():
                sem, v = self._sem(lane, ops[d].sig)
                eng.wait_ge(sem, v)
                self.nwait += 1
            if o.fn is None:
                continue
            ins = o.fn()
            if o.signal:
                sem, v = self._sem(o.lane, o.sig)
                ins.then_inc(sem, o.lane.inc)
            o.fn = None
        self.emitted = len(ops)


class Ctx:
    def __init__(self, nc):
        self.nc = nc
        self.P = Prog(nc)
        self.dram = {}
        self.stack = None

    def sb(self, name, shape, dt):
        self.uid = getattr(self, "uid", 0) + 1
        t = self.stack.enter_context(self.nc.sbuf_tensor(f"{name}_u{self.uid}", list(shape), dt))
        return t

    def ps(self, name, shape, dt):
        self.uid = getattr(self, "uid", 0) + 1
        t = self.stack.enter_context(self.nc.psum_tensor(f"{name}_u{self.uid}", list(shape), dt))
        return t


def load_consts(C, st):
    nc, P = C.nc, C.P
    ident = st.enter_context(nc.sbuf_tensor("ident_sb", [128, 128], BF16))
    C.ident = ident
    C.b_ident = P.buf("ident")
    P.dma("pool", lambda: nc.gpsimd.dma_start(out=ident[:], in_=C.dram["ident"][:, :]), [], [C.b_ident], C.b_ident)
    C.eps_t = st.enter_context(nc.sbuf_tensor("eps_t", [128, 1], F32))
    C.b_cst = P.buf("cst")
    P.op("pool", lambda: nc.gpsimd.memset(C.eps_t[:], EPS), [], [C.b_cst])
    C.ones_bf = st.enter_context(nc.sbuf_tensor("ones_bf", [128, 128], BF16))
    P.op("pool", lambda: nc.gpsimd.memset(C.ones_bf[:], 1.0), [], [C.b_cst])


def phase_ffn(C, x_src, x_dst, xb_src, xb_dst, gain, wg, wu, wd):
    nc, P = C.nc, C.P
    TS = 1024
    NST = SEQ // TS
    NTB = TS // 512
    NFT = D_FF // 128
    panels = [(f0, min(512, D_FF - f0)) for f0 in range(0, D_FF, 512)]
    with contextlib.ExitStack() as st:
        C.stack = st
        gsb = C.sb("gsb", [128, 8], F32)
        xt = [C.sb(f"xt{i}", [128, 1024], F32) for i in range(3)]
        junk = C.sb("junk", [128, 1024], BF16)
        ss = [C.sb(f"ss{i}", [128, 1], F32) for i in range(3)]
        rs = [C.sb(f"rs{i}", [128, 1], F32) for i in range(3)]
        xn = [C.sb(f"xn{i}", [128, 1024], BF16) for i in range(2)]
        xnT2 = [C.sb(f"xnT{i}", [128, 8, TS], BF16) for i in range(2)]
        aT = C.sb("aT", [128, NFT, TS], BF16)
        wgp = [C.sb(f"wgp{i}", [128, 8, 512], BF16) for i in range(2)]
        wup = [C.sb(f"wup{i}", [128, 8, 512], BF16) for i in range(2)]
        wds = C.sb("wds", [128, NFT, 1024], BF16)
        sgt = [C.sb(f"sgt{i}", [128, 512], BF16) for i in range(2)]
        ot = [C.sb(f"ot{i}", [128, 1024], F32) for i in range(2)]
        pst = [C.ps(f"pst{i}", [128, 8, 128], BF16) for i in range(2)]
        pg = [C.ps(f"pg{i}", [128, 512], F32) for i in range(2)]
        pu = [C.ps(f"pu{i}", [128, 512], F32) for i in range(2)]
        pd = [C.ps(f"pd{i}", [128, 512], F32) for i in range(2)]

        b_gsb = P.buf("gsb")
        b_xt = P.bufs_n("xt", 3); b_ss = P.bufs_n("ss", 3); b_rs = P.bufs_n("rs", 3)
        b_junk = P.buf("junk")
        b_xn = P.bufs_n("xn", 2)
        b_xnT2 = [P.bufs_n(f"xnT{i}_", TS // 128) for i in range(2)]
        b_aT = [[P.buf(f"aT{f}_{tb}") for tb in range(NTB)] for f in range(NFT)]
        b_wgp = P.bufs_n("wgp", 2); b_wup = P.bufs_n("wup", 2)
        b_wds = P.bufs_n("wds", NFT)
        b_sgt = P.bufs_n("sgt", 2)
        b_ot = P.bufs_n("ot", 2)
        b_pst = P.bufs_n("pst", 2)
        b_pg = P.bufs_n("pg", 2); b_pu = P.bufs_n("pu", 2); b_pd = P.bufs_n("pd", 2)

        P.dma("sp", lambda: nc.sync.dma_start(out=gsb[:], in_=gain), [], [b_gsb], b_gsb)
        wd_v = wd.rearrange("(fc p) m -> p fc m", p=128)
        wg_v = wg.rearrange("(dc p) f -> p dc f", p=128)
        wu_v = wu.rearrange("(dc p) f -> p dc f", p=128)
        cnt = dict(x=0, xn=0, pst=0, pan=0, gu=0, sg=0, pd=0, ot=0)

        def load_wd():
            for f0 in range(0, NFT, 2):
                P.dma("pool", (lambda f0=f0: nc.gpsimd.dma_start(out=wds[:, f0:f0 + 2, :], in_=wd_v[:, f0:f0 + 2, :])),
                      [], b_wds[f0:f0 + 2], b_wds[f0])

        def load_panel(pi):
            f0, w = panels[pi]
            sl = cnt["pan"] % 2; cnt["pan"] += 1
            for h in range(2):
                P.dma("pool", (lambda sl=sl, f0=f0, w=w, h=h: nc.gpsimd.dma_start(out=wgp[sl][:, 4 * h:4 * h + 4, 0:w], in_=wg_v[:, 4 * h:4 * h + 4, f0:f0 + w])),
                      [], [b_wgp[sl]], b_wgp[sl])
            for h in range(2):
                P.dma("pool", (lambda sl=sl, f0=f0, w=w, h=h: nc.gpsimd.dma_start(out=wup[sl][:, 4 * h:4 * h + 4, 0:w], in_=wu_v[:, 4 * h:4 * h + 4, f0:f0 + w])),
                      [], [b_wup[sl]], b_wup[sl])
            return sl

        def stage1_tile(s, j):
            xnT = xnT2[s % 2]; b_xnT = b_xnT2[s % 2]
            tt = s * (TS // 128) + j
            xs = cnt["x"] % 3; cnt["x"] += 1
            ns = cnt["xn"] % 2; cnt["xn"] += 1
            pp = cnt["pst"] % 2; cnt["pst"] += 1
            P.dma("sp", (lambda: nc.sync.dma_start(out=xt[xs][:], in_=x_src[tt * 128:(tt + 1) * 128, :])),
                  [xb_src[tt]], [b_xt[xs]], b_xt[xs])
            P.op("act", (lambda: nc.scalar.activation(out=junk[:], in_=xt[xs][:], func=AF.Square, accum_out=ss[xs][:])),
                 [b_xt[xs]], [b_junk, b_ss[xs]])
            P.op("act", (lambda: nc.scalar.activation(out=ss[xs][:], in_=ss[xs][:], func=AF.Ln, scale=1.0 / D_MODEL, bias=C.eps_t[:])),
                 [b_ss[xs], C.b_cst], [b_ss[xs]])
            P.op("act", (lambda: nc.scalar.activation(out=rs[xs][:], in_=ss[xs][:], func=AF.Exp, scale=-0.5)),
                 [b_ss[xs]], [b_rs[xs]])
            P.op("dve", (lambda: nc.vector.tensor_scalar(out=xn[ns][:], in0=xt[xs][:], scalar1=rs[xs][:], scalar2=None, op0=ALU.mult)),
                 [b_xt[xs], b_rs[xs]], [b_xn[ns]])
            for dc in range(8):
                P.op("pe", (lambda dc=dc: nc.tensor.transpose(out=pst[pp][:, dc, :], in_=xn[ns][:, dc * 128:(dc + 1) * 128], identity=C.ident[:])),
                     [b_xn[ns], C.b_ident], [b_pst[pp]])
            P.op("dve", (lambda: nc.vector.tensor_tensor(out=xnT[:, :, j * 128:(j + 1) * 128], in0=pst[pp][:], in1=gsb[:, :, None].broadcast_to([128, 8, 128]), op=ALU.mult)),
                 [b_pst[pp], b_gsb], [b_xnT[j]])

        for j in range(TS // 128):
            stage1_tile(0, j)
        sched = [2, 2, 1, 1, 1, 1]
        for s in range(NST):
            xnT = xnT2[s % 2]; b_xnT = b_xnT2[s % 2]
            nxt_tile = 0
            if s == 0:
                nxt = load_panel(0)
            for pi, (f0, w) in enumerate(panels):
                sl = nxt
                if pi + 1 < len(panels):
                    nxt = load_panel(pi + 1)
                elif s + 1 < NST:
                    nxt = load_panel(0)
                if s == 0 and pi == 0:
                    load_wd()
                for fl in range(w // 128):
                    f = f0 // 128 + fl
                    for tb in range(NTB):
                        gs = cnt["gu"] % 2; cnt["gu"] += 1
                        sgs = cnt["sg"] % 2; cnt["sg"] += 1
                        rd = [b_xnT[tb * 4 + k] for k in range(4)]
                        for dc in range(8):
                            P.op("pe", (lambda sl=sl, gs=gs, fl=fl, tb=tb, dc=dc, xnT=xnT: nc.tensor.matmul(out=pg[gs][:], lhsT=wgp[sl][:, dc, fl * 128:(fl + 1) * 128], rhs=xnT[:, dc, tb * 512:(tb + 1) * 512], start=(dc == 0), stop=(dc == 7))),
                                 [b_wgp[sl]] + rd, [b_pg[gs]])
                        for dc in range(8):
                            P.op("pe", (lambda sl=sl, gs=gs, fl=fl, tb=tb, dc=dc, xnT=xnT: nc.tensor.matmul(out=pu[gs][:], lhsT=wup[sl][:, dc, fl * 128:(fl + 1) * 128], rhs=xnT[:, dc, tb * 512:(tb + 1) * 512], start=(dc == 0), stop=(dc == 7))),
                                 [b_wup[sl]] + rd, [b_pu[gs]])
                        P.op("act", (lambda gs=gs, sgs=sgs: nc.scalar.activation(out=sgt[sgs][:], in_=pg[gs][:], func=AF.Silu)),
                             [b_pg[gs]], [b_sgt[sgs]])
                        P.op("dve", (lambda gs=gs, sgs=sgs, f=f, tb=tb: nc.vector.tensor_tensor(out=aT[:, f, tb * 512:(tb + 1) * 512], in0=pu[gs][:], in1=sgt[sgs][:], op=ALU.mult)),
                             [b_pu[gs], b_sgt[sgs]], [b_aT[f][tb]])
                if s + 1 < NST:
                    for _ in range(sched[pi]):
                        stage1_tile(s + 1, nxt_tile); nxt_tile += 1
            for j in range(TS // 128):
                tt = s * (TS // 128) + j
                xs = cnt["x"] % 3; cnt["x"] += 1
                os_ = cnt["ot"] % 2; cnt["ot"] += 1
                P.dma("sp", (lambda xs=xs, tt=tt: nc.sync.dma_start(out=xt[xs][:], in_=x_src[tt * 128:(tt + 1) * 128, :])),
                      [xb_src[tt]], [b_xt[xs]], b_xt[xs])
                for mh in range(2):
                    ds = cnt["pd"] % 2; cnt["pd"] += 1
                    for fc in range(NFT):
                        P.op("pe", (lambda ds=ds, fc=fc, j=j, mh=mh: nc.tensor.matmul(out=pd[ds][:], lhsT=aT[:, fc, j * 128:(j + 1) * 128], rhs=wds[:, fc, mh * 512:(mh + 1) * 512], start=(fc == 0), stop=(fc == NFT - 1))),
                             [b_aT[fc][j // 4], b_wds[fc]], [b_pd[ds]])
                    P.op("dve", (lambda ds=ds, os_=os_, xs=xs, mh=mh: nc.vector.scalar_tensor_tensor(out=ot[os_][:, mh * 512:(mh + 1) * 512], in0=pd[ds][:], scalar=0.5, in1=xt[xs][:, mh * 512:(mh + 1) * 512], op0=ALU.mult, op1=ALU.add)),
                         [b_pd[ds], b_xt[xs]], [b_ot[os_]])
                P.dma("sp", (lambda os_=os_, tt=tt: nc.sync.dma_start(out=x_dst[tt * 128:(tt + 1) * 128, :], in_=ot[os_][:])),
                      [b_ot[os_]], [xb_dst[tt]], b_ot[os_])
        P.barrier()
    C.stack = None


FM = dict(aq=0, ak=256, iq=512, bq=768, bk=896, ca=1024, cg=1280, dcq=1536, dckv=1792, ik=1920, dkpe=1952, bg=1984)
N_FM = 2016
TM = dict(av=0, br=256, bv=512, iw=768)
N_TM = 776
ORIG = dict(aq=(0, 256), ak=(256, 512), av=(512, 768), iq=(768, 1024), ik=(1024, 1056), iw=(1056, 1064),
            bq=(1064, 1192), bk=(1192, 1320), bv=(1320, 1576), bg=(1576, 1592), br=(1592, 1848),
            cu=(1848, 2360), dcq=(2360, 2616), dckv=(2616, 2744), dkpe=(2744, 2776))


def w_in_perm():
    order = ["aq", "ak", "iq", "bq", "bk", "cu", "dcq", "dckv", "ik", "dkpe", "bg", "av", "br", "bv", "iw"]
    parts = []
    for k in order:
        parts.append(np.arange(*ORIG[k]))
        if k == "bg":
            parts.append(np.full(16, -1))
    idx = np.concatenate(parts)
    assert idx.shape[0] == IN_WIDTH_P
    return idx


def permute_w_in(w):
    idx = w_in_perm()
    out = np.zeros((w.shape[0], IN_WIDTH_P), np.float32)
    m = idx >= 0
    out[:, m] = np.asarray(w)[:, idx[m]]
    return out


def emit_norm_T(C, st_bufs, x_src, tt, j, xnT_ap_fn, b_out):
    nc, P = C.nc, C.P
    S = st_bufs
    xs = S["cnt"]["x"] % 3; S["cnt"]["x"] += 1
    ns = S["cnt"]["xn"] % 2; S["cnt"]["xn"] += 1
    pp = S["cnt"]["pst"] % 2; S["cnt"]["pst"] += 1
    xt, ss, rs, xn, pst, junk, gsb = S["xt"], S["ss"], S["rs"], S["xn"], S["pst"], S["junk"], S["gsb"]
    b = S["b"]
    P.dma("sp", (lambda: nc.sync.dma_start(out=xt[xs][:], in_=x_src[tt * 128:(tt + 1) * 128, :])),
          [], [b["xt"][xs]], b["xt"][xs])
    P.op("act", (lambda: nc.scalar.activation(out=junk[:], in_=xt[xs][:], func=AF.Square, accum_out=ss[xs][:])),
         [b["xt"][xs]], [b["junk"], b["ss"][xs]])
    P.op("act", (lambda: nc.scalar.activation(out=ss[xs][:], in_=ss[xs][:], func=AF.Ln, scale=1.0 / D_MODEL, bias=C.eps_t[:])),
         [b["ss"][xs], C.b_cst], [b["ss"][xs]])
    P.op("act", (lambda: nc.scalar.activation(out=rs[xs][:], in_=ss[xs][:], func=AF.Exp, scale=-0.5)),
         [b["ss"][xs]], [b["rs"][xs]])
    P.op("dve", (lambda: nc.vector.tensor_scalar(out=xn[ns][:], in0=xt[xs][:], scalar1=rs[xs][:], scalar2=None, op0=ALU.mult)),
         [b["xt"][xs], b["rs"][xs]], [b["xn"][ns]])
    for dc in range(8):
        P.op("pe", (lambda dc=dc: nc.tensor.transpose(out=pst[pp][:, dc, :], in_=xn[ns][:, dc * 128:(dc + 1) * 128], identity=C.ident[:])),
             [b["xn"][ns], C.b_ident], [b["pst"][pp]])
    P.op("dve", (lambda: nc.vector.tensor_tensor(out=xnT_ap_fn(j), in0=pst[pp][:], in1=gsb[:, :, None].broadcast_to([128, 8, 128]), op=ALU.mult)),
         [b["pst"][pp], b["gsb"]], [b_out])
    return xs


def norm_T_state(C, gain):
    nc, P = C.nc, C.P
    S = dict(cnt=dict(x=0, xn=0, pst=0))
    S["gsb"] = C.sb("gsb", [128, 8], F32)
    S["xt"] = [C.sb(f"xt{i}", [128, 1024], F32) for i in range(3)]
    S["junk"] = C.sb("junk", [128, 1024], BF16)
    S["ss"] = [C.sb(f"ss{i}", [128, 1], F32) for i in range(3)]
    S["rs"] = [C.sb(f"rs{i}", [128, 1], F32) for i in range(3)]
    S["xn"] = [C.sb(f"xn{i}", [128, 1024], BF16) for i in range(2)]
    S["pst"] = [C.ps(f"pst{i}", [128, 8, 128], BF16) for i in range(2)]
    S["b"] = dict(gsb=P.buf("gsb"), xt=P.bufs_n("xt", 3), ss=P.bufs_n("ss", 3), rs=P.bufs_n("rs", 3),
                  junk=P.buf("junk"), xn=P.bufs_n("xn", 2), pst=P.bufs_n("pst", 2))
    P.dma("sp", lambda: nc.sync.dma_start(out=S["gsb"][:], in_=gain), [], [S["b"]["gsb"]], S["b"]["gsb"])
    return S


def phase_proj(C, x_src, gain, w_in):
    nc, P = C.nc, C.P
    zT, ztm = C.dram["zT"], C.dram["ztm"]
    NMT = (N_FM + 127) // 128
    with contextlib.ExitStack() as st:
        C.stack = st
        S = norm_T_state(C, gain)
        win = C.sb("win", [128, 8, IN_WIDTH_P], BF16)
        NCB = (IN_WIDTH_P + 511) // 512
        b_win = P.bufs_n("win", NCB)
        w_v = w_in.rearrange("(dc p) f -> p dc f", p=128)
        for cb in range(NCB):
            c0 = cb * 512
            c1 = min(IN_WIDTH_P, c0 + 512)
            P.dma("pool", (lambda c0=c0, c1=c1: nc.gpsimd.dma_start(out=win[:, :, c0:c1], in_=w_v[:, :, c0:c1])), [], [b_win[cb]], b_win[cb])
        xnT = [C.sb(f"xnT{i}", [128, 8, 512], BF16) for i in range(2)]
        b_xnT = [P.bufs_n(f"xnT{i}_", 4) for i in range(2)]
        zsb = [C.sb(f"zsb{i}", [128, 512], BF16) for i in range(3)]
        b_zsb = P.bufs_n("zsb", 3)
        zts = [C.sb(f"zts{i}", [128, N_TM], BF16) for i in range(2)]
        b_zts = P.bufs_n("zts", 2)
        pz = [C.ps(f"pz{i}", [128, 512], F32) for i in range(6)]
        b_pz = P.bufs_n("pz", 6)
        cz = 0; cs = 0; ct = 0; cz2 = 0
        for tb in range(SEQ // 512):
            xi = tb % 2
            for j in range(4):
                emit_norm_T(C, S, x_src, tb * 4 + j, j, (lambda j, xi=xi: xnT[xi][:, :, j * 128:(j + 1) * 128]), b_xnT[xi][j])
            import os as _os
            for mt in range(NMT if not _os.environ.get('NO_FM') else 0):
                rows = min(128, N_FM - mt * 128)
                pi = cz % 4; cz += 1
                si = cs % 3; cs += 1
                for dc in range(8):
                    P.op("pe", (lambda dc=dc, pi=pi, mt=mt, rows=rows, xi=xi: nc.tensor.matmul(out=pz[pi][0:rows, :], lhsT=win[:, dc, mt * 128:mt * 128 + rows], rhs=xnT[xi][:, dc, :], start=(dc == 0), stop=(dc == 7))),
                         [b_win[(mt * 128) // 512]] + b_xnT[xi], [b_pz[pi]])
                if mt % 2 == 0:
                    P.op("act", (lambda pi=pi, si=si, rows=rows: nc.scalar.copy(out=zsb[si][0:rows, :], in_=pz[pi][0:rows, :])), [b_pz[pi]], [b_zsb[si]])
                else:
                    P.op("dve", (lambda pi=pi, si=si, rows=rows: nc.vector.tensor_copy(out=zsb[si][0:rows, :], in_=pz[pi][0:rows, :])), [b_pz[pi]], [b_zsb[si]])
                P.dma("sp", (lambda si=si, mt=mt, rows=rows, tb=tb: nc.sync.dma_start(out=zT[mt * 128:mt * 128 + rows, tb * 512:(tb + 1) * 512], in_=zsb[si][0:rows, :])),
                      [b_zsb[si]], [], b_zsb[si])
            for j in range(4 if not _os.environ.get('NO_TM') else 0):
                ti = ct % 2; ct += 1
                for (c0, n) in ((0, 512), (512, N_TM - 512)):
                    pi = 4 + cz2 % 2; cz2 += 1
                    for dc in range(8):
                        P.op("pe", (lambda dc=dc, pi=pi, j=j, c0=c0, n=n, xi=xi: nc.tensor.matmul(out=pz[pi][:, 0:n], lhsT=xnT[xi][:, dc, j * 128:(j + 1) * 128], rhs=win[:, dc, N_FM + c0:N_FM + c0 + n], start=(dc == 0), stop=(dc == 7))),
                             [b_win[cbi] for cbi in range((N_FM + c0) // 512, (N_FM + c0 + n - 1) // 512 + 1)] + [b_xnT[xi][j]], [b_pz[pi]])
                    if c0 == 0:
                        P.op("act", (lambda pi=pi, ti=ti, c0=c0, n=n: nc.scalar.copy(out=zts[ti][:, c0:c0 + n], in_=pz[pi][:, 0:n])), [b_pz[pi]], [b_zts[ti]])
                    else:
                        P.op("dve", (lambda pi=pi, ti=ti, c0=c0, n=n: nc.vector.tensor_copy(out=zts[ti][:, c0:c0 + n], in_=pz[pi][:, 0:n])), [b_pz[pi]], [b_zts[ti]])
                tt = tb * 4 + j
                P.dma("sp", (lambda ti=ti, tt=tt: nc.sync.dma_start(out=ztm[tt * 128:(tt + 1) * 128, :], in_=zts[ti][:])),
                      [b_zts[ti]], [], b_zts[ti])
        P.barrier()
    C.stack = None


def phase_conv(C, cw, cb, cg):
    nc, P = C.nc, C.P
    zT, yT = C.dram["zT"], C.dram["yT"]
    KW = 31
    with contextlib.ExitStack() as st:
        C.stack = st
        cw_sb = C.sb("cw", [128, 2, KW], F32); cb_sb = C.sb("cb", [128, 2], F32); cg_sb = C.sb("cg", [128, 2], F32)
        b_par = P.bufs_n("cpar", 3)
        P.dma("sp", lambda: nc.sync.dma_start(out=cw_sb[:], in_=cw), [], [b_par[0]], b_par[0])
        P.dma("sp", lambda: nc.sync.dma_start(out=cb_sb[:], in_=cb), [], [b_par[1]], b_par[1])
        P.dma("sp", lambda: nc.sync.dma_start(out=cg_sb[:], in_=cg), [], [b_par[2]], b_par[2])
        dg = C.sb("dg", [128, 2, KW, 128], BF16)
        b_dg = P.bufs_n("dg", 2)
        for ct in range(2):
            for j in range(KW):
                P.op("dve", (lambda ct=ct, j=j: nc.vector.tensor_scalar(out=dg[:, ct, j, :], in0=C.ident[:], scalar1=cw_sb[:, ct, j:j + 1], scalar2=None, op0=ALU.mult)),
                     [C.b_ident, b_par[0]], [b_dg[ct]])
        a_sb = C.sb("a_sb", [128, SEQ], BF16); g_sb = C.sb("g_sb", [128, SEQ], BF16); sg = C.sb("sg", [128, SEQ], BF16)
        b_a = P.buf("a_sb"); b_g = P.buf("g_sb"); b_sg = P.buf("sg")
        hp = [C.sb(f"hp{i}", [128, 32 + SEQ], BF16) for i in range(2)]
        b_hp = P.bufs_n("hp", 2)
        for ct in range(2):
            P.dma("sp", (lambda ct=ct: nc.sync.dma_start(out=a_sb[:], in_=zT[FM["ca"] + ct * 128:FM["ca"] + (ct + 1) * 128, :])), [], [b_a], b_a)
            P.dma("sp", (lambda ct=ct: nc.sync.dma_start(out=g_sb[:], in_=zT[FM["cg"] + ct * 128:FM["cg"] + (ct + 1) * 128, :])), [], [b_g], b_g)
            P.op("pool", (lambda ct=ct: nc.gpsimd.memset(hp[ct][:, 0:32], 0.0)), [], [b_hp[ct]])
            P.op("act", (lambda: nc.scalar.activation(out=sg[:], in_=g_sb[:], func=AF.Sigmoid)), [b_g], [b_sg])
            P.op("dve", (lambda ct=ct: nc.vector.tensor_tensor(out=hp[ct][:, 32:], in0=a_sb[:], in1=sg[:], op=ALU.mult)), [b_a, b_sg], [b_hp[ct]])
        pc = [C.ps(f"pc{i}", [128, 512], F32) for i in range(4)]
        b_pc = P.bufs_n("pc", 4)
        pss = [C.ps(f"pss{i}", [128, 512], F32) for i in range(2)]
        b_pss = P.bufs_n("pss", 2)
        cvb = [C.sb(f"cvb{i}", [128, 512], F32) for i in range(4)]; b_cvb = P.bufs_n("cvb", 4)
        sq = [C.sb(f"sq{i}", [128, 512], BF16) for i in range(4)]; b_sq = P.bufs_n("sq", 4)
        lnt = [C.sb(f"lnt{i}", [128, 512], F32) for i in range(2)]; b_lnt = P.bufs_n("lnt", 2)
        rr = [C.sb(f"rr{i}", [128, 512], F32) for i in range(2)]; b_rr = P.bufs_n("rr", 2)
        uu = [C.sb(f"uu{i}", [128, 512], F32) for i in range(2)]; b_uu = P.bufs_n("uu", 2)
        ys = [C.sb(f"ys{i}", [128, 512], BF16) for i in range(2)]; b_ys = P.bufs_n("ys", 2)
        cu = 0
        for tb in range(SEQ // 512):
            k2 = tb % 2
            for ct in range(2):
                k = (tb % 2) * 2 + ct
                for j in range(KW):
                    c0 = 32 + tb * 512 - (KW - 1) + j
                    P.op("pe", (lambda k=k, ct=ct, j=j, c0=c0: nc.tensor.matmul(out=pc[k][:], lhsT=dg[:, ct, j, :], rhs=hp[ct][:, c0:c0 + 512], start=(j == 0), stop=(j == KW - 1))),
                         [b_dg[ct], b_hp[ct]], [b_pc[k]])
                P.op("act", (lambda k=k, ct=ct: nc.scalar.activation(out=cvb[k][:], in_=pc[k][:], func=AF.Identity, bias=cb_sb[:, ct:ct + 1])), [b_pc[k], b_par[1]], [b_cvb[k]])
                P.op("act", (lambda k=k, ct=ct: nc.scalar.activation(out=sq[k][:], in_=pc[k][:], func=AF.Square, bias=cb_sb[:, ct:ct + 1])), [b_pc[k], b_par[1]], [b_sq[k]])
            for ct in range(2):
                k = (tb % 2) * 2 + ct
                P.op("pe", (lambda k=k, ct=ct, k2=k2: nc.tensor.matmul(out=pss[k2][:], lhsT=C.ones_bf[:], rhs=sq[k][:], start=(ct == 0), stop=(ct == 1))),
                     [C.b_cst, b_sq[k]], [b_pss[k2]])
            P.op("act", (lambda k2=k2: nc.scalar.activation(out=lnt[k2][:], in_=pss[k2][:], func=AF.Ln, scale=1.0 / 256, bias=C.eps_t[:])), [b_pss[k2], C.b_cst], [b_lnt[k2]])
            P.op("act", (lambda k2=k2: nc.scalar.activation(out=rr[k2][:], in_=lnt[k2][:], func=AF.Exp, scale=-0.5)), [b_lnt[k2]], [b_rr[k2]])
            for ct in range(2):
                k = (tb % 2) * 2 + ct
                ui = cu % 2; cu += 1
                P.op("dve", (lambda k=k, ct=ct, k2=k2, ui=ui: nc.vector.scalar_tensor_tensor(out=uu[ui][:], in0=cvb[k][:], scalar=cg_sb[:, ct:ct + 1], in1=rr[k2][:], op0=ALU.mult, op1=ALU.mult)),
                     [b_cvb[k], b_par[2], b_rr[k2]], [b_uu[ui]])
                P.op("act", (lambda ui=ui: nc.scalar.activation(out=ys[ui][:], in_=uu[ui][:], func=AF.Silu)), [b_uu[ui]], [b_ys[ui]])
                P.dma("sp", (lambda ui=ui, ct=ct, tb=tb: nc.sync.dma_start(out=yT[512 + ct * 128:512 + (ct + 1) * 128, tb * 512:(tb + 1) * 512], in_=ys[ui][:])),
                      [b_ys[ui]], [], b_ys[ui])
        P.barrier()
    C.stack = None


def phase_mla(C, prm):
    nc, P = C.nc, C.P
    zT, ytm = C.dram["zT"], C.dram["ytm"]
    D = C.dram
    NB = SEQ // 512
    with contextlib.ExitStack() as outer:
        C.stack = outer
        qT = C.sb("qT", [96, 4, SEQ], BF16); kT = C.sb("kT", [96, 4, SEQ], BF16)
        v1 = C.sb("v1", [128, SEQ // 128, 4, 65], BF16)
        ctab = C.sb("ctab", [96, SEQ], F32); stab = C.sb("stab", [96, SEQ], F32)
        tri = C.sb("tri", [128, 128], BF16)
        with contextlib.ExitStack() as st:
            C.stack = st
            b_tab = P.bufs_n("tab", 3)
            P.dma("sp", lambda: nc.sync.dma_start(out=ctab[64:96, :], in_=D["rope_c"][:, :]), [], [b_tab[0]], b_tab[0])
            P.dma("sp", lambda: nc.sync.dma_start(out=stab[64:96, :], in_=D["rope_s"][:, :]), [], [b_tab[1]], b_tab[1])
            P.dma("pool", lambda: nc.gpsimd.dma_start(out=tri[:], in_=D["tri"][:, :]), [], [b_tab[2]], b_tab[2])
            wuq_f = C.sb("wuq_f", [128, 2, 384], F32); wuq_s = C.sb("wuq_s", [128, 2, 384], BF16)
            wk_f = C.sb("wk_f", [128, 4, 96], F32); wk_s = C.sb("wk_s", [128, 4, 96], BF16)
            wv_f = C.sb("wv_f", [128, 256], F32); wv_s = C.sb("wv_s", [128, 256], BF16)
            gqa = C.sb("gqa", [128, 2], F32); gkva = C.sb("gkva", [128, 1], F32)
            gq = C.sb("gq", [96, 1], F32); gk = C.sb("gk", [96, 1], F32); gqs = C.sb("gqs", [96, 1], F32)
            prot = C.sb("prot", [96, 96], BF16); emat = C.sb("emat", [32, 96], BF16)
            b_w = P.bufs_n("mlaw", 12)
            ld = [(wuq_f, prm["wuq"], "sp"), (wk_f, prm["wk"], "sp"), (wv_f, prm["wv"], "sp"), (gqa, prm["gqa"], "sp"),
                  (gkva, prm["gkva"], "sp"), (gq, prm["gq"], "sp"), (gk, prm["gk"], "sp"),
                  (prot, D["prot"], "pool"), (emat, D["emat"], "pool")]
            for n, (t, a, q) in enumerate(ld):
                if q == "sp":
                    P.dma("sp", (lambda t=t, a=a: nc.sync.dma_start(out=t[:], in_=a)), [], [b_w[n]], b_w[n])
                else:
                    P.dma("pool", (lambda t=t, a=a: nc.gpsimd.dma_start(out=t[:], in_=a)), [], [b_w[n]], b_w[n])
            P.op("dve", lambda: nc.vector.tensor_tensor(out=wuq_s[:], in0=wuq_f[:], in1=gqa[:, :, None].broadcast_to([128, 2, 384]), op=ALU.mult), [b_w[0], b_w[3]], [b_w[9]])
            P.op("dve", lambda: nc.vector.tensor_scalar(out=wk_s[:], in0=wk_f[:], scalar1=gkva[:, 0:1], scalar2=None, op0=ALU.mult), [b_w[1], b_w[4]], [b_w[10]])
            P.op("dve", lambda: nc.vector.tensor_scalar(out=wv_s[:], in0=wv_f[:], scalar1=gkva[:, 0:1], scalar2=None, op0=ALU.mult), [b_w[2], b_w[4]], [b_w[11]])
            P.op("dve", lambda: nc.vector.tensor_scalar(out=gqs[:], in0=gq[:], scalar1=float(96 ** -0.5), scalar2=None, op0=ALU.mult), [b_w[5]], [b_w[5]])
            b_v1 = P.buf("v1")
            P.op("pool", lambda: nc.gpsimd.memset(v1[:], 1.0), [], [b_v1])
            b_qT = [[P.buf(f"qT{h}_{tb}") for tb in range(NB)] for h in range(4)]
            b_kT = [[P.buf(f"kT{h}_{tb}") for tb in range(NB)] for h in range(4)]

            cq = [C.sb(f"cq{i}", [128, 2, 512], BF16) for i in range(2)]; b_cq = P.bufs_n("cq", 2)
            ckv = [C.sb(f"ckv{i}", [128, 512], BF16) for i in range(2)]; b_ckv = P.bufs_n("ckv", 2)
            kpe = [C.sb(f"kpe{i}", [32, 512], BF16) for i in range(2)]; b_kpe = P.bufs_n("kpe", 2)
            sqc = [C.sb(f"sqc{i}", [128, 3, 512], BF16) for i in range(2)]; b_sqc = P.bufs_n("sqc", 2)
            lnt = [C.sb(f"lnt{i}", [128, 512], F32) for i in range(2)]; b_lnt = P.bufs_n("lnt", 2)
            r1 = [C.sb(f"r1{i}", [128, 512], F32) for i in range(2)]; b_r1 = P.bufs_n("r1", 2)
            r2 = [C.sb(f"r2{i}", [128, 512], F32) for i in range(2)]; b_r2 = P.bufs_n("r2", 2)
            cqn = [C.sb(f"cqn{i}", [128, 2, 512], BF16) for i in range(2)]; b_cqn = P.bufs_n("cqn", 2)
            ckvn = [C.sb(f"ckvn{i}", [128, 512], BF16) for i in range(2)]; b_ckvn = P.bufs_n("ckvn", 2)
            sqh = [C.sb(f"sqh{i}", [96, 512], BF16) for i in range(2)]; b_sqh = P.bufs_n("sqh", 2)
            lnh = [C.sb(f"lnh{i}", [96, 512], F32) for i in range(2)]; b_lnh = P.bufs_n("lnh", 2)
            rh = [C.sb(f"rh{i}", [96, 512], F32) for i in range(2)]; b_rh = P.bufs_n("rh", 2)
            t1 = [C.sb(f"t1{i}", [96, 512], F32) for i in range(2)]; b_t1 = P.bufs_n("t1", 2)
            t2 = [C.sb(f"t2{i}", [96, 512], F32) for i in range(2)]; b_t2 = P.bufs_n("t2", 2)
            pss = [C.ps(f"pss{i}", [128, 512], F32) for i in range(2)]; b_pss = P.bufs_n("pss", 2)
            praw = [C.ps(f"praw{i}", [128, 512], F32) for i in range(3)]; b_praw = P.bufs_n("praw", 3)
            prt = [C.ps(f"prt{i}", [128, 512], F32) for i in range(2)]; b_prt = P.bufs_n("prt", 2)
            pvv = C.ps("pvv", [128, 256], F32); b_pvv = P.buf("pvv")
            cn = dict(ss=0, raw=0, h=0)

            def nr_stages(mm_fn, gvec, b_g, dst, b_dst, cols, bidx):
                k = bidx % 2
                st8 = {}

                def s1():
                    ri = cn["raw"] % 3; cn["raw"] += 1
                    st8["ri"] = ri
                    mm_fn(ri)
                    P.op("act", (lambda: nc.scalar.activation(out=sqh[k][:], in_=praw[ri][0:96, :], func=AF.Square)), [b_praw[ri]], [b_sqh[k]])

                def s2():
                    ri = st8["ri"]
                    s2i = cn["ss"] % 2; cn["ss"] += 1
                    P.op("pe", (lambda: nc.tensor.matmul(out=pss[s2i][0:96, :], lhsT=C.ones_bf[0:96, 0:96], rhs=sqh[k][:], start=True, stop=True)), [C.b_cst, b_sqh[k]], [b_pss[s2i]])
                    P.op("act", (lambda: nc.scalar.activation(out=lnh[k][:], in_=pss[s2i][0:96, :], func=AF.Ln, scale=1.0 / 96, bias=C.eps_t[0:96, :])), [b_pss[s2i], C.b_cst], [b_lnh[k]])
                    P.op("act", (lambda: nc.scalar.activation(out=rh[k][:], in_=lnh[k][:], func=AF.Exp, scale=-0.5)), [b_lnh[k]], [b_rh[k]])
                    P.op("dve", (lambda: nc.vector.scalar_tensor_tensor(out=dst, in0=praw[ri][0:96, :], scalar=gvec[:, 0:1], in1=rh[k][:], op0=ALU.mult, op1=ALU.mult)), [b_praw[ri], b_g, b_rh[k]], [b_dst])

                def s3():
                    P.op("pe", (lambda: nc.tensor.matmul(out=prt[k][0:96, :], lhsT=prot[:], rhs=dst, start=True, stop=True)), [b_w[7], b_dst], [b_prt[k]])
                    P.op("dve", (lambda: nc.vector.tensor_tensor(out=t1[k][64:96, :], in0=dst[64:96, :], in1=ctab[64:96, cols], op=ALU.mult)), [b_dst, b_tab[0]], [b_t1[k]])
                    P.op("dve", (lambda: nc.vector.tensor_tensor(out=t2[k][64:96, :], in0=prt[k][64:96, :], in1=stab[64:96, cols], op=ALU.mult)), [b_prt[k], b_tab[1]], [b_t2[k]])
                    P.op("dve", (lambda: nc.vector.tensor_tensor(out=dst[64:96, :], in0=t1[k][64:96, :], in1=t2[k][64:96, :], op=ALU.add)), [b_t1[k], b_t2[k]], [b_dst])
                return (s1, s2, s3)

            for tb in range(NB):
                i2 = tb % 2
                cols = slice(tb * 512, (tb + 1) * 512)
                P.dma("sp", (lambda i2=i2, cols=cols: nc.sync.dma_start(out=cq[i2][:], in_=zT[FM["dcq"]:FM["dcq"] + 256, cols].rearrange("(ct p) t -> p ct t", p=128))), [], [b_cq[i2]], b_cq[i2])
                P.dma("sp", (lambda i2=i2, cols=cols: nc.sync.dma_start(out=ckv[i2][:], in_=zT[FM["dckv"]:FM["dckv"] + 128, cols])), [], [b_ckv[i2]], b_ckv[i2])
                P.dma("sp", (lambda i2=i2, cols=cols: nc.sync.dma_start(out=kpe[i2][:], in_=zT[FM["dkpe"]:FM["dkpe"] + 32, cols])), [], [b_kpe[i2]], b_kpe[i2])
                P.op("act", (lambda i2=i2: nc.scalar.activation(out=sqc[i2][:, 0:2, :], in_=cq[i2][:], func=AF.Square)), [b_cq[i2]], [b_sqc[i2]])
                P.op("act", (lambda i2=i2: nc.scalar.activation(out=sqc[i2][:, 2, :], in_=ckv[i2][:], func=AF.Square)), [b_ckv[i2]], [b_sqc[i2]])
                s2 = cn["ss"] % 2; cn["ss"] += 1
                for ct in range(2):
                    P.op("pe", (lambda i2=i2, ct=ct, s2=s2: nc.tensor.matmul(out=pss[s2][:], lhsT=C.ones_bf[:], rhs=sqc[i2][:, ct, :], start=(ct == 0), stop=(ct == 1))), [C.b_cst, b_sqc[i2]], [b_pss[s2]])
                P.op("act", (lambda i2=i2, s2=s2: nc.scalar.activation(out=lnt[i2][:], in_=pss[s2][:], func=AF.Ln, scale=1.0 / 256, bias=C.eps_t[:])), [b_pss[s2], C.b_cst], [b_lnt[i2]])
                P.op("act", (lambda i2=i2: nc.scalar.activation(out=r1[i2][:], in_=lnt[i2][:], func=AF.Exp, scale=-0.5)), [b_lnt[i2]], [b_r1[i2]])
                P.op("dve", (lambda i2=i2: nc.vector.tensor_tensor(out=cqn[i2][:], in0=cq[i2][:], in1=r1[i2][:, None, :].broadcast_to([128, 2, 512]), op=ALU.mult)), [b_cq[i2], b_r1[i2]], [b_cqn[i2]])
                s2 = cn["ss"] % 2; cn["ss"] += 1
                P.op("pe", (lambda i2=i2, s2=s2: nc.tensor.matmul(out=pss[s2][:], lhsT=C.ones_bf[:], rhs=sqc[i2][:, 2, :], start=True, stop=True)), [C.b_cst, b_sqc[i2]], [b_pss[s2]])
                P.op("act", (lambda i2=i2, s2=s2: nc.scalar.activation(out=lnt[i2][:], in_=pss[s2][:], func=AF.Ln, scale=1.0 / 128, bias=C.eps_t[:])), [b_pss[s2], C.b_cst], [b_lnt[i2]])
                P.op("act", (lambda i2=i2: nc.scalar.activation(out=r2[i2][:], in_=lnt[i2][:], func=AF.Exp, scale=-0.5)), [b_lnt[i2]], [b_r2[i2]])
                P.op("dve", (lambda i2=i2: nc.vector.tensor_tensor(out=ckvn[i2][:], in0=ckv[i2][:], in1=r2[i2][:], op=ALU.mult)), [b_ckv[i2], b_r2[i2]], [b_ckvn[i2]])
                blks = []
                for h in range(4):
                    def mmq(ri, h=h, i2=i2):
                        for ct in range(2):
                            P.op("pe", (lambda ct=ct: nc.tensor.matmul(out=praw[ri][0:96, :], lhsT=wuq_s[:, ct, h * 96:(h + 1) * 96], rhs=cqn[i2][:, ct, :], start=(ct == 0), stop=(ct == 1))),
                                 [b_w[9], b_cqn[i2]], [b_praw[ri]])
                    blks.append(nr_stages(mmq, gqs, b_w[5], qT[:, h, cols], b_qT[h][tb], cols, len(blks)))
                for h in range(4):
                    def mmk(ri, h=h, i2=i2):
                        P.op("pe", (lambda: nc.tensor.matmul(out=praw[ri][0:96, :], lhsT=wk_s[:, h, :], rhs=ckvn[i2][:], start=True, stop=False)), [b_w[10], b_ckvn[i2]], [b_praw[ri]])
                        P.op("pe", (lambda: nc.tensor.matmul(out=praw[ri][0:96, :], lhsT=emat[:], rhs=kpe[i2][:], start=False, stop=True)), [b_w[8], b_kpe[i2]], [b_praw[ri]])
                    blks.append(nr_stages(mmk, gk, b_w[6], kT[:, h, cols], b_kT[h][tb], cols, len(blks)))
                nb_ = len(blks)
                for idx in range(nb_ + 2):
                    if idx < nb_:
                        blks[idx][0]()
                    if 0 <= idx - 1 < nb_:
                        blks[idx - 1][1]()
                    if 0 <= idx - 2 < nb_:
                        blks[idx - 2][2]()
                for j in range(4):
                    P.op("pe", (lambda i2=i2, j=j: nc.tensor.matmul(out=pvv[:], lhsT=ckvn[i2][:, j * 128:(j + 1) * 128], rhs=wv_s[:], start=True, stop=True)), [b_ckvn[i2], b_w[11]], [b_pvv])
                    P.op("act", (lambda tb=tb, j=j: nc.scalar.copy(out=v1[:, tb * 4 + j, :, 0:64], in_=pvv[:].rearrange("p (h d) -> p h d", h=4))), [b_pvv], [b_v1])
            P.barrier()
        with contextlib.ExitStack() as st:
            C.stack = st
            psS = [C.ps(f"psS{i}", [128, 512], F32) for i in range(3)]; b_psS = P.bufs_n("psS", 3)
            po = [C.ps(f"po{i}", [128, 512], F32) for i in range(2)]; b_po = P.bufs_n("po", 2)
            pT = [C.sb(f"pT{i}", [128, 512], BF16) for i in range(3)]; b_pT = P.bufs_n("pT", 3)
            rc = [C.sb(f"rc{i}", [128, 4], F32) for i in range(2)]; b_rc = P.bufs_n("rc", 2)
            yd = [C.sb(f"yd{i}", [128, 4, 4, 64], BF16) for i in range(2)]; b_yd = P.bufs_n("yd", 2)
            cS = [0]
            for tt in range(NB):
                t0 = tt * 512
                yi = tt % 2
                for h in range(4):
                    oi = (tt * 4 + h) % 2
                    nch = 4 * tt + 4
                    slots = {}

                    def qk(c, h=h, t0=t0, tt=tt):
                        d = c - 4 * tt
                        off = max(d, 0) * 128
                        N = 512 - off
                        si = cS[0] % 3; cS[0] += 1
                        slots[c] = (si, d, off, N)
                        P.op("pe", (lambda: nc.tensor.matmul(out=psS[si][:, 0:N], lhsT=kT[:, h, c * 128:(c + 1) * 128], rhs=qT[:, h, t0 + off:t0 + 512], start=True, stop=True)), [], [b_psS[si]])

                    def ex(c):
                        si, d, off, N = slots[c]
                        P.op("act", (lambda: nc.scalar.activation(out=pT[si][:, 0:N], in_=psS[si][:, 0:N], func=AF.Exp)), [b_psS[si]], [b_pT[si]])
                        if d >= 0:
                            P.op("dve", (lambda: nc.vector.tensor_tensor(out=pT[si][:, 0:128], in0=pT[si][:, 0:128], in1=tri[:], op=ALU.mult)), [b_pT[si]], [b_pT[si]])

                    def pv(c, h=h, tt=tt, oi=oi):
                        si, d, off, N = slots[c]
                        for j in range(max(d, 0), 4):
                            P.op("pe", (lambda j=j: nc.tensor.matmul(out=po[oi][:, j * 65:(j + 1) * 65], lhsT=pT[si][:, j * 128 - off:j * 128 - off + 128], rhs=v1[:, c, h, :], start=(c == 0 and j == 0), stop=(c == 4 * tt + j), skip_group_check=True)),
                                 [b_pT[si]], [b_po[oi]])

                    for idx in range(nch + 2):
                        if idx < nch:
                            qk(idx)
                        if 0 <= idx - 1 < nch:
                            ex(idx - 1)
                        if 0 <= idx - 2 < nch:
                            pv(idx - 2)
                    P.op("dve", (lambda oi=oi: nc.vector.reciprocal(out=rc[oi][:], in_=po[oi][:, 0:260].rearrange("p (j e) -> p j e", e=65)[:, :, 64])), [b_po[oi]], [b_rc[oi]])
                    P.op("dve", (lambda oi=oi, yi=yi, h=h: nc.vector.tensor_tensor(out=yd[yi][:, :, h, :], in0=po[oi][:, 0:260].rearrange("p (j e) -> p j e", e=65)[:, :, 0:64], in1=rc[oi][:, :, None].broadcast_to([128, 4, 64]), op=ALU.mult)),
                         [b_po[oi], b_rc[oi]], [b_yd[yi]])
                P.dma("sp", (lambda yi=yi, t0=t0: nc.sync.dma_start(out=ytm[t0:t0 + 512, 768:1024].rearrange("(j p) (h d) -> p j h d", p=128, h=4), in_=yd[yi][:])), [b_yd[yi]], [], b_yd[yi])
            P.barrier()
    C.stack = None


def phase_out(C, x_src, x_dst, w_out):
    nc, P = C.nc, C.P
    ytm, yT = C.dram["ytm"], C.dram["yT"]
    with contextlib.ExitStack() as st:
        C.stack = st
        wo = C.sb("wo", [128, 8, 1024], BF16); b_wo = P.bufs_n("wo", 8)
        w_v = w_out.rearrange("(c p) m -> p c m", p=128)
        for c in range(0, 8, 2):
            P.dma("pool", (lambda c=c: nc.gpsimd.dma_start(out=wo[:, c:c + 2, :], in_=w_v[:, c:c + 2, :])), [], b_wo[c:c + 2], b_wo[c])
        NS = 3
        yt = [C.sb(f"yt{i}", [128, 1024], BF16) for i in range(NS)]; b_yt = P.bufs_n("yt", NS)
        yTt = [C.sb(f"yTt{i}", [128, 8, 128], BF16) for i in range(NS)]
        b_yTa = P.bufs_n("yTa", NS); b_yTc = P.bufs_n("yTc", NS)
        xt = [C.sb(f"xt{i}", [128, 1024], F32) for i in range(NS)]; b_xt = P.bufs_n("xt", NS)
        ot = [C.sb(f"ot{i}", [128, 1024], F32) for i in range(2)]; b_ot = P.bufs_n("ot", 2)
        pst = [C.ps(f"pst{i}", [128, 8, 128], BF16) for i in range(2)]; b_pst = P.bufs_n("pst", 2)
        pd = [C.ps(f"pd{i}", [128, 512], F32) for i in range(4)]; b_pd = P.bufs_n("pd", 4)
        cd = 0
        NT = SEQ // 128

        def loads(tt):
            k = tt % NS
            rows = slice(tt * 128, (tt + 1) * 128)
            P.dma("sp", (lambda: nc.sync.dma_start(out=yt[k][:, 0:512], in_=ytm[rows, 0:512])), [], [b_yt[k]], b_yt[k])
            P.dma("sp", (lambda: nc.sync.dma_start(out=yt[k][:, 768:1024], in_=ytm[rows, 768:1024])), [], [b_yt[k]], b_yt[k])
            P.dma("sp", (lambda: nc.sync.dma_start(out=yTt[k][:, 4:6, :], in_=yT[512:768, rows].rearrange("(ct p) t -> p ct t", p=128))), [], [b_yTc[k]], b_yTc[k])
            P.dma("sp", (lambda: nc.sync.dma_start(out=xt[k][:], in_=x_src[rows, :])), [], [b_xt[k]], b_xt[k])

        loads(0)
        loads(1)
        for tt in range(NT):
            k = tt % NS
            k2 = tt % 2
            rows = slice(tt * 128, (tt + 1) * 128)
            if tt + 2 < NT:
                loads(tt + 2)
            for c in (0, 1, 2, 3, 6, 7):
                P.op("pe", (lambda k=k, k2=k2, c=c: nc.tensor.transpose(out=pst[k2][:, c, :], in_=yt[k][:, c * 128:(c + 1) * 128], identity=C.ident[:])), [b_yt[k], C.b_ident], [b_pst[k2]])
            P.op("act", (lambda k=k, k2=k2: nc.scalar.copy(out=yTt[k][:, 0:4, :], in_=pst[k2][:, 0:4, :])), [b_pst[k2]], [b_yTa[k]])
            P.op("dve", (lambda k=k, k2=k2: nc.vector.tensor_copy(out=yTt[k][:, 6:8, :], in_=pst[k2][:, 6:8, :])), [b_pst[k2]], [b_yTa[k]])
            for mh in range(2):
                di = cd % 4; cd += 1
                for c in range(8):
                    P.op("pe", (lambda k=k, c=c, mh=mh, di=di: nc.tensor.matmul(out=pd[di][:], lhsT=yTt[k][:, c, :], rhs=wo[:, c, mh * 512:(mh + 1) * 512], start=(c == 0), stop=(c == 7))),
                         [b_yTa[k], b_yTc[k], b_wo[c]], [b_pd[di]])
                P.op("dve", (lambda k=k, k2=k2, mh=mh, di=di: nc.vector.tensor_tensor(out=ot[k2][:, mh * 512:(mh + 1) * 512], in0=pd[di][:], in1=xt[k][:, mh * 512:(mh + 1) * 512], op=ALU.add)),
                     [b_pd[di], b_xt[k]], [b_ot[k2]])
            P.dma("sp", (lambda k2=k2, rows=rows: nc.sync.dma_start(out=x_dst[rows, :], in_=ot[k2][:])), [b_ot[k2]], [], b_ot[k2])
        P.barrier()
    C.stack = None


def phase_gla(C, prm):
    nc, P = C.nc, C.P
    zT, ztm, ytm, glo = C.dram["zT"], C.dram["ztm"], C.dram["ytm"], C.dram["gla_o"]
    D = C.dram
    NCH = SEQ // 64
    with contextlib.ExitStack() as st:
        C.stack = st
        qT = C.sb("gq", [128, SEQ], BF16); kT = C.sb("gk", [128, SEQ], BF16); bgT = C.sb("bgT", [16, SEQ], BF16)
        vv = C.sb("gv", [128, SEQ // 128, 256], BF16)
        b_in = P.bufs_n("gin", 4)
        P.dma("sp", lambda: nc.sync.dma_start(out=qT[:], in_=zT[FM["bq"]:FM["bq"] + 128, :]), [], [b_in[0]], b_in[0])
        P.dma("sp", lambda: nc.sync.dma_start(out=kT[:], in_=zT[FM["bk"]:FM["bk"] + 128, :]), [], [b_in[1]], b_in[1])
        P.dma("sp", lambda: nc.sync.dma_start(out=bgT[:], in_=zT[FM["bg"]:FM["bg"] + 16, :]), [], [b_in[2]], b_in[2])
        P.dma("sp", lambda: nc.sync.dma_start(out=vv[:], in_=ztm[:, TM["bv"]:TM["bv"] + 256].rearrange("(n p) c -> p n c", p=128)), [], [b_in[3]], b_in[3])
        wgu = C.sb("wgu", [16, 128], BF16); gb = C.sb("gb", [128, 1], F32); ngb = C.sb("ngb", [128, 1], F32)
        gout = C.sb("gout", [128, 64], F32)
        bmask = C.sb("bmask", [128, 4, 64], BF16); tri64 = C.sb("tri64", [128, 4, 64], BF16); rmask = C.sb("rmask", [128, 512], F32)
        one_c = C.sb("one_c", [128, 1], F32)
        b_p = P.bufs_n("gpar", 8)
        P.dma("pool", lambda: nc.gpsimd.dma_start(out=wgu[:], in_=prm["wgu"]), [], [b_p[0]], b_p[0])
        P.dma("sp", lambda: nc.sync.dma_start(out=gb[:], in_=prm["gbias"]), [], [b_p[1]], b_p[1])
        P.dma("sp", lambda: nc.sync.dma_start(out=gout[:], in_=prm["gout"].partition_broadcast(128)), [], [b_p[2]], b_p[2])
        P.dma("pool", lambda: nc.gpsimd.dma_start(out=bmask[:], in_=D["bmask"].rearrange("p (h d) -> p h d", h=4)), [], [b_p[3]], b_p[3])
        P.dma("pool", lambda: nc.gpsimd.dma_start(out=tri64[:], in_=D["tri64"].rearrange("p (h d) -> p h d", h=4)), [], [b_p[4]], b_p[4])
        P.dma("sp", lambda: nc.sync.dma_start(out=rmask[:], in_=D["rmask"][:, :]), [], [b_p[5]], b_p[5])
        P.op("dve", lambda: nc.vector.tensor_scalar(out=ngb[:], in0=gb[:], scalar1=-1.0, scalar2=None, op0=ALU.mult), [b_p[1]], [b_p[6]])
        P.op("pool", lambda: nc.gpsimd.memset(one_c[:], 1.0), [], [b_p[7]])
        la = C.sb("la", [128, SEQ], F32); cs = C.sb("cs", [128, SEQ], F32); tmpf = C.sb("tmpf", [128, SEQ], F32)
        b_la = P.bufs_n("la", 8); b_cs = P.bufs_n("cs", 8)
        b_tmp = P.buf("tmpf")
        pgt = [C.ps(f"pgt{i}", [128, 512], F32) for i in range(1)]; b_pgt = P.bufs_n("pgt", 1)
        for tb in range(8):
            cols = slice(tb * 512, (tb + 1) * 512)
            k = 0
            P.op("pe", (lambda k=k, cols=cols: nc.tensor.matmul(out=pgt[k][:], lhsT=wgu[:], rhs=bgT[:, cols], start=True, stop=True)), [b_p[0], b_in[2]], [b_pgt[k]])
            P.op("act", (lambda k=k, cols=cols: nc.scalar.activation(out=la[:, cols], in_=pgt[k][:], func=AF.Exp, scale=-1.0, bias=ngb[:])), [b_pgt[k], b_p[6]], [b_la[tb]])
            P.op("act", (lambda cols=cols: nc.scalar.activation(out=la[:, cols], in_=la[:, cols], func=AF.Ln, bias=one_c[:])), [b_la[tb], b_p[7]], [b_la[tb]])
            P.op("dve", (lambda cols=cols: nc.vector.tensor_tensor_scan(out=cs[:, cols], data0=rmask[:], data1=la[:, cols], initial=0.0, op0=ALU.mult, op1=ALU.add)), [b_la[tb], b_p[5]], [b_cs[tb]])
        qe = C.sb("qe", [128, SEQ], BF16); ke = C.sb("ke", [128, SEQ], BF16); kdT = C.sb("kdT", [128, SEQ], BF16)
        ebl = C.sb("ebl", [128, NCH], F32)
        b_qe = P.buf("qe"); b_ke = P.buf("ke"); b_kd = P.buf("kdT"); b_ebl = P.buf("ebl")
        csl = cs[:].rearrange("p (c i) -> p c i", i=64)[:, :, 63]
        P.op("act", lambda: nc.scalar.activation(out=tmpf[:], in_=cs[:], func=AF.Exp, scale=-1.0 / 16), b_cs, [b_tmp])
        P.op("dve", lambda: nc.vector.scalar_tensor_tensor(out=qe[:], in0=qT[:], scalar=float(32 ** -0.5), in1=tmpf[:], op0=ALU.mult, op1=ALU.mult), [b_in[0], b_tmp], [b_qe])
        P.op("act", lambda: nc.scalar.activation(out=tmpf[:], in_=cs[:], func=AF.Exp, scale=1.0 / 16), b_cs, [b_tmp])
        P.op("dve", lambda: nc.vector.tensor_tensor(out=ke[:], in0=kT[:], in1=tmpf[:], op=ALU.mult), [b_in[1], b_tmp], [b_ke])
        P.op("act", lambda: nc.scalar.activation(out=ebl[:], in_=csl, func=AF.Exp, scale=-1.0 / 16), b_cs, [b_ebl])
        P.op("dve", lambda: nc.vector.tensor_tensor(out=tmpf[:].rearrange("p (c i) -> p c i", i=64), in0=cs[:].rearrange("p (c i) -> p c i", i=64), in1=csl.unsqueeze(2).broadcast_to([128, NCH, 64]), op=ALU.subtract), b_cs, [b_tmp])
        P.op("act", lambda: nc.scalar.activation(out=tmpf[:], in_=tmpf[:], func=AF.Exp, scale=1.0 / 16), [b_tmp], [b_tmp])
        P.op("dve", lambda: nc.vector.tensor_tensor(out=kdT[:], in0=kT[:], in1=tmpf[:], op=ALU.mult), [b_in[1], b_tmp], [b_kd])
        S = C.sb("S", [128, 4, 64], F32); Sbf = C.sb("Sbf", [128, 256], BF16)
        b_S = P.buf("S"); b_Sbf = P.buf("Sbf")
        P.op("dve", lambda: nc.vector.memset(S[:], 0.0), [], [b_S])
        P.op("pool", lambda: nc.gpsimd.memset(Sbf[:], 0.0), [], [b_Sbf])
        Qbd = [C.sb(f"Qbd{i}", [128, 4, 64], BF16) for i in range(2)]; b_Qbd = P.bufs_n("Qbd", 2)
        Am = [C.sb(f"Am{i}", [128, 256], BF16) for i in range(2)]; b_Am = P.bufs_n("Am", 2)
        kdm = [C.sb(f"kdm{i}", [128, 128], BF16) for i in range(2)]; b_kdm = P.bufs_n("kdm", 2)
        tS = [C.sb(f"tS{i}", [128, 4, 64], F32) for i in range(2)]; b_tS = P.bufs_n("tS", 2)
        osb = [C.sb(f"osb{i}", [64, 256], F32) for i in range(2)]; b_osb = P.bufs_n("osb", 2)
        pA = [C.ps(f"pA{i}", [128, 256], F32) for i in range(2)]; b_pA = P.bufs_n("pA", 2)
        pO = [C.ps(f"pO{i}", [64, 256], F32) for i in range(2)]; b_pO = P.bufs_n("pO", 2)
        pK = C.ps("pK", [128, 128], BF16); b_pK = P.buf("pK")
        pS = [C.ps(f"pS{i}", [128, 256], F32) for i in range(2)]; b_pS = P.bufs_n("pS", 2)
        def pre(c):
            k = c % 2
            pr = c // 2
            base = k * 64
            cols = slice(c * 64, (c + 1) * 64)
            pcols = slice(pr * 128, (pr + 1) * 128)
            kk = pr % 2
            if k == 0:
                P.op("pe", (lambda: nc.tensor.transpose(out=pK[:], in_=kdT[:, pcols], identity=C.ident[:])), [b_kd, C.b_ident], [b_pK])
                P.op("act", (lambda: nc.scalar.copy(out=kdm[kk][:], in_=pK[:])), [b_pK], [b_kdm[kk]])
            P.op("pool", (lambda: nc.gpsimd.tensor_tensor(out=Qbd[k][:], in0=qe[:, cols].unsqueeze(1).broadcast_to([128, 4, 64]), in1=bmask[:], op=ALU.mult)), [b_qe, b_p[3]], [b_Qbd[k]])
            P.op("pe", (lambda: nc.tensor.matmul(out=pA[k][:], lhsT=ke[:, pcols], rhs=Qbd[k][:].rearrange("p h d -> p (h d)"), start=True, stop=True)), [b_ke, b_Qbd[k]], [b_pA[k]])
            P.op("dve", (lambda: nc.vector.tensor_tensor(out=Am[k][base:base + 64, :], in0=pA[k][base:base + 64, :], in1=tri64[base:base + 64, :, :].rearrange("p h d -> p (h d)"), op=ALU.mult)), [b_pA[k], b_p[4]], [b_Am[k]])
            P.op("pe", (lambda: nc.tensor.matmul(out=pS[k][:], lhsT=kdm[kk][base:base + 64, :], rhs=vv[base:base + 64, pr, :], start=True, stop=True)), [b_kdm[kk], b_in[3]], [b_pS[k]])
            P.op("dve", (lambda: nc.vector.tensor_tensor(out=tS[k][:], in0=pS[k][:].rearrange("p (h d) -> p h d", h=4), in1=bmask[:], op=ALU.mult)), [b_pS[k], b_p[3]], [b_tS[k]])

        def post(c):
            k = c % 2
            pr = c // 2
            base = k * 64
            cols = slice(c * 64, (c + 1) * 64)
            P.op("pe", (lambda: nc.tensor.matmul(out=pO[k][:], lhsT=qe[:, cols], rhs=Sbf[:], start=True, stop=False, skip_group_check=True)), [b_qe, b_Sbf], [b_pO[k]])
            for h in range(4):
                P.op("pe", (lambda h=h: nc.tensor.matmul(out=pO[k][:, h * 64:(h + 1) * 64], lhsT=Am[k][base:base + 64, h * 64:(h + 1) * 64], rhs=vv[base:base + 64, pr, h * 64:(h + 1) * 64], start=False, stop=(h == 3), skip_group_check=True)),
                     [b_Am[k], b_in[3]], [b_pO[k]])
            P.op("dve", (lambda: nc.vector.scalar_tensor_tensor(out=S[:], in0=S[:], scalar=ebl[:, c:c + 1], in1=tS[k][:], op0=ALU.mult, op1=ALU.add)), [b_S, b_ebl, b_tS[k]], [b_S])
            P.op("act", (lambda: nc.scalar.copy(out=Sbf[:], in_=S[:].rearrange("p h d -> p (h d)"))), [b_S], [b_Sbf])
            P.op("act", (lambda: nc.scalar.copy(out=osb[k][:], in_=pO[k][:])), [b_pO[k]], [b_osb[k]])
            P.dma("sp", (lambda: nc.sync.dma_start(out=glo[c * 64:(c + 1) * 64, :], in_=osb[k][:])), [b_osb[k]], [], b_osb[k])

        pre(0)
        for c in range(NCH):
            if c + 1 < NCH:
                pre(c + 1)
            post(c)
        P.barrier()
    with contextlib.ExitStack() as st:
        C.stack = st
        gout = C.sb("gout", [128, 64], F32); b_g = P.buf("gout")
        P.dma("sp", lambda: nc.sync.dma_start(out=gout[:], in_=prm["gout"].partition_broadcast(128)), [], [b_g], b_g)
        ot = [C.sb(f"got{i}", [128, 4, 64], F32) for i in range(3)]; b_ot = P.bufs_n("got", 3)
        junk = C.sb("gjunk", [128, 64], BF16); b_junk = P.buf("gjunk")
        ssa = C.sb("ssa", [128, SEQ // 128, 4], F32); b_ssa = P.buf("ssa")
        rsa = C.sb("rsa", [128, SEQ // 128, 4], F32); b_rsa = P.buf("rsa")
        NT = SEQ // 128
        for tt in range(NT):
            k = tt % 3
            P.dma("sp", (lambda k=k, tt=tt: nc.sync.dma_start(out=ot[k][:], in_=glo[tt * 128:(tt + 1) * 128, :].rearrange("p (h d) -> p h d", h=4))), [], [b_ot[k]], b_ot[k])
            for h in range(4):
                P.op("act", (lambda k=k, tt=tt, h=h: nc.scalar.activation(out=junk[:], in_=ot[k][:, h, :], func=AF.Square, accum_out=ssa[:, tt, h:h + 1])), [b_ot[k]], [b_junk, b_ssa])
        P.op("act", lambda: nc.scalar.activation(out=rsa[:], in_=ssa[:], func=AF.Ln, scale=1.0 / 64, bias=C.eps_t[:]), [b_ssa, C.b_cst], [b_rsa])
        P.op("act", lambda: nc.scalar.activation(out=rsa[:], in_=rsa[:], func=AF.Exp, scale=-0.5), [b_rsa], [b_rsa])
        rt = [C.sb(f"grt{i}", [128, 256], BF16) for i in range(2)]; b_rt = P.bufs_n("grt", 2)
        sr = [C.sb(f"gsr{i}", [128, 4, 64], F32) for i in range(2)]; b_sr = P.bufs_n("gsr", 2)
        on = [C.sb(f"gon{i}", [128, 4, 64], F32) for i in range(2)]; b_on = P.bufs_n("gon", 2)
        yb = [C.sb(f"gyb{i}", [128, 4, 64], BF16) for i in range(2)]; b_yb = P.bufs_n("gyb", 2)
        for tt in range(NT):
            k = tt % 3
            k2 = tt % 2
            rows = slice(tt * 128, (tt + 1) * 128)
            P.dma("sp", (lambda k=k, rows=rows: nc.sync.dma_start(out=ot[k][:], in_=glo[rows, :].rearrange("p (h d) -> p h d", h=4))), [], [b_ot[k]], b_ot[k])
            P.dma("sp", (lambda k2=k2, rows=rows: nc.sync.dma_start(out=rt[k2][:], in_=ztm[rows, TM["br"]:TM["br"] + 256])), [], [b_rt[k2]], b_rt[k2])
            P.op("act", (lambda k2=k2: nc.scalar.activation(out=sr[k2][:].rearrange("p h d -> p (h d)"), in_=rt[k2][:], func=AF.Silu)), [b_rt[k2]], [b_sr[k2]])
            P.op("dve", (lambda k=k, k2=k2, tt=tt: nc.vector.tensor_tensor(out=on[k2][:], in0=ot[k][:], in1=rsa[:, tt, :].unsqueeze(2).broadcast_to([128, 4, 64]), op=ALU.mult)), [b_ot[k], b_rsa], [b_on[k2]])
            P.op("dve", (lambda k2=k2: nc.vector.tensor_tensor(out=on[k2][:], in0=on[k2][:], in1=gout[:].unsqueeze(1).broadcast_to([128, 4, 64]), op=ALU.mult)), [b_on[k2], b_g], [b_on[k2]])
            P.op("dve", (lambda k2=k2: nc.vector.tensor_tensor(out=yb[k2][:], in0=on[k2][:], in1=sr[k2][:], op=ALU.mult)), [b_on[k2], b_sr[k2]], [b_yb[k2]])
            P.dma("sp", (lambda k2=k2, rows=rows: nc.sync.dma_start(out=ytm[rows, 256:512], in_=yb[k2][:].rearrange("p h d -> p (h d)"))), [b_yb[k2]], [], b_yb[k2])
        P.barrier()
    C.stack = None


NBIS = 16
TOPK = 256
NEG_MASK = -30000.0


def phase_dsa(C, prm, eng_bis="dve"):
    nc, P = C.nc, C.P
    zT, ztm, ytm = C.dram["zT"], C.dram["ztm"], C.dram["ytm"]
    D = C.dram
    NB = SEQ // 128
    with contextlib.ExitStack() as outer:
        C.stack = outer
        qT2 = C.sb("aqT", [64, 4, SEQ], BF16); kT2 = C.sb("akT", [64, 4, SEQ], BF16)
        v1 = C.sb("av1", [128, NB, 4, 65], BF16)
        ikT = C.sb("ikT", [32, SEQ], BF16)
        with contextlib.ExitStack() as st:
            C.stack = st
            raw = C.sb("araw", [64, 4, SEQ], BF16); b_raw = P.bufs_n("araw", 2)
            vst = C.sb("avst", [128, NB, 256], BF16); b_vst = P.buf("avst")
            g2 = C.sb("ag2", [128, 2], F32); g2s = C.sb("ag2s", [128, 1], F32); b_g2 = P.bufs_n("ag2", 2)
            bd64 = C.sb("bd64", [128, 128], BF16); b_bd = P.buf("bd64")
            b_v1 = P.buf("av1"); b_ik = P.buf("ikT")
            P.dma("sp", lambda: nc.sync.dma_start(out=g2[:], in_=prm["g2"]), [], [b_g2[0]], b_g2[0])
            P.dma("pool", lambda: nc.gpsimd.dma_start(out=bd64[:], in_=D["bd64"][:, :]), [], [b_bd], b_bd)
            P.dma("sp", lambda: nc.sync.dma_start(out=ikT[:], in_=zT[FM["ik"]:FM["ik"] + 32, :]), [], [b_ik], b_ik)
            P.dma("sp", lambda: nc.sync.dma_start(out=vst[:], in_=ztm[:, TM["av"]:TM["av"] + 256].rearrange("(n p) c -> p n c", p=128)), [], [b_vst], b_vst)
            P.op("pool", lambda: nc.gpsimd.memset(v1[:], 1.0), [], [b_v1])
            P.op("dve", lambda: nc.vector.tensor_scalar(out=g2s[:], in0=g2[:, 0:1], scalar1=0.125, scalar2=None, op0=ALU.mult), [b_g2[0]], [b_g2[1]])
            for n in range(0, NB, 8):
                P.op("act", (lambda n=n: nc.scalar.copy(out=v1[:, n:n + 8, :, 0:64], in_=vst[:, n:n + 8, :].rearrange("p n (h d) -> p n h d", h=4))), [b_vst, b_v1], [b_v1])
            b_qk = P.buf("aqk")
            NSL = 3
            sq = [C.sb(f"asq{i}", [128, 512], BF16) for i in range(NSL)]; b_sq = P.bufs_n("asq", NSL)
            lnt = [C.sb(f"alnt{i}", [128, 512], F32) for i in range(NSL)]; b_lnt = P.bufs_n("alnt", NSL)
            rr = [C.sb(f"arr{i}", [128, 512], F32) for i in range(NSL)]; b_rr = P.bufs_n("arr", NSL)
            pss = [C.ps(f"apss{i}", [128, 512], F32) for i in range(NSL)]; b_pss = P.bufs_n("apss", NSL)
            for which, (dst, row0, gv, bg) in enumerate(((qT2, FM["aq"], g2s, b_g2[1]), (kT2, FM["ak"], g2, b_g2[0]))):
                P.dma("sp", (lambda row0=row0: nc.sync.dma_start(out=raw[:], in_=zT[row0:row0 + 256, :].rearrange("(hp p) t -> p hp t", p=64))), [], b_raw, b_raw[0])
                gcol = gv[0:64, 0:1] if which == 0 else gv[0:64, 1:2]
                blks = [(hp, tb) for hp in range(4) for tb in range(SEQ // 512)]

                def stg(n, stage, dst=dst, gcol=gcol, bg=bg):
                    hp, tb = blks[n]
                    k = n % NSL
                    cols = slice(tb * 512, (tb + 1) * 512)
                    if stage == 0:
                        P.op("act", (lambda: nc.scalar.activation(out=sq[k][0:64, :], in_=raw[:, hp, cols], func=AF.Square)), b_raw, [b_sq[k]])
                    elif stage == 1:
                        P.op("pe", (lambda: nc.tensor.matmul(out=pss[k][0:64, :], lhsT=C.ones_bf[0:64, 0:64], rhs=sq[k][0:64, :], start=True, stop=True)), [C.b_cst, b_sq[k]], [b_pss[k]])
                        P.op("act", (lambda: nc.scalar.activation(out=lnt[k][0:64, :], in_=pss[k][0:64, :], func=AF.Ln, scale=1.0 / 64, bias=C.eps_t[0:64, :])), [b_pss[k], C.b_cst], [b_lnt[k]])
                    elif stage == 2:
                        P.op("act", (lambda: nc.scalar.activation(out=rr[k][0:64, :], in_=lnt[k][0:64, :], func=AF.Exp, scale=-0.5)), [b_lnt[k]], [b_rr[k]])
                    else:
                        P.op("dve", (lambda: nc.vector.scalar_tensor_tensor(out=dst[:, hp, cols], in0=raw[:, hp, cols], scalar=gcol, in1=rr[k][0:64, :], op0=ALU.mult, op1=ALU.mult)),
                             b_raw + [bg, b_rr[k]], [b_qk])
                nb_ = len(blks)
                for idx in range(nb_ + 3):
                    for stage in range(4):
                        n = idx - stage
                        if 0 <= n < nb_:
                            stg(n, stage)
            P.barrier()
        with contextlib.ExitStack() as st:
            C.stack = st
            Bn = C.sb("Bn", [128, 4, 256], BF16); I4 = C.sb("I4", [128, 4, 128], BF16); cneg = C.sb("cneg", [128, 128], F32)
            cfrow = C.sb("cfrow", [1, 512], BF16); onesrow = C.sb("onesrow", [1, 128], BF16)
            p2 = C.sb("p2", [128, NBIS + 1], F32); halfc = C.sb("halfc", [128, 1], F32)
            b_c = P.bufs_n("dcst", 8)
            P.dma("pool", lambda: nc.gpsimd.dma_start(out=Bn[:], in_=D["a_bn"].rearrange("p (h s) -> p h s", h=4)), [], [b_c[0]], b_c[0])
            P.dma("pool", lambda: nc.gpsimd.dma_start(out=I4[:], in_=D["i4"].rearrange("p (h s) -> p h s", h=4)), [], [b_c[1]], b_c[1])
            P.dma("sp", lambda: nc.sync.dma_start(out=cneg[:], in_=D["cneg"][:, :]), [], [b_c[2]], b_c[2])
            P.dma("pool", lambda: nc.gpsimd.dma_start(out=cfrow[:], in_=D["a_cf"][:, :]), [], [b_c[3]], b_c[3])
            P.dma("sp", lambda: nc.sync.dma_start(out=p2[:], in_=D["pow2"][:, :]), [], [b_c[4]], b_c[4])
            P.op("pool", lambda: nc.gpsimd.memset(onesrow[:], 1.0), [], [b_c[5]])
            score = [C.sb(f"score{i}", [128, SEQ], F32) for i in range(2)]; b_score = P.bufs_n("score", 2)
            maskb = [C.sb(f"maskb{i}", [128, SEQ], BF16) for i in range(2)]; b_maskb = P.bufs_n("maskb", 2)
            junk = C.sb("bjunk", [128, SEQ], BF16); b_junk = P.buf("bjunk")
            junk2 = C.sb("bjunk2", [128, SEQ], BF16); b_junk2 = P.buf("bjunk2")
            b_cnt = P.bufs_n("bcnt", 2); b_sgd = P.bufs_n("bsgd", 2)
            b_scp = [P.bufs_n(f"scp{kk}_", SEQ // 512) for kk in range(2)]
            iqb = [C.sb(f"iqb{i}", [32, 8, 128], BF16) for i in range(2)]; b_iqb = P.bufs_n("iqb", 2)
            iwb = [C.sb(f"iwb{i}", [128, 8], BF16) for i in range(2)]; b_iwb = P.bufs_n("iwb", 2)
            iwf = [C.sb(f"iwf{i}", [128, 8], F32) for i in range(2)]; b_iwf = P.bufs_n("iwf", 2)
            qb = [C.sb(f"qb{i}", [128, 2, 128], BF16) for i in range(2)]; b_qb = P.bufs_n("qb", 2)
            rsb = [C.sb(f"rsb{i}", [128, 512], F32) for i in range(3)]; b_rsb = P.bufs_n("rsb", 3)
            pT = [C.sb(f"apT{i}", [128, 512], BF16) for i in range(3)]; b_pT = P.bufs_n("apT", 3)
            st_ = [dict(amax=C.sb(f"amax{i}", [128, 1], F32), dt=C.sb(f"dt{i}", [128, NBIS + 1], F32), d2=C.sb(f"d2{i}", [128, NBIS + 1], F32),
                        mid=C.sb(f"mid{i}", [128, 1], F32), cnt=C.sb(f"cnt{i}", [128, 1], F32), sgd=C.sb(f"sgd{i}", [128, 1], F32),
                        thr=C.sb(f"thr{i}", [128, 1], F32)) for i in range(2)]
            b_st = [P.buf(f"bst{i}") for i in range(2)]
            rc = [C.sb(f"arc{i}", [128, 4], F32) for i in range(2)]; b_rc = P.bufs_n("arc", 2)
            ya = [C.sb(f"aya{i}", [128, 4, 64], BF16) for i in range(2)]; b_ya = P.bufs_n("aya", 2)
            psc = [C.ps(f"psc{i}", [128, 512], F32) for i in range(3)]; b_psc = P.bufs_n("psc", 3)
            psA = [C.ps(f"psA{i}", [128, 512], F32) for i in range(3)]; b_psA = P.bufs_n("psA", 3)
            po = [C.ps(f"apo{i}", [128, 512], F32) for i in range(2)]; b_po = P.bufs_n("apo", 2)
            ce = dict(sc=0, A=0)
            sgn = [C.sb(f"sgn{i}", [128, 1], F32) for i in range(2)]
            nthr = C.sb("nthr", [128, NB], F32)
            P.dma("sp", lambda: nc.sync.dma_start(out=nthr[:], in_=D["a_nthr"][:, :]), [], [b_c[6]], b_c[6])

            def emit_S2(blocks):
                units = []
                for i in blocks:
                    k = i % 2
                    rows = slice(i * 128, (i + 1) * 128)
                    N = 128 * (i + 1)
                    P.dma("sp", (lambda k=k, rows=rows: nc.sync.dma_start(out=iqb[k][:], in_=zT[FM["iq"]:FM["iq"] + 256, rows].rearrange("(h d) t -> d h t", d=32))), [], [b_iqb[k]], b_iqb[k])
                    P.dma("sp", (lambda k=k, rows=rows: nc.sync.dma_start(out=iwb[k][:], in_=ztm[rows, TM["iw"]:TM["iw"] + 8])), [], [b_iwb[k]], b_iwb[k])
                    P.op("act", (lambda k=k: nc.scalar.copy(out=iwf[k][:], in_=iwb[k][:])), [b_iwb[k]], [b_iwf[k]])
                    for pi_, p0 in enumerate(range(0, N, 512)):
                        units.append((k, pi_, p0, min(512, N - p0)))
                for hh in range(8):
                    for (k, pi_, p0, w) in units:
                        si = ce["sc"] % 3; ce["sc"] += 1
                        bs = b_scp[k][pi_]
                        P.op("pe", (lambda k=k, hh=hh, si=si, p0=p0, w=w: nc.tensor.matmul(out=psc[si][:, 0:w], lhsT=iqb[k][:, hh, :], rhs=ikT[:, p0:p0 + w], start=True, stop=True)), [b_iqb[k]], [b_psc[si]])
                        P.op("act", (lambda si=si, w=w: nc.scalar.activation(out=rsb[si][:, 0:w], in_=psc[si][:, 0:w], func=AF.Relu)), [b_psc[si]], [b_rsb[si]])
                        if hh == 0:
                            P.op("dve", (lambda k=k, si=si, p0=p0, w=w: nc.vector.tensor_scalar(out=score[k][:, p0:p0 + w], in0=rsb[si][:, 0:w], scalar1=iwf[k][:, 0:1], scalar2=None, op0=ALU.mult)), [b_rsb[si], b_iwf[k]], [bs, b_score[k]])
                        else:
                            P.op("dve", (lambda k=k, si=si, p0=p0, w=w, hh=hh: nc.vector.scalar_tensor_tensor(out=score[k][:, p0:p0 + w], in0=rsb[si][:, 0:w], scalar=iwf[k][:, hh:hh + 1], in1=score[k][:, p0:p0 + w], op0=ALU.mult, op1=ALU.add)), [b_rsb[si], b_iwf[k], bs], [bs])
                for i in blocks:
                    k = i % 2
                    S_ = st_[k]
                    N = 128 * (i + 1)
                    npc = (N + 511) // 512
                    if i >= 2:
                        P.op("dve", (lambda k=k, N=N, S_=S_: nc.vector.tensor_reduce(out=S_["amax"][:], in_=score[k][:, 0:N], axis=AX.X, op=ALU.max, apply_absolute_value=True)), b_scp[k][0:npc], [b_st[k]])
                    P.op("dve", (lambda k=k, i=i: nc.vector.tensor_tensor(out=score[k][:, i * 128:(i + 1) * 128], in0=score[k][:, i * 128:(i + 1) * 128], in1=cneg[:], op=ALU.add)), b_scp[k][0:npc] + [b_c[2]], [b_score[k]])

            def steps_B(i0):
                steps = []
                blocks = (i0, i0 + 1)
                if i0 >= 2:
                    def init():
                        for i in blocks:
                            k = i % 2; S_ = st_[k]
                            P.op("dve", (lambda S_=S_: nc.vector.tensor_scalar(out=S_["amax"][:], in0=S_["amax"][:], scalar1=1.001, scalar2=1e-30, op0=ALU.mult, op1=ALU.add)), [b_st[k]], [b_st[k]])
                            P.op("dve", (lambda S_=S_: nc.vector.tensor_scalar(out=S_["dt"][:], in0=p2[:], scalar1=S_["amax"][:, 0:1], scalar2=None, op0=ALU.mult)), [b_st[k], b_c[4]], [b_st[k]])
                            P.op("dve", (lambda S_=S_: nc.vector.tensor_scalar(out=S_["d2"][:], in0=S_["dt"][:], scalar1=2.0, scalar2=None, op0=ALU.mult)), [b_st[k]], [b_st[k]])
                            P.op("dve", (lambda S_=S_: nc.vector.memset(S_["mid"][:], 0.0)), [], [b_st[k]])
                    steps.append(init)
                    for it in range(NBIS):
                        def one(it=it):
                            i = blocks[0]; k = i % 2; S_ = st_[k]; N = 128 * (i + 1)
                            P.op("dve", (lambda k=k, N=N, S_=S_: nc.vector.tensor_scalar(out=junk[:, 0:N], in0=score[k][:, 0:N], scalar1=S_["mid"][:, 0:1], scalar2=0.0, op0=ALU.is_ge, op1=ALU.add, accum_out=S_["cnt"][:])),
                                 [b_score[k], b_st[k]], [b_junk, b_cnt[k]])
                            i = blocks[1]; k1 = i % 2; S1 = st_[k1]; N1 = 128 * (i + 1)
                            P.op("act", (lambda k1=k1, N1=N1, S1=S1: nc.scalar.activation(out=junk2[:, 0:N1], in_=score[k1][:, 0:N1], func=AF.Sign, bias=S1["mid"][:, 0:1], accum_out=S1["cnt"][:])),
                                 [b_score[k1], b_st[k1]], [b_junk2, b_cnt[k1]])
                            P.op("dve", (lambda S1=S1, i=i: nc.vector.scalar_tensor_tensor(out=S1["sgd"][:], in0=S1["cnt"][:], scalar=nthr[:, i:i + 1], in1=S1["d2"][:, it + 1:it + 2], op0=ALU.is_ge, op1=ALU.mult)), [b_cnt[k1], b_st[k1], b_c[6]], [b_sgd[k1]])
                            P.op("dve", (lambda S_=S_: nc.vector.scalar_tensor_tensor(out=S_["sgd"][:], in0=S_["cnt"][:], scalar=TOPK - 0.5, in1=S_["d2"][:, it + 1:it + 2], op0=ALU.is_ge, op1=ALU.mult)), [b_cnt[k], b_st[k]], [b_sgd[k]])
                            P.op("dve", (lambda S1=S1: nc.vector.scalar_tensor_tensor(out=S1["mid"][:], in0=S1["dt"][:, it + 1:it + 2], scalar=S1["sgd"][:, 0:1], in1=S1["mid"][:], op0=ALU.subtract, op1=ALU.add)), [b_sgd[k1], b_st[k1]], [b_st[k1]])
                            P.op("dve", (lambda S_=S_: nc.vector.scalar_tensor_tensor(out=S_["mid"][:], in0=S_["sgd"][:], scalar=S_["dt"][:, it + 1:it + 2], in1=S_["mid"][:], op0=ALU.subtract, op1=ALU.add)), [b_sgd[k], b_st[k]], [b_st[k]])
                        steps.append(one)

                    def fin():
                        i = blocks[0]; k = i % 2; S_ = st_[k]
                        P.op("dve", (lambda S_=S_: nc.vector.tensor_tensor(out=S_["thr"][:], in0=S_["mid"][:], in1=S_["dt"][:, NBIS:NBIS + 1], op=ALU.subtract)), [b_st[k]], [b_st[k]])
                        i = blocks[1]; k = i % 2; S_ = st_[k]
                        P.op("dve", (lambda S_=S_: nc.vector.scalar_tensor_tensor(out=S_["thr"][:], in0=S_["mid"][:], scalar=-1.0, in1=S_["dt"][:, NBIS:NBIS + 1], op0=ALU.mult, op1=ALU.subtract)), [b_st[k]], [b_st[k]])
                    steps.append(fin)
                else:
                    def init0():
                        for i in blocks:
                            k = i % 2; S_ = st_[k]
                            P.op("dve", (lambda S_=S_: nc.vector.memset(S_["thr"][:], -1e29)), [], [b_st[k]])
                    steps.append(init0)

                def mk():
                    for i in blocks:
                        k = i % 2; S_ = st_[k]; N = 128 * (i + 1)
                        P.op("dve", (lambda k=k, N=N, S_=S_: nc.vector.tensor_scalar(out=maskb[k][:, 0:N], in0=score[k][:, 0:N], scalar1=S_["thr"][:, 0:1], scalar2=NEG_MASK, op0=ALU.is_lt, op1=ALU.mult)), [b_score[k], b_st[k]], [b_maskb[k]])
                steps.append(mk)
                return steps

            def steps_A(i):
                k = i % 2
                oi = i % 2
                rows = slice(i * 128, (i + 1) * 128)
                slots = {}

                def qk(c):
                    ai = ce["A"] % 3; ce["A"] += 1
                    slots[c] = ai
                    ccols = slice(c * 128, (c + 1) * 128)
                    for h in range(4):
                        P.op("pe", (lambda h=h: nc.tensor.matmul(out=psA[ai][:, h * 128:(h + 1) * 128], lhsT=kT2[:, h, ccols], rhs=qT2[:, h, rows], start=(h == 0), stop=False, skip_group_check=True)), [], [b_psA[ai]])
                    P.op("pe", (lambda: nc.tensor.matmul(out=psA[ai][:], lhsT=maskb[k][:, ccols], rhs=I4[:].rearrange("p h s -> p (h s)"), start=False, stop=False, skip_group_check=True)), [b_maskb[k], b_c[1]], [b_psA[ai]])
                    if c >= i - 1:
                        o0 = 128 if c == i else 0
                        for h in range(4):
                            P.op("pe", (lambda h=h: nc.tensor.matmul(out=psA[ai][:, h * 128:(h + 1) * 128], lhsT=Bn[:, h, o0:o0 + 128], rhs=C.ident[:], start=False, stop=(h == 3), skip_group_check=True)), [b_c[0], C.b_ident], [b_psA[ai]])
                    else:
                        P.op("pe", (lambda: nc.tensor.matmul(out=psA[ai][:], lhsT=onesrow[:], rhs=cfrow[:], start=False, stop=True, skip_group_check=True)), [b_c[3], b_c[5]], [b_psA[ai]])

                def ex(c):
                    ai = slots[c]
                    P.op("act", (lambda: nc.scalar.activation(out=pT[ai][:], in_=psA[ai][:], func=AF.Exp)), [b_psA[ai]], [b_pT[ai]])

                def pv(c):
                    ai = slots[c]
                    for h in range(4):
                        P.op("pe", (lambda h=h: nc.tensor.matmul(out=po[oi][:, h * 65:(h + 1) * 65], lhsT=pT[ai][:, h * 128:(h + 1) * 128], rhs=v1[:, c, h, :], start=(c == 0 and h == 0), stop=(c == i), skip_group_check=True)), [b_pT[ai]], [b_po[oi]])

                def fin():
                    P.op("dve", (lambda: nc.vector.reciprocal(out=rc[oi][:], in_=po[oi][:, 0:260].rearrange("p (j e) -> p j e", e=65)[:, :, 64])), [b_po[oi]], [b_rc[oi]])
                    P.op("dve", (lambda: nc.vector.tensor_tensor(out=ya[oi][:], in0=po[oi][:, 0:260].rearrange("p (j e) -> p j e", e=65)[:, :, 0:64], in1=rc[oi][:, :].unsqueeze(2).broadcast_to([128, 4, 64]), op=ALU.mult)), [b_po[oi], b_rc[oi]], [b_ya[oi]])
                    P.dma("sp", (lambda: nc.sync.dma_start(out=ytm[rows, 0:256], in_=ya[oi][:].rearrange("p h d -> p (h d)"))), [b_ya[oi]], [], b_ya[oi])

                n = i + 1
                steps = []
                for idx in range(n + 2):
                    def st(idx=idx):
                        if idx < n:
                            qk(idx)
                        if 0 <= idx - 1 < n:
                            ex(idx - 1)
                        if 0 <= idx - 2 < n:
                            pv(idx - 2)
                        if idx == n + 1:
                            fin()
                    steps.append(st)
                return steps

            def merge(a, b):
                na, nb = len(a), len(b)
                ia = ib = 0
                while ia < na or ib < nb:
                    if ib >= nb or (ia < na and ia * nb <= ib * na):
                        a[ia](); ia += 1
                    else:
                        b[ib](); ib += 1

            NP = NB // 2
            emit_S2((0, 1))
            for f in steps_B(0):
                f()
            for p in range(NP):
                i0 = 2 * p
                sa = steps_A(i0) + steps_A(i0 + 1)
                if p + 1 < NP:
                    emit_S2((i0 + 2, i0 + 3))
                    sb_ = steps_B(i0 + 2)
                else:
                    sb_ = []
                merge(sa, sb_[:-1])
                if sb_:
                    sb_[-1]()
            P.barrier()
    C.stack = None


def build_program(depth, phases=("f1", "proj", "conv", "mla", "gla", "dsa", "out", "f2"), debug=False):
    nc = bass.Bass("TRN2", target_bir_lowering=False)
    C = Ctx(nc)
    P = C.P
    D = C.dram

    def din(name, shape, dt=F32):
        D[name] = nc.dram_tensor(name, list(shape), dt, kind="ExternalInput").ap()
        return D[name]

    def dscr(name, shape, dt):
        D[name] = nc.dram_tensor(name, list(shape), dt, kind=("ExternalOutput" if debug else "Internal")).ap()
        return D[name]

    din("x", [SEQ, D_MODEL])
    din("ident", [128, 128])
    din("ffn_g", [2 * depth, 128, 8])
    din("ffn_wg", [2 * depth, D_MODEL, D_FF])
    din("ffn_wu", [2 * depth, D_MODEL, D_FF])
    din("ffn_wd", [2 * depth, D_FF, D_MODEL])
    din("mix_g", [depth, 128, 8])
    din("w_in", [depth, D_MODEL, IN_WIDTH_P])
    dscr("zT", [N_FM, SEQ], BF16)
    dscr("ztm", [SEQ, N_TM], BF16)
    dscr("yT", [D_MODEL, SEQ], BF16)
    dscr("ytm", [SEQ, D_MODEL], BF16)
    din("rope_c", [32, SEQ]); din("rope_s", [32, SEQ]); din("tri", [128, 128]); din("prot", [96, 96]); din("emat", [32, 96])
    din("d_gqa", [depth, 128, 2]); din("d_wuq", [depth, 128, 2, 384]); din("d_gkva", [depth, 128, 1])
    din("d_wk", [depth, 128, 4, 96]); din("d_wv", [depth, 128, 256]); din("d_gq", [depth, 96, 1]); din("d_gk", [depth, 96, 1])
    din("w_out", [depth, D_MODEL, D_MODEL])
    din("bd64", [128, 128]); din("a_bn", [128, 1024]); din("i4", [128, 512]); din("cneg", [128, 128]); din("a_cf", [1, 512]); din("pow2", [128, NBIS + 1])
    din("a_g2", [depth, 128, 2]); din("a_nthr", [128, SEQ // 128])
    din("bmask", [128, 256]); din("tri64", [128, 256]); din("rmask", [128, 512])
    din("b_wgu", [depth, 16, 128]); din("b_gbias", [depth, 128, 1]); din("b_gout", [depth, 1, 64])
    D["gla_o"] = nc.dram_tensor("gla_o", [SEQ, 256], F32, kind="Internal").ap()
    din("c_w", [depth, 128, 2, 31])
    din("c_b", [depth, 128, 2])
    din("c_g", [depth, 128, 2])
    if debug:
        D["dbg"] = nc.dram_tensor("dbg", [SEQ // 128, 128, 4], F32, kind="ExternalOutput").ap()
    y = nc.dram_tensor("y", [SEQ, D_MODEL], F32, kind="ExternalOutput").ap()
    D["y"] = y
    xb_in = P.bufs_n("xin", SEQ // 128)
    xb_y = P.bufs_n("xy", SEQ // 128)

    with contextlib.ExitStack() as cst:
        load_consts(C, cst)
        src, sb = D["x"], xb_in
        for l in range(depth):
            if "f1" in phases:
                phase_ffn(C, src, y, sb, xb_y, D["ffn_g"][2 * l], D["ffn_wg"][2 * l], D["ffn_wu"][2 * l], D["ffn_wd"][2 * l])
                src, sb = y, xb_y
            if "proj" in phases:
                phase_proj(C, src, D["mix_g"][l], D["w_in"][l])
            if "conv" in phases:
                phase_conv(C, D["c_w"][l], D["c_b"][l], D["c_g"][l])
            if "mla" in phases:
                phase_mla(C, dict(gqa=D["d_gqa"][l], wuq=D["d_wuq"][l], gkva=D["d_gkva"][l], wk=D["d_wk"][l], wv=D["d_wv"][l], gq=D["d_gq"][l], gk=D["d_gk"][l]))
            if "dsa" in phases:
                phase_dsa(C, dict(g2=D["a_g2"][l]))
            if "gla" in phases:
                phase_gla(C, dict(wgu=D["b_wgu"][l], gbias=D["b_gbias"][l], gout=D["b_gout"][l]))
            if "out" in phases:
                phase_out(C, src, y, D["w_out"][l])
                src, sb = y, xb_y
            if "f2" in phases:
                phase_ffn(C, src, y, sb, xb_y, D["ffn_g"][2 * l + 1], D["ffn_wg"][2 * l + 1], D["ffn_wu"][2 * l + 1], D["ffn_wd"][2 * l + 1])
                src, sb = y, xb_y
        P.barrier()
    return nc, C


def arrange_gain(g):
    return np.ascontiguousarray(np.asarray(g, np.float32).reshape(8, 128).T)


def make_inputs(inp, depth=DEPTH, batch=0, x_override=None, layer0=0):
    f32 = np.float32
    L = range(layer0, layer0 + depth)
    m = {}
    m["x"] = np.ascontiguousarray(inp["x"][batch] if x_override is None else x_override, dtype=f32)
    m["ident"] = np.eye(128, dtype=f32)

    def il(a, b):
        return np.ascontiguousarray(np.stack([v for l in L for v in (inp[a][l], inp[b][l])]).astype(f32, copy=False))

    m["ffn_g"] = np.stack([arrange_gain(v) for l in L for v in (inp["ffn1_norm"][l], inp["ffn2_norm"][l])])
    m["ffn_wg"] = il("ffn1_gate", "ffn2_gate")
    m["ffn_wu"] = il("ffn1_up", "ffn2_up")
    m["ffn_wd"] = il("ffn1_down", "ffn2_down")
    m["mix_g"] = np.stack([arrange_gain(inp["mix_norm"][l]) for l in L])
    m["w_in"] = np.stack([permute_w_in(inp["w_in"][l]) for l in L])

    def pc2(v):
        return np.ascontiguousarray(np.asarray(v, f32).reshape(2, 128).T)


    half = 16
    freqs = (np.float32(10000.0) ** (-np.arange(half, dtype=f32) / np.float32(half))).astype(f32)
    ang = np.arange(SEQ, dtype=f32)[None, :] * freqs[:, None]
    m["rope_c"] = np.concatenate([np.cos(ang), np.cos(ang)], 0).astype(f32)
    m["rope_s"] = np.concatenate([np.sin(ang), np.sin(ang)], 0).astype(f32)
    m["tri"] = np.triu(np.ones((128, 128), f32))
    pr = np.zeros((96, 96), f32)
    for i in range(16):
        pr[80 + i, 64 + i] = -1.0
        pr[64 + i, 80 + i] = 1.0
    m["prot"] = pr
    em = np.zeros((32, 96), f32)
    em[np.arange(32), 64 + np.arange(32)] = 1.0
    m["emat"] = em
    m["d_gqa"] = np.stack([pc2(inp["d_qa_norm"][l]) for l in L])
    m["d_wuq"] = np.stack([np.ascontiguousarray(np.asarray(inp["d_uq"][l], f32).reshape(2, 128, 384).transpose(1, 0, 2)) for l in L])
    m["d_gkva"] = np.stack([np.asarray(inp["d_kva_norm"][l], f32).reshape(128, 1) for l in L])
    wk = []
    wv = []
    for l in L:
        w = np.asarray(inp["d_ukv"][l], f32).reshape(128, 4, 128)
        k96 = np.zeros((128, 4, 96), f32)
        k96[:, :, 0:64] = w[:, :, 0:64]
        wk.append(k96)
        wv.append(np.ascontiguousarray(w[:, :, 64:128].reshape(128, 256)))
    m["d_wk"] = np.stack(wk)
    m["d_wv"] = np.stack(wv)
    m["d_gq"] = np.stack([np.asarray(inp["d_q_norm"][l], f32).reshape(96, 1) for l in L])
    m["d_gk"] = np.stack([np.asarray(inp["d_k_norm"][l], f32).reshape(96, 1) for l in L])
    m["bd64"] = np.kron(np.eye(2, dtype=f32), np.ones((64, 64), f32))
    rb = np.asarray(inp["rel_bias"], f32)
    tq = np.arange(128)[:, None]
    sp = np.arange(256)[None, :]
    dist = tq - (sp - 128)
    dpos = np.maximum(dist, 0)
    df = np.maximum(dpos, 1).astype(f32)
    large = 16 + (np.log(df / f32(16)) / f32(math.log(128 / 16)) * f32(16)).astype(np.int32)
    large = np.minimum(large, 31)
    bucket = np.where(dpos < 16, dpos, large)
    m["a_bn"] = np.ascontiguousarray(rb[bucket].transpose(0, 2, 1).reshape(128, 1024))
    m["i4"] = np.ascontiguousarray(np.tile(np.eye(128, dtype=f32), (1, 4)))
    m["cneg"] = np.where(np.arange(128)[None, :] <= np.arange(128)[:, None], f32(0), f32(-1e30)).astype(f32)
    m["a_cf"] = np.ascontiguousarray(np.repeat(rb[31, :], 128)[None, :].astype(f32))
    m["pow2"] = np.ascontiguousarray(np.broadcast_to((2.0 ** -np.arange(NBIS + 1)).astype(f32)[None, :], (128, NBIS + 1)))
    m["a_nthr"] = np.ascontiguousarray(np.broadcast_to((2 * TOPK - 0.5 - 128.0 * (np.arange(SEQ // 128) + 1)).astype(f32)[None, :], (128, SEQ // 128)))
    m["a_g2"] = np.stack([np.stack([np.tile(np.asarray(inp["a_q_norm"][l], f32), 2), np.tile(np.asarray(inp["a_k_norm"][l], f32), 2)], 1) for l in L])
    bm = np.zeros((128, 4, 64), f32)
    for h in range(4):
        bm[h * 32:(h + 1) * 32, h, :] = 1.0
    m["bmask"] = bm.reshape(128, 256)
    t64 = (np.arange(128)[:, None] % 64 <= np.arange(64)[None, :]).astype(f32)
    m["tri64"] = np.ascontiguousarray(np.broadcast_to(t64[:, None, :], (128, 4, 64)).reshape(128, 256))
    rm = np.ones((128, 512), f32)
    rm[:, ::64] = 0.0
    m["rmask"] = rm
    m["b_wgu"] = np.stack([np.asarray(inp["b_gate_up"][l], f32) for l in L])
    m["b_gbias"] = np.stack([np.asarray(inp["b_gate_bias"][l], f32).reshape(128, 1) for l in L])
    m["b_gout"] = np.stack([np.asarray(inp["b_out_norm"][l], f32).reshape(1, 64) for l in L])
    m["w_out"] = np.ascontiguousarray(np.stack([np.asarray(inp["w_out"][l], f32) for l in L]))
    m["c_w"] = np.stack([np.ascontiguousarray(np.asarray(inp["c_dw_w"][l], f32)[:, 0, :].reshape(31, 2, 128).transpose(2, 1, 0)) for l in L])
    m["c_b"] = np.stack([pc2(inp["c_dw_b"][l]) for l in L])
    m["c_g"] = np.stack([pc2(inp["c_norm"][l]) for l in L])
    return m


N_CORES_USED = 4


def kernel(**inputs):
    inp = {k: np.asarray(v) for k, v in inputs.items()}
    nc, C = build_program(DEPTH)
    shared = make_inputs(inp, depth=DEPTH, batch=0)
    in_maps = []
    for b in range(BATCH):
        m = dict(shared)
        m["x"] = np.ascontiguousarray(inp["x"][b], dtype=np.float32)
        in_maps.append(m)
    res = run_bass_kernel_spmd(nc, in_maps, core_ids=list(range(N_CORES_USED)))
    out = np.stack([np.asarray(res.results[b]["y"], dtype=np.float32) for b in range(BATCH)], axis=0)
    return out
```
